# Optimizing a Trainium2 kernel written in Bass

```python
import jax, jax.numpy as jnp
from jax import lax
import numpy as np

D_MODEL = 1024
BATCH = 4
SEQ = 8192
DEPTH = 2

HEAD_DIM = 64
POOL_WINDOWS = (2, 4, 8, 16)
POOL_WIDTH = D_MODEL // 4
POOL_GROUP = POOL_WIDTH // len(POOL_WINDOWS)
DIL_PAIRS = ((128, 1), (512, 4), (2048, 16))
DIL_HEADS_PER_GROUP = D_MODEL // 512
DIL_HEADS = DIL_HEADS_PER_GROUP * len(DIL_PAIRS)
DIL_WIDTH = DIL_HEADS * HEAD_DIM
DIL_BLOCK = 64
WIN_Q_HEADS = DIL_HEADS
WIN_KV_HEADS = WIN_Q_HEADS // 3
WIN_GROUP = WIN_Q_HEADS // WIN_KV_HEADS
WIN_RADIUS = 128
WIN_BLOCK = 128
WIN_Q_WIDTH = WIN_Q_HEADS * HEAD_DIM
WIN_KV_WIDTH = WIN_KV_HEADS * HEAD_DIM
MIX_WIDTH = POOL_WIDTH + DIL_WIDTH + WIN_Q_WIDTH
IN_WIDTH = POOL_WIDTH + 3 * DIL_WIDTH + WIN_Q_WIDTH + 2 * WIN_KV_WIDTH
N_ATTN_HEADS = DIL_HEADS + WIN_Q_HEADS
D_FF = -(-8 * D_MODEL // (3 * 256)) * 256
N_MOD = 6
EPS = 1e-6
NEG = -1e30

kernel_name = "hybrid_pool_dilated_swa_encoder"


def rmsnorm(x, g):
    xf = x.astype(jnp.float32)
    y = xf * lax.rsqrt(jnp.mean(xf * xf, axis=-1, keepdims=True) + EPS)
    return (y * g.astype(jnp.float32)).astype(x.dtype)


def alibi_slopes():
    i = jnp.arange(1, N_ATTN_HEADS + 1, dtype=jnp.float32)
    return jnp.exp2(-8.0 * i / N_ATTN_HEADS)


def banded_attention(q, k, v, radius, block, slopes, dist_scale, sink):
    N, L, Hk, G, D = q.shape
    nb = -(-L // block)
    Lp = nb * block
    W = block + 2 * radius
    q = jnp.pad(q, ((0, 0), (0, Lp - L), (0, 0), (0, 0), (0, 0)))
    kpad = ((0, 0), (radius, Lp - L + radius), (0, 0), (0, 0))
    k = jnp.pad(k, kpad)
    v = jnp.pad(v, kpad)
    idx = (jnp.arange(nb) * block)[:, None] + jnp.arange(W)[None, :]
    kb = k[:, idx]
    vb = v[:, idx]
    qb = q.reshape(N, nb, block, Hk, G, D)
    s = jnp.einsum('nbqhgd,nbkhd->nbhgqk', qb, kb).astype(jnp.float32) * (D ** -0.5)
    rel = jnp.arange(W)[None, :] - radius - jnp.arange(block)[:, None]
    dist = jnp.abs(rel)
    key_pos = idx - radius
    valid = (dist <= radius)[None] & ((key_pos >= 0) & (key_pos < L))[:, None, :]
    bias = -(slopes.astype(jnp.float32) * dist_scale)[:, :, None, None] * dist.astype(jnp.float32)
    s = jnp.where(valid[None, :, None, None], s + bias[None, None], NEG)
    m = jnp.max(s, axis=-1)
    if sink is not None:
        sk = sink.astype(jnp.float32)[None, None, :, :, None]
        m = jnp.maximum(m, sk)
    e = jnp.exp(s - m[..., None])
    denom = jnp.sum(e, axis=-1)
    if sink is not None:
        denom = denom + jnp.exp(sk - m)
    p = e / denom[..., None]
    lse = m + jnp.log(denom)
    out = jnp.einsum('nbhgqk,nbkhd->nbqhgd', p.astype(v.dtype), vb)
    out = out.reshape(N, Lp, Hk, G, D)[:, :L]
    lse = lse.transpose(0, 1, 4, 2, 3).reshape(N, Lp, Hk, G)[:, :L]
    return out, lse


def multiscale_pool(u, w_pool, scale):
    B, S, C = u.shape
    uf = u.astype(jnp.float32)
    cs = jnp.pad(jnp.cumsum(uf, axis=1), ((0, 0), (1, 0), (0, 0)))
    t = jnp.arange(S)
    outs = []
    for g, w in enumerate(POOL_WINDOWS):
        r = w // 2
        lo = jnp.clip(t - r, 0, S)
        hi = jnp.clip(t + r + 1, 0, S)
        sl = slice(g * POOL_GROUP, (g + 1) * POOL_GROUP)
        csg = cs[..., sl]
        mean = (csg[:, hi] - csg[:, lo]) / (hi - lo).astype(jnp.float32)[None, :, None]
        outs.append(mean - uf[..., sl])
    pooled = jnp.stack(outs, axis=2).astype(u.dtype)
    y = jnp.einsum('bsgc,gcd->bsgd', pooled, w_pool).reshape(B, S, C)
    return y * scale


def dilated_attention(q, k, v, slopes):
    B, S, H, D = q.shape
    hg = DIL_HEADS_PER_GROUP
    outs, lses = [], []
    for g, (w, d) in enumerate(DIL_PAIRS):
        hs = slice(g * hg, (g + 1) * hg)

        def to_res(t):
            return t[:, :, hs].reshape(B, S // d, d, hg, D).transpose(0, 2, 1, 3, 4).reshape(B * d, S // d, hg, D)

        o, lse = banded_attention(to_res(q)[:, :, :, None, :], to_res(k), to_res(v),
                                  w // (2 * d), DIL_BLOCK, slopes[hs][:, None], d, None)
        outs.append(o.reshape(B, d, S // d, hg, D).transpose(0, 2, 1, 3, 4).reshape(B, S, hg, D))
        lses.append(lse.reshape(B, d, S // d, hg).transpose(0, 2, 1, 3).reshape(B, S, hg))
    alpha = jax.nn.softmax(jnp.stack(lses, axis=0), axis=0)
    y = jnp.concatenate([alpha[g][..., None].astype(outs[g].dtype) * outs[g]
                         for g in range(len(DIL_PAIRS))], axis=2)
    return y.reshape(B, S, DIL_WIDTH)


def windowed_gqa(q, k, v, slopes, sink):
    B, S, _ = q.shape
    qh = q.reshape(B, S, WIN_KV_HEADS, WIN_GROUP, HEAD_DIM)
    kh = k.reshape(B, S, WIN_KV_HEADS, HEAD_DIM)
    vh = v.reshape(B, S, WIN_KV_HEADS, HEAD_DIM)
    o, _ = banded_attention(qh, kh, vh, WIN_RADIUS, WIN_BLOCK,
                            slopes.reshape(WIN_KV_HEADS, WIN_GROUP), 1,
                            sink.reshape(WIN_KV_HEADS, WIN_GROUP))
    return o.reshape(B, S, WIN_Q_WIDTH)


def setup_inputs(seed: int = 0) -> dict:
    key = jax.random.key(seed)
    ks = jax.random.split(key, 16)
    f32 = jnp.float32
    nrm = lambda k, shape, s: jax.random.normal(k, shape, f32) * s
    return {
        "x": nrm(ks[0], (BATCH, SEQ, D_MODEL), 1.0),
        "c": nrm(ks[1], (BATCH, D_MODEL), 1.0),
        "norm1_g": 1.0 + nrm(ks[2], (DEPTH, D_MODEL), 0.1),
        "norm2_g": 1.0 + nrm(ks[3], (DEPTH, D_MODEL), 0.1),
        "w_ada": nrm(ks[4], (DEPTH, D_MODEL, N_MOD * D_MODEL), 0.5 * D_MODEL ** -0.5),
        "b_ada": nrm(ks[5], (DEPTH, N_MOD * D_MODEL), 0.02),
        "w_in": nrm(ks[6], (DEPTH, D_MODEL, IN_WIDTH), D_MODEL ** -0.5),
        "w_pool": nrm(ks[7], (DEPTH, len(POOL_WINDOWS), POOL_GROUP, POOL_GROUP), POOL_GROUP ** -0.5),
        "pool_scale": 1.0 + nrm(ks[8], (DEPTH, POOL_WIDTH), 0.1),
        "sink_logit": nrm(ks[9], (DEPTH, WIN_Q_HEADS), 1.0),
        "w_out": nrm(ks[10], (DEPTH, MIX_WIDTH, D_MODEL), MIX_WIDTH ** -0.5),
        "w_gate": nrm(ks[11], (DEPTH, D_MODEL, D_FF), D_MODEL ** -0.5),
        "w_up": nrm(ks[12], (DEPTH, D_MODEL, D_FF), D_MODEL ** -0.5),
        "w_down": nrm(ks[13], (DEPTH, D_FF, D_MODEL), D_FF ** -0.5),
        "final_g": 1.0 + nrm(ks[14], (D_MODEL,), 0.1),
    }


def reference(x, c, norm1_g, norm2_g, w_ada, b_ada, w_in, w_pool, pool_scale, sink_logit,
              w_out, w_gate, w_up, w_down, final_g):
    slopes = alibi_slopes()
    slopes_win = slopes[:WIN_Q_HEADS]
    slopes_dil = slopes[WIN_Q_HEADS:].astype(jnp.float32)
    o_q_b = POOL_WIDTH
    o_k_b = o_q_b + DIL_WIDTH
    o_v_b = o_k_b + DIL_WIDTH
    o_q_c = o_v_b + DIL_WIDTH
    o_k_c = o_q_c + WIN_Q_WIDTH
    o_v_c = o_k_c + WIN_KV_WIDTH
    B, S, _ = x.shape
    c_act = jax.nn.silu(c)
    for l in range(DEPTH):
        mod = (c_act @ w_ada[l] + b_ada[l])[:, None, :]
        sh1, sc1, g1, sh2, sc2, g2 = jnp.split(mod, N_MOD, axis=-1)
        h = rmsnorm(x, norm1_g[l]) * (1.0 + sc1) + sh1
        z = h @ w_in[l]
        y_a = multiscale_pool(z[..., :o_q_b], w_pool[l], pool_scale[l])
        qb = z[..., o_q_b:o_k_b].reshape(B, S, DIL_HEADS, HEAD_DIM)
        kb = z[..., o_k_b:o_v_b].reshape(B, S, DIL_HEADS, HEAD_DIM)
        vb = z[..., o_v_b:o_q_c].reshape(B, S, DIL_HEADS, HEAD_DIM)
        y_b = dilated_attention(qb, kb, vb, slopes_dil)
        y_c = windowed_gqa(z[..., o_q_c:o_k_c], z[..., o_k_c:o_v_c], z[..., o_v_c:],
                           slopes_win, sink_logit[l])
        mix = jnp.concatenate([y_a, y_b, y_c], axis=-1) @ w_out[l]
        x = x + g1 * mix
        h = rmsnorm(x, norm2_g[l]) * (1.0 + sc2) + sh2
        ffn = (jax.nn.silu(h @ w_gate[l]) * (h @ w_up[l])) @ w_down[l]
        x = x + g2 * ffn
    return rmsnorm(x, final_g)
```

```python
import numpy as np
import concourse.bass as bass
import concourse.mybir as mybir
from concourse.bass_utils import run_bass_kernel_spmd

F32 = mybir.dt.float32
BF16 = mybir.dt.bfloat16
ALU = mybir.AluOpType
AF = mybir.ActivationFunctionType
AX = mybir.AxisListType

D = 1024
KC = 8
NLOC = 6144
L = 2
NKV = (6144, 5120)
NQ = (5120, 4096)
NOUT = 4096
TT = 512
DFF = 2816
FC = 22
EPS = 1e-6
NEGM = -30000.0
DEBUG = False

CH_POOL = (0, 1)
CH_QB = (2, 3, 4)
CH_KB = (5, 6, 7)
CH_VB = (8, 9, 10)
CH_QC = (11, 12, 13)
CH_KC = 14
CH_VC = 15
Q_CHUNKS = set(CH_QB) | set(CH_QC)
DIL = (1, 4, 16)
SK = 3


class Op:
    __slots__ = ("eng", "fn", "deps", "sig", "cnt", "dsem", "dval")


class Prog:
    ENGS = ("pe", "act", "dve", "pool", "sp")

    def __init__(self):
        self.ops = {e: [] for e in self.ENGS}
        self.dtot = {}
        self.last_dma = {}
        self.pending = {e: [] for e in self.ENGS}

    def add(self, eng, meth, args, kw, deps=(), dsem=None):
        op = Op()
        op.eng = eng
        op.fn = (meth, args, kw)
        op.sig = False
        op.cnt = 0
        op.dsem = dsem
        op.dval = 0
        dl = []

        def flat(x):
            if x is None:
                return
            if isinstance(x, (list, tuple)):
                for y in x:
                    flat(y)
            else:
                dl.append(x)

        flat(deps)
        if self.pending[eng]:
            dl.extend(self.pending[eng])
            self.pending[eng] = []
        op.deps = dl
        for d in dl:
            if d.dsem is None and d.eng != eng:
                d.sig = True
        if dsem is not None:
            self.dtot[dsem] = self.dtot.get(dsem, 0) + 16
            op.dval = self.dtot[dsem]
            self.last_dma[dsem] = op
        self.ops[eng].append(op)
        return op

    def barrier(self):
        deps = []
        for e in self.ENGS:
            for op in reversed(self.ops[e]):
                if op.dsem is None:
                    deps.append(op)
                    break
        deps.extend(self.last_dma.values())
        for e in self.ENGS:
            self.pending[e] = list(self.pending[e]) + deps

    def emit(self, nc, final_deps):
        for e in self.ENGS:
            c = 0
            for op in self.ops[e]:
                if op.dsem is None and op.sig:
                    c += 1
                    op.cnt = c
        import contextlib
        with contextlib.ExitStack() as st:
            esem = {e: st.enter_context(nc.semaphore("s_" + e)) for e in self.ENGS}
            dsem = {k: st.enter_context(nc.semaphore("d_" + k)) for k in self.dtot}
            block = st.enter_context(nc.Block())

            def run(engobj, name):
                waited = {}

                def wait_for(d):
                    if d.dsem is not None:
                        key, sem, val = "d_" + d.dsem, dsem[d.dsem], d.dval
                    elif d.eng == name:
                        return
                    else:
                        key, sem, val = "e_" + d.eng, esem[d.eng], d.cnt
                        assert val > 0
                    if waited.get(key, 0) < val:
                        engobj.wait_ge(sem, val)
                        waited[key] = val

                for op in self.ops[name]:
                    for d in op.deps:
                        wait_for(d)
                    ins = getattr(engobj, op.fn[0])(*op.fn[1], **op.fn[2])
                    if op.dsem is not None:
                        ins.then_inc(dsem[op.dsem], 16)
                    elif op.sig:
                        ins.then_inc(esem[name], 1)
                if name == "sp":
                    for d in final_deps:
                        wait_for(d)

            @block.tensor
            def _(e):
                run(e, "pe")

            @block.scalar
            def _(e):
                run(e, "act")

            @block.vector
            def _(e):
                run(e, "dve")

            @block.gpsimd
            def _(e):
                run(e, "pool")

            @block.sync
            def _(e):
                run(e, "sp")


class Arena:
    def __init__(self, ap, nwords):
        self.ap = ap
        self.n = nwords
        self.top = 0

    def alloc(self, nelem, dt=F32):
        words = nelem if dt == F32 else (nelem + 1) // 2
        a = self.top
        self.top += words
        assert self.top <= self.n, ("SBUF arena overflow", self.top, self.n)
        v = self.ap[:, a:a + words]
        return v if dt == F32 else v.bitcast(BF16)

    def mark(self):
        return self.top

    def release(self, m):
        self.top = m


def build_program(stop=None, DEBUG=DEBUG):
    nc = bass.Bass("TRN2", target_bir_lowering=False)

    def din(name, shape, dt=F32):
        return nc.dram_tensor(name, shape, dt, kind="ExternalInput").ap()

    xT = din("xT", [128, KC, NLOC])
    c_rep = din("c_rep", [128, D])
    w_adaT = din("w_adaT", [L, 128, 48, D])
    b_adaT = din("b_adaT", [128, L * 48])
    ngs = din("ngs", [128, (2 * L + 1) * KC])
    w_in = din("w_in", [L, 128, KC * 2048])
    w_out = din("w_out", [L, 128, KC * D])
    wgu = din("wgu", [L, FC, 128, 2048])
    wd = din("wd", [L, KC, 128, DFF])
    wpool = din("wpool", [128, L * 2 * 128])
    pscale = din("pscale", [128, L * 2])
    sink = din("sink", [128, L * 3])
    biasC = din("biasC", [128, 6 * 384])
    biasB = din("biasB", [128, 6 * 256])
    pconst = din("pconst", [128, 2 + 16])
    ident = din("ident", [128, 128])
    yT = nc.dram_tensor("yT", [128, KC, NOUT], F32, kind="ExternalOutput").ap()
    zT = nc.dram_tensor("zT", [16, 128, NLOC], BF16).ap()
    mixT = nc.dram_tensor("mixT", [8, 128, NQ[0]], BF16).ap()
    x1T = nc.dram_tensor("x1T", [128, KC, NQ[0]], F32).ap()
    wgu_bf = nc.dram_tensor("wgu_bf", [L, FC, 128, 2048], BF16).ap()
    wd_bf = nc.dram_tensor("wd_bf", [L, KC, 128, DFF], BF16).ap()
    dbg = {}
    if DEBUG:
        dbg["zT"] = nc.dram_tensor("dbg_zT", [16, 128, NLOC], BF16, kind="ExternalOutput").ap()
        dbg["mixT"] = nc.dram_tensor("dbg_mixT", [8, 128, NQ[0]], BF16, kind="ExternalOutput").ap()
        dbg["x1T"] = nc.dram_tensor("dbg_x1T", [128, KC, NQ[0]], F32, kind="ExternalOutput").ap()
        dbg["mod"] = nc.dram_tensor("dbg_mod", [128, L * 48], F32, kind="ExternalOutput").ap()

    import contextlib
    stack = contextlib.ExitStack()
    NW = 51200
    arena_t = stack.enter_context(nc.sbuf_tensor("arena", [128, NW], F32))
    A = Arena(arena_t[:, :], NW)
    ps = [stack.enter_context(nc.psum_tensor("ps%d" % i, [128, 512], F32)) for i in range(8)]
    ps = [p[:, :] for p in ps]
    P = Prog()
    final_deps = []

    def I(eng, meth, *args, deps=(), dsem=None, **kw):
        return P.add(eng, meth, args, kw, deps, dsem)

    def DMA(q, out, in_, deps=(), dsem=None):
        return P.add(q, "dma_start", (), dict(out=out, in_=in_), deps, dsem)

    identb = A.alloc(128, BF16)
    ones_f = A.alloc(128)
    ones_b = A.alloc(128, BF16)
    biasC_sb = A.alloc(6 * 384, BF16)
    biasB_sb = A.alloc(6 * 256, BF16)
    pconst_sb = A.alloc(18)
    ngs_sb = A.alloc((2 * L + 1) * KC)
    pscale_sb = A.alloc(L * 2)
    esink = A.alloc(L * 3)
    mod = A.alloc(L * 48)
    coef = A.alloc(L * 2 * KC)
    ident_f = A.alloc(128)
    wpool_sb = A.alloc(L * 2 * 128, BF16)
    vaug = A.alloc(48 * 256, BF16)
    vaug_v = vaug.rearrange("p (t c) -> p t c", c=256)

    def modv(l, m, kc):
        i = l * 48 + m * 8 + kc
        return mod[:, i:i + 1]

    def coefv(l, which, kc):
        i = (l * 2 + which) * KC + kc
        return coef[:, i:i + 1]

    const_ld = None
    for (dst, src) in ((ident_f, ident[:, :]), (pconst_sb, pconst[:, :]), (ngs_sb, ngs[:, :]), (pscale_sb, pscale[:, :]),
                       (esink, sink[:, :])):
        const_ld = DMA("sp", dst, src, dsem="const")
    wpool_ld = DMA("pool", wpool_sb, wpool[:, :], dsem="wpool")
    cast_done = [None, None]
    import os
    cast_list = {l: [(wgu_bf[l, fc], wgu[l, fc]) for fc in range(FC)] + [(wd_bf[l, dc], wd[l, dc]) for dc in range(KC)]
                 for l in range(L)}

    def issue_casts(lc, n):
        for _ in range(n):
            if cast_list[lc]:
                o, i_ = cast_list[lc].pop(0)
                cast_done[lc] = DMA("pool", o, i_, dsem="cast%d" % lc)

    bada_sb = A.alloc(L * 48)
    m0 = A.mark()
    scb = A.alloc(D)
    junk = A.alloc(D)
    wa = [A.alloc(8 * D) for _ in range(2)]
    bC_f = A.alloc(6 * 384)
    bB_f = A.alloc(6 * 256)
    DMA("sp", bC_f, biasC[:, :], dsem="bld")
    bld = DMA("sp", bB_f, biasB[:, :], dsem="bld")
    I("dve", "tensor_copy", out=biasC_sb, in_=bC_f, deps=[bld])
    wtab_ready = I("dve", "tensor_copy", out=biasB_sb, in_=bB_f)
    DMA("sp", scb, c_rep[:, :], dsem="cld")
    bada_ld = DMA("sp", bada_sb, b_adaT[:, :], dsem="cld")
    silu_op = I("act", "activation", out=scb, in_=scb, func=AF.Silu, deps=[bada_ld])
    I("dve", "tensor_copy", out=identb, in_=ident_f, deps=[const_ld])
    I("dve", "memset", ones_f, 1.0)
    I("dve", "memset", ones_b, 1.0)
    I("dve", "memset", vaug, 1.0)
    I("dve", "memset", mod, 0.0)
    I("act", "activation", out=esink, in_=esink, func=AF.Exp, deps=[const_ld])
    ada = dict(rd=[None, None], cnt=0, last=None)

    def adaln_block(l, j0, nj, wab, junkb):
        s = ada["cnt"] % 2
        ada["cnt"] += 1
        ld = DMA("sp", wab[s].rearrange("p (j k) -> p j k", j=nj), w_adaT[l, :, j0:j0 + nj, :],
                 deps=[ada["rd"][s]], dsem="wa%d" % s)
        for jj in range(nj):
            i = l * 48 + j0 + jj
            ada["last"] = I("dve", "scalar_tensor_tensor", out=junkb, in0=wab[s][:, jj * D:(jj + 1) * D], scalar=1.0,
                            in1=scb, op0=ALU.mult, op1=ALU.mult, accum_out=mod[:, i:i + 1], deps=[ld, silu_op])
        ada["rd"][s] = ada["last"]

    def adaln_finish(l):
        modadd = I("pool", "tensor_tensor", out=mod[:, l * 48:(l + 1) * 48], in0=mod[:, l * 48:(l + 1) * 48],
                   in1=bada_sb[:, l * 48:(l + 1) * 48], op=ALU.add, deps=[bada_ld, ada["last"]])
        lastc = None
        for which in range(2):
            b0 = l * 48 + (1 + 3 * which) * 8
            sc = mod[:, b0:b0 + 8]
            g0 = (which * L + l) * KC
            c0 = (l * 2 + which) * KC
            lastc = I("dve", "scalar_tensor_tensor", out=coef[:, c0:c0 + KC], in0=sc, scalar=1.0, in1=ngs_sb[:, g0:g0 + KC],
                      op0=ALU.add, op1=ALU.mult, deps=[const_ld, modadd])
        return [modadd, lastc]

    for jb in range(6):
        adaln_block(0, jb * 8, 8, wa, junk)
    adaln_finish(0)
    ada["rd"] = [None, None]
    ada["cnt"] = 0
    if DEBUG:
        final_deps.append(DMA("sp", dbg["mod"][:, :], mod, deps=[P.ops["dve"][-1]], dsem="dbg"))
    P.barrier()
    A.release(m0)
    if stop == "prologue":
        P.emit(nc, final_deps)
        stack.close()
        return nc

    class Norm:
        def __init__(self, sqb, tnv, rstd, psb):
            self.sqb, self.tn, self.rstd, self.psb = sqb, tnv, rstd, psb
            self.war = {}

        def part1(self, xv, deps):
            self.xv = xv
            self.sq = I("act", "activation", out=self.sqb, in_=xv, func=AF.Square, deps=[deps, self.war.get("sqb")])

        def part2(self, outs, scales, biases):
            self.p2a()
            self.p2b()
            self.p2c()
            return self.p2d(outs, scales, biases)

        def p2a(self):
            mm = None
            for kc in range(KC):
                mm = I("pe", "matmul", self.psb, ones_b, self.sqb[:, kc * TT:(kc + 1) * TT], start=(kc == 0),
                       stop=(kc == KC - 1), deps=[self.sq, self.war.get("psb")] if kc == 0 else [])
            self.war["sqb"] = mm
            self.mm = mm

        def p2b(self):
            ln = I("act", "activation", out=self.rstd, in_=self.psb, func=AF.Ln, scale=1.0 / D, bias=EPS,
                   deps=[self.mm, self.war.get("rstd")])
            self.war["psb"] = ln
            self.ex = I("act", "activation", out=self.rstd, in_=self.rstd, func=AF.Exp, scale=-0.5)

        def p2c(self):
            mul = None
            for kc in range(KC):
                mul = I("pool", "tensor_tensor", out=self.tn[:, kc * TT:(kc + 1) * TT], in0=self.xv[:, kc * TT:(kc + 1) * TT],
                        in1=self.rstd, op=ALU.mult, deps=[self.ex, self.war.get("tn")] if kc == 0 else [])
            self.war["rstd"] = mul
            self.x_done = mul
            self.mul = mul

        def p2d(self, outs, scales, biases):
            io = None
            for kc in range(KC):
                kw = dict(scale=scales[kc])
                if biases is not None:
                    kw["bias"] = biases[kc]
                io = I("act", "activation", out=outs[kc], in_=self.tn[:, kc * TT:(kc + 1) * TT], func=AF.Identity,
                       deps=[self.mul, self.out_war] if kc == 0 else [], **kw)
            self.war["tn"] = io
            return io

    for l in range(L):
        nkv, nq = NKV[l], NQ[l]
        xsrc = xT if l == 0 else x1T
        m1 = A.mark()
        if l == 0:
            A.alloc(D)
            junk1 = A.alloc(D)
            wa1 = [A.alloc(2 * D) for _ in range(2)]
        w_in_sb = A.alloc(KC * 2048, BF16)
        w_in_v = w_in_sb.rearrange("p (k n) -> p k n", k=KC)
        xt = [A.alloc(KC * TT) for _ in range(2)]
        sqb = A.alloc(KC * TT, BF16)
        tn = A.alloc(KC * TT)
        rstd = A.alloc(TT)
        hs = [A.alloc(KC * TT, BF16) for _ in range(2)]
        zst = [A.alloc(16 * TT, BF16) for _ in range(2)]
        hvs = [h_.rearrange("p (k t) -> p k t", k=KC) for h_ in hs]
        win_ld = None
        for q in range(4):
            win_ld = DMA("pool", w_in_sb[:, q * 4096:(q + 1) * 4096], w_in[l, :, q * 4096:(q + 1) * 4096], dsem="win")
        nt = nkv // TT
        nrm = Norm(sqb, tn, rstd, ps[4])
        x_rd = [None, None]
        zw = [None, None]
        h_rds = [None, None]
        bank_rd = [None] * 8
        xlds = {}
        hready = {}

        def m1_load(t):
            s = t % 2
            xlds[t] = DMA("sp", xt[s].rearrange("p (k t) -> p k t", k=KC), xsrc[:, :, t * TT:(t + 1) * TT],
                          deps=[x_rd[s]], dsem="x%d" % s)

        def m1_norm2(t):
            s = t % 2
            nrm.out_war = h_rds[s]
            hready[t] = nrm.part2([hvs[s][:, kc, :] for kc in range(KC)], [coefv(l, 0, kc) for kc in range(KC)],
                                  [modv(l, 0, kc) for kc in range(KC)])
            x_rd[s] = nrm.x_done

        m1_load(0)
        nrm.part1(xt[0], xlds[0])
        m1_norm2(0)
        for t in range(nt):
            s = t % 2
            hv = hvs[s]
            if l == 0:
                issue_casts(0, 1)
                for q in range(2):
                    if ada["cnt"] < 24:
                        adaln_block(1, ada["cnt"] * 2, 2, wa1, junk1)
            if t + 1 < nt:
                m1_load(t + 1)
                nrm.part1(xt[(t + 1) % 2], xlds[t + 1])
            full = (t * TT) < nq + TT
            chunks = list(range(16)) if full else (list(CH_KB) + list(CH_VB) + [CH_KC, CH_VC])
            evs = []
            lastmm = None
            for ci, ch in enumerate(chunks):
                if ci == len(chunks) // 2 and t + 1 < nt:
                    m1_norm2(t + 1)
                bk = ci % 4
                for kc in range(KC):
                    lastmm = I("pe", "matmul", ps[bk], w_in_v[:, kc, ch * 128:(ch + 1) * 128], hv[:, kc, :],
                               start=(kc == 0), stop=(kc == KC - 1),
                               deps=[hready[t], win_ld, bank_rd[bk]] if kc == 0 else [])
                dst = zst[s][:, ch * TT:(ch + 1) * TT]
                scl = 0.125 if ch in Q_CHUNKS else 1.0
                ev = I("dve", "tensor_scalar", out=dst, in0=ps[bk], scalar1=scl, scalar2=None, op0=ALU.mult,
                       deps=[lastmm, zw[s]])
                bank_rd[bk] = ev
                evs.append(ev)
            h_rds[s] = lastmm
            if full:
                zw[s] = DMA("sp", zT[:, :, t * TT:(t + 1) * TT].rearrange("c p n -> p c n"),
                            zst[s].rearrange("p (c n) -> p c n", c=16), deps=evs[-1:], dsem="zw%d" % s)
            else:
                for (c0, c1) in ((5, 11), (14, 16)):
                    zw[s] = DMA("sp", zT[c0:c1, :, t * TT:(t + 1) * TT].rearrange("c p n -> p c n"),
                                zst[s][:, c0 * TT:c1 * TT].rearrange("p (c n) -> p c n", c=c1 - c0),
                                deps=evs[-1:], dsem="zw%d" % s)
        if l == 0:
            assert ada["cnt"] == 24
            adaln_finish(1)
        P.barrier()
        A.release(m1)
        if DEBUG and l == 0:
            final_deps.append(DMA("sp", dbg["zT"][:, :, :], zT[:, :, :], dsem="dbg"))
            P.barrier()
        if stop == "m1" and l == 0:
            if DEBUG:
                P.barrier()
                final_deps.append(DMA("sp", dbg["mixT"][:, :, :], mixT[:, :, :], dsem="dbg2"))
            P.emit(nc, final_deps)
            stack.close()
            return nc
        ma = A.mark()
        W = nq + 16
        H = 8 + (int(0.6 * nq) // 8) * 8
        nA = H + 16
        LB = H - 16
        nB = W - LB
        zb = A.alloc(W, BF16)
        bufA = [A.alloc(nA) for _ in range(4)]
        bufB = [A.alloc(nB) for _ in range(4)]
        pl = A.alloc(nq, BF16)
        mst = A.alloc(nq, BF16)
        tmp8 = A.alloc(8)
        lo, hi = slice(0, 64), slice(64, 128)
        zpad = I("dve", "memset", zb[:, 0:8], 0.0)
        if l == 0:
            issue_casts(0, 99)

        def pool_chain(eng, zsrc, bufs, n, pc, deps):
            zf, T0, T1, T2 = bufs

            def add(o, x, y, d=()):
                return I(eng, "tensor_tensor", out=o, in0=x, in1=y, op=ALU.add, deps=d)
            I(eng, "tensor_copy", out=zf, in_=zsrc, deps=deps)
            add(T0[:, 0:n - 1], zf[:, 0:n - 1], zf[:, 1:n])
            add(T1[:, 0:n - 3], T0[:, 0:n - 3], T0[:, 2:n - 1])
            if pc == 0:
                add(T2[lo, 8:n - 8], T0[lo, 7:n - 9], zf[lo, 9:n - 7])
                last = add(T2[hi, 8:n - 8], T1[hi, 6:n - 10], zf[hi, 10:n - 6])
            else:
                add(T0[:, 0:n - 7], T1[:, 0:n - 7], T1[:, 4:n - 3])
                add(T1[hi, 0:n - 15], T0[hi, 0:n - 15], T0[hi, 8:n - 7])
                add(T2[lo, 8:n - 8], T0[lo, 4:n - 12], zf[lo, 12:n - 4])
                last = add(T2[hi, 8:n - 8], T1[hi, 0:n - 16], zf[hi, 16:n])
            return last

        for pc in range(2):
            ld = DMA("sp", zb[:, 8:W], zT[pc, :, 0:nq + 8], dsem="attld")
            lastB = pool_chain("pool", zb[:, LB:W], bufB, nB, pc, [ld, zpad])
            lastA = pool_chain("dve", zb[:, 0:nA], bufA, nA, pc, [ld, zpad])
            zfA, _, _, T2A = bufA
            zfB, _, _, T2B = bufB
            edge = I("pool", "tensor_tensor", out=tmp8, in0=T2A[:, 8:16], in1=pconst_sb[:, 2 + pc * 8:2 + pc * 8 + 8],
                     op=ALU.mult, deps=[const_ld, lastA])
            inv = pconst_sb[:, pc:pc + 1]
            I("dve", "scalar_tensor_tensor", out=pl[:, 0:H - 8], in0=T2A[:, 8:H], scalar=inv, in1=zfA[:, 8:H],
              op0=ALU.mult, op1=ALU.subtract, deps=[const_ld])
            I("dve", "scalar_tensor_tensor", out=pl[:, H - 8:nq], in0=T2B[:, H - LB:nq + 8 - LB], scalar=inv,
              in1=zfB[:, H - LB:nq + 8 - LB], op0=ALU.mult, op1=ALU.subtract, deps=[lastB])
            plast = I("dve", "tensor_tensor", out=pl[:, 0:8], in0=tmp8, in1=zfA[:, 8:16], op=ALU.subtract, deps=[edge])
            evl = None
            for tq in range(nq // TT):
                bk = tq % 2
                mm = I("pe", "matmul", ps[bk], wpool_sb[:, (l * 2 + pc) * 128:(l * 2 + pc + 1) * 128],
                       pl[:, tq * TT:(tq + 1) * TT], start=True, stop=True, deps=[plast, wpool_ld, bank_rd[bk]])
                evl = I("act", "activation", out=mst[:, tq * TT:(tq + 1) * TT], in_=ps[bk], func=AF.Identity,
                        scale=pscale_sb[:, l * 2 + pc:l * 2 + pc + 1], deps=[mm])
                bank_rd[bk] = evl
            DMA("sp", mixT[pc, :, 0:nq], mst, deps=[evl], dsem="mixw")
            P.barrier()
        A.release(ma)
        if stop == "pool" and l == 0:
            if DEBUG:
                P.barrier()
                final_deps.append(DMA("sp", dbg["mixT"][:, :, :], mixT[:, :, :], dsem="dbg2"))
            P.emit(nc, final_deps)
            stack.close()
            return nc
        def attn_group(QT, KT, VT, bias_of, out_fn, Lq, Lk, d, R, E0b, Eb, build_v, pe_deps):
            nm = -(-Lk // 128)
            off = (128 - R) % 128
            ns = -(-(Lq + off) // 128)
            QTv = QT.rearrange("p (j s) -> p j s", s=d)
            KTv = KT.rearrange("p (j s) -> p j s", s=d)
            VTv = VT.rearrange("p (j s) -> p j s", s=d)
            vops = []
            if build_v:
                psts = [ps[k].bitcast(BF16).rearrange("p (t c) -> p t c", c=128) for k in range(2)]
                tiles = [(r, m) for r in range(d) for m in range(nm)]
                grp_rd = [None, None]
                for gi in range(0, len(tiles), 4):
                    grp = tiles[gi:gi + 4]
                    hb = (gi // 4) % 2
                    pst = psts[hb]
                    tl = None
                    for j, (r, m) in enumerate(grp):
                        ks = min(128, Lk - 128 * m)
                        tl = I("pe", "transpose", pst[0:ks, j, :], VTv[:, 128 * m:128 * m + ks, r], identb,
                               deps=[grp_rd[hb], pe_deps] if j == 0 else [])
                    n = len(grp)
                    if hb == 0:
                        o0 = I("act", "activation", out=vaug_v[:, gi:gi + n, 0:64], in_=pst[:, 0:n, 0:64],
                               func=AF.Copy, deps=[tl])
                        o1 = I("act", "activation", out=vaug_v[:, gi:gi + n, 192:256], in_=pst[:, 0:n, 64:128],
                               func=AF.Copy, deps=[tl])
                    else:
                        o0 = I("dve", "tensor_copy", out=vaug_v[:, gi:gi + n, 0:64], in_=pst[:, 0:n, 0:64], deps=[tl])
                        o1 = I("dve", "tensor_copy", out=vaug_v[:, gi:gi + n, 192:256], in_=pst[:, 0:n, 64:128],
                               deps=[tl])
                    grp_rd[hb] = [o0, o1]
                    vops = [v for v in vops if v.eng != o1.eng] + [o1]
            steps = []
            for r in range(d):
                for m in range(nm):
                    k0 = 128 * m
                    ks = min(128, Lk - k0)
                    qlo = max(0, k0 - R)
                    qhi = min(Lq, k0 + 128 + R)
                    if ks <= 0 or qhi <= qlo:
                        continue
                    pieces = []
                    for s_ in range((qlo + off) // 128, (qhi - 1 + off) // 128 + 1):
                        a = max(qlo, 128 * s_ - off)
                        b = min(qhi, 128 * s_ - off + 128)
                        pieces.append((s_, a, b))
                    steps.append((r, m, k0, ks, qlo, qhi, pieces))
            ncontrib = {}
            for (r, m, k0, ks, qlo, qhi, pieces) in steps:
                for (s_, a, b) in pieces:
                    ncontrib[(r, s_)] = ncontrib.get((r, s_), 0) + 1
            seen = {}
            total_sb = d * ns
            gen_started = [[-1, -1], [-1, -1]]
            cur_evacs = [[[], []], [[], []]]
            gen_done = {}
            Lops, Eops, Xops = {}, {}, {}
            pv_last = {}
            nsteps = len(steps)
            for i in range(nsteps + SK):
                if i < nsteps:
                    (r, m, k0, ks, qlo, qhi, pieces) = steps[i]
                    sl = i % 2
                    sl3 = i % (SK + 1)
                    nqs = qhi - qlo
                    bo = qlo - (k0 - R)
                    for hh in range(2):
                        rows = slice(64 * hh, 64 * hh + 64)
                        psS = ps[hh * 2 + sl]
                        bt = bias_of(hh)
                        eb = Eb[hh][sl3]
                        e0 = E0b[hh][sl3]
                        so = I("pe", "matmul", psS[0:ks, 0:nqs], KTv[rows, k0:k0 + ks, r], QTv[rows, qlo:qhi, r],
                               start=True, stop=True,
                               deps=[Xops.get((i - 2, hh)), (vops + [pe_deps]) if i < 2 else None])
                        xo = I("act", "activation", out=e0[0:ks, 0:nqs], in_=psS[0:ks, 0:nqs], func=AF.Exp,
                               deps=[so, Eops.get((i - SK - 1, hh))])
                        Xops[(i, hh)] = xo
                        eo = I("dve" if hh == 0 else "pool", "tensor_tensor", out=eb[0:ks, 0:nqs], in0=e0[0:ks, 0:nqs],
                               in1=bt[0:ks, bo:bo + nqs], op=ALU.mult, deps=[xo, pv_last.get((i - SK - 1, hh)), wtab_ready])
                        lo_ = eo
                        Lops[(i, hh)] = lo_
                        Eops[(i, hh)] = eo
                if i >= SK:
                    j = i - SK
                    (r, m, k0, ks, qlo, qhi, pieces) = steps[j]
                    ti = r * nm + m
                    for hh in range(2):
                        eb = Eb[hh][j % (SK + 1)]
                        pvl = None
                        groups = []
                        for (s_, a, b) in pieces:
                            sbi = r * ns + s_
                            G = sbi // 4
                            col = (sbi % 4) * 128 + (a - (128 * s_ - off))
                            if groups and groups[-1]["G"] == G:
                                groups[-1]["pcs"].append((s_, a, b, col))
                            else:
                                groups.append(dict(G=G, pcs=[(s_, a, b, col)]))
                        for grp in groups:
                            G = grp["G"]
                            bank = G % 2
                            acc = ps[4 + hh * 2 + bank]
                            a0 = grp["pcs"][0][1]
                            b1 = grp["pcs"][-1][2]
                            col0 = grp["pcs"][0][3]
                            assert grp["pcs"][-1][3] + (b1 - grp["pcs"][-1][1]) - col0 == b1 - a0
                            first = gen_started[hh][bank] != G
                            deps = [Eops[(j, hh)]]
                            if first:
                                gen_started[hh][bank] = G
                                deps.append(cur_evacs[hh][bank])
                                cur_evacs[hh][bank] = []
                            pvl = I("pe", "matmul", acc[:, col0:col0 + (b1 - a0)], vaug_v[0:ks, ti, hh * 128:(hh + 1) * 128],
                                    eb[0:ks, a0 - qlo:b1 - qlo], start=first, stop=False, skip_group_check=True, deps=deps)
                            for (s_, a, b, col) in grp["pcs"]:
                                seen[(hh, r, s_)] = seen.get((hh, r, s_), 0) + 1
                                if seen[(hh, r, s_)] == ncontrib[(r, s_)]:
                                    gd = gen_done.setdefault((hh, G), [])
                                    gd.append((r, s_, a, b, col))
                                    if len(gd) == min(4, total_sb - 4 * G):
                                        evs_ = out_fn(hh, G, gd, acc, pvl)
                                        for h2 in range(2):
                                            cur_evacs[h2][bank].extend(evs_)
                        pv_last[(j, hh)] = pvl

        mb = A.mark()
        QT = A.alloc(NLOC, BF16)
        KT = A.alloc(NLOC, BF16)
        VT = A.alloc(NLOC, BF16)
        Ynum = [A.alloc(nq) for _ in range(3)]
        Dsum = A.alloc(nq)
        lgb = [[A.alloc(384, BF16) for _ in range(SK + 1)] for _ in range(2)]
        Eb = [[A.alloc(384, BF16) for _ in range(SK + 1)] for _ in range(2)]
        for g in range(3):
            d = DIL[g]
            DMA("sp", QT[:, 0:nq], zT[CH_QB[g], :, 0:nq], dsem="attld")
            DMA("sp", KT[:, 0:nkv], zT[CH_KB[g], :, 0:nkv], dsem="attld")
            ldl = DMA("sp", VT[:, 0:nkv], zT[CH_VB[g], :, 0:nkv], dsem="attld")
            Yv = Ynum[g].rearrange("p (j s) -> p j s", s=d)
            Dv = Dsum.rearrange("p (j s) -> p j s", s=d)

            def out_B(hh, G, slots, acc, pvl, g=g, Yv=Yv, Dv=Dv):
                nrows = slice(0, 64) if hh == 0 else slice(64, 128)
                drows = slice(64, 128) if hh == 0 else slice(0, 64)
                ops = []
                for (r, s_, a, b, col) in slots:
                    n = b - a
                    ops.append(I("act", "activation", out=Yv[nrows, a:b, r], in_=acc[nrows, col:col + n], func=AF.Copy,
                                 deps=[pvl]))
                    if g == 0:
                        ops.append(I("dve", "tensor_copy", out=Dv[nrows, a:b, r], in_=acc[drows, col:col + n], deps=[pvl]))
                    else:
                        ops.append(I("dve", "tensor_tensor", out=Dv[nrows, a:b, r], in0=Dv[nrows, a:b, r],
                                     in1=acc[drows, col:col + n], op=ALU.add, deps=[pvl]))
                return ops

            attn_group(QT[:, 0:nq], KT[:, 0:nkv], VT[:, 0:nkv],
                       (lambda hh, g=g: biasB_sb[:, (2 * g + hh) * 256:(2 * g + hh + 1) * 256]),
                       out_B, nq // d, nkv // d, d, 64, lgb, Eb, True, ldl)
            P.barrier()
        I("act", "activation", out=Dsum, in_=Dsum, func=AF.Ln)
        rcp = I("act", "activation", out=Dsum, in_=Dsum, func=AF.Exp, scale=-1.0)
        mstb = [KT[:, 0:nq], VT[:, 0:nq], QT[:, 0:nq]]
        for g in range(3):
            eng = "dve" if g != 1 else "pool"
            mo = I(eng, "tensor_tensor", out=mstb[g], in0=Ynum[g], in1=Dsum, op=ALU.mult, deps=[rcp])
            DMA("sp", mixT[2 + g, :, 0:nq], mstb[g], deps=[mo], dsem="mixw")
        P.barrier()
        A.release(mb)

        if stop == "attb" and l == 0:
            if DEBUG:
                P.barrier()
                final_deps.append(DMA("sp", dbg["mixT"][:, :, :], mixT[:, :, :], dsem="dbg2"))
            P.emit(nc, final_deps)
            stack.close()
            return nc
        mc = A.mark()
        w_out_sb = A.alloc(KC * D, BF16)
        wout_ld = None
        for q in range(2):
            wout_ld = DMA("pool", w_out_sb[:, q * 4096:(q + 1) * 4096], w_out[l, :, q * 4096:(q + 1) * 4096], dsem="wout")
        nkc = nq + 128
        KT = A.alloc(nkc, BF16)
        VT = A.alloc(nkc, BF16)
        QTc = [A.alloc(nq, BF16) for _ in range(2)]
        mstc = [A.alloc(nq, BF16) for _ in range(2)]
        Dt = [A.alloc(512) for _ in range(2)]
        lgb = [[A.alloc(384, BF16) for _ in range(SK + 1)] for _ in range(2)]
        Eb = [[A.alloc(384, BF16) for _ in range(SK + 1)] for _ in range(2)]
        DMA("sp", KT, zT[CH_KC, :, 0:nkc], dsem="attld")
        ldk = DMA("sp", VT, zT[CH_VC, :, 0:nkc], dsem="attld")
        dctr = [0]
        for c in range(3):
            qs = c % 2
            ldq = DMA("sp", QTc[qs], zT[CH_QC[c], :, 0:nq], dsem="attq%d" % qs)
            state = {}

            def out_C(hh, G, slots, acc, pvl, c=c, qs=qs, state=state):
                state[(G, hh)] = (acc, pvl)
                if (G, 0) not in state or (G, 1) not in state:
                    return []
                acc0, pv0 = state[(G, 0)]
                acc1, pv1 = state[(G, 1)]
                Dd = Dt[dctr[0] % 2]
                dctr[0] += 1
                q0 = G * 512
                ops = []
                ops.append(I("dve", "tensor_copy", out=Dd[0:64, :], in_=acc0[64:128, :], deps=[pv0, pv1]))
                ops.append(I("dve", "tensor_copy", out=Dd[64:128, :], in_=acc1[0:64, :]))
                cp2 = ops[-1]
                ops.append(I("act", "activation", out=Dd, in_=Dd, func=AF.Ln, bias=esink[:, l * 3 + c:l * 3 + c + 1],
                             deps=[cp2]))
                rcp_ = I("act", "activation", out=Dd, in_=Dd, func=AF.Exp, scale=-1.0)
                ops.append(rcp_)
                ops.append(I("dve", "tensor_tensor", out=mstc[qs][0:64, q0:q0 + 512], in0=acc0[0:64, :],
                             in1=Dd[0:64, :], op=ALU.mult, deps=[rcp_]))
                ops.append(I("dve", "tensor_tensor", out=mstc[qs][64:128, q0:q0 + 512], in0=acc1[64:128, :],
                             in1=Dd[64:128, :], op=ALU.mult))
                state["last"] = ops[-1]
                return ops

            attn_group(QTc[qs], KT, VT,
                       (lambda hh, c=c: biasC_sb[:, (2 * c + hh) * 384:(2 * c + hh + 1) * 384]),
                       out_C, nq, nkc, 1, 128, lgb, Eb, c == 0, [ldq, ldk])
            DMA("sp", mixT[5 + c, :, 0:nq], mstc[qs], deps=[state["last"]], dsem="mixc%d" % qs)
            P.barrier()
        A.release(mc)
        if DEBUG and l == 0:
            final_deps.append(DMA("sp", dbg["mixT"][:, :, :], mixT[:, :, :], dsem="dbg"))
            P.barrier()

        if stop == "attc" and l == 0:
            if DEBUG:
                P.barrier()
                final_deps.append(DMA("sp", dbg["mixT"][:, :, :], mixT[:, :, :], dsem="dbg2"))
            P.emit(nc, final_deps)
            stack.close()
            return nc
        m2 = A.mark()
        w_out_sb = A.alloc(KC * D, BF16)
        w_out_v = w_out_sb.rearrange("p (k n) -> p k n", k=KC)
        xt = [A.alloc(KC * TT) for _ in range(2)]
        mx = [A.alloc(KC * TT, BF16) for _ in range(2)]
        sqb = A.alloc(KC * TT, BF16)
        tn = A.alloc(KC * TT)
        rstd = A.alloc(TT)
        h = A.alloc(KC * TT, BF16)
        hid = A.alloc(FC * TT, BF16)
        sg = [A.alloc(TT) for _ in range(2)]
        NWG, NWD = 5, 3
        wgs = [A.alloc(2048, BF16) for _ in range(NWG)]
        wds = [A.alloc(DFF, BF16) for _ in range(NWD)]
        hv = h.rearrange("p (k t) -> p k t", k=KC)
        nt2 = nq // TT
        nrm = Norm(sqb, tn, rstd, ps[6])
        st = dict(x_rd=[None, None], mx_rd=[None, None], h_rd=None, hid_rd=None, xw=[None, None],
                  wgc=0, wdc=0, lastres={}, ldx={}, ldm={}, hready={}, fin=None)
        wgs_rd = [None] * NWG
        wds_rd = [None] * NWD
        sg_rd = [None, None]
        bank_rd = [None] * 8

        def m2_loads(t):
            s = t % 2
            st["ldx"][t] = DMA("pool", xt[s].rearrange("p (k t) -> p k t", k=KC), xsrc[:, :, t * TT:(t + 1) * TT],
                               deps=[st["x_rd"][s], st["xw"][s]], dsem="x%d" % s)
            st["ldm"][t] = DMA("pool", mx[s].rearrange("p (k t) -> p k t", k=KC),
                               mixT[:, :, t * TT:(t + 1) * TT].rearrange("c p n -> p c n"),
                               deps=[st["mx_rd"][s]], dsem="mx%d" % s)

        def m2_outproj(t):
            s = t % 2
            xs = xt[s]
            mxv = mx[s].rearrange("p (k t) -> p k t", k=KC)
            res = None
            lastmm = None
            for dc in range(KC):
                bk = 4 + dc % 2
                for kc in range(KC):
                    lastmm = I("pe", "matmul", ps[bk], w_out_v[:, kc, dc * 128:(dc + 1) * 128], mxv[:, kc, :],
                               start=(kc == 0), stop=(kc == KC - 1),
                               deps=[st["ldm"][t], wout_ld, bank_rd[bk]] if kc == 0 else [])
                res = I("dve", "scalar_tensor_tensor", out=xs[:, dc * TT:(dc + 1) * TT], in0=ps[bk], scalar=modv(l, 2, dc),
                        in1=xs[:, dc * TT:(dc + 1) * TT], op0=ALU.mult, op1=ALU.add, deps=[lastmm, st["ldx"][t]])
                bank_rd[bk] = res
            st["mx_rd"][s] = lastmm
            nrm.part1(xs, res)

        def m2_norm2(t):
            nrm.out_war = st["h_rd"]
            st["hready"][t] = nrm.part2([hv[:, kc, :] for kc in range(KC)], [coefv(l, 1, kc) for kc in range(KC)],
                                        [modv(l, 3, kc) for kc in range(KC)])

        def m2_down(t, dcs):
            s = t % 2
            xs = xt[s]
            res = None
            lastmm = None
            for dc in dcs:
                ws = st["wdc"] % NWD
                st["wdc"] += 1
                wl = DMA("sp", wds[ws], wd_bf[l, dc], deps=[cast_done[l], wds_rd[ws]], dsem="wd%d" % ws)
                bk = 4 + dc % 2
                wv = wds[ws].rearrange("p (f n) -> p f n", f=FC)
                for fc in range(FC):
                    lastmm = I("pe", "matmul", ps[bk], wv[:, fc, :], hid[:, fc * TT:(fc + 1) * TT],
                               start=(fc == 0), stop=(fc == FC - 1),
                               deps=[st["ho"], wl, bank_rd[bk]] if fc == 0 else [])
                wds_rd[ws] = lastmm
                res = I("dve", "scalar_tensor_tensor", out=xs[:, dc * TT:(dc + 1) * TT], in0=ps[bk], scalar=modv(l, 5, dc),
                        in1=xs[:, dc * TT:(dc + 1) * TT], op0=ALU.mult, op1=ALU.add, deps=[lastmm])
                bank_rd[bk] = res
            st["hid_rd"] = lastmm
            st["lastres"][t] = res

        def m2_finish(t):
            s = t % 2
            xs = xt[s]
            res = st["lastres"][t]
            if l == 0:
                def fin0(stage, t=t, s=s, xs=xs, res=res):
                    if stage == 3:
                        st["xw"][s] = DMA("pool", x1T[:, :, t * TT:(t + 1) * TT], xs.rearrange("p (k t) -> p k t", k=KC),
                                          deps=[res], dsem="xw%d" % s)
                        st["x_rd"][s] = res
                st["fin"] = fin0
            else:
                def fin(stage, t=t, s=s, xs=xs, res=res):
                    if stage == -1:
                        nrmf.part1(xs, res)
                    elif stage == 0:
                        nrmf.p2a()
                    elif stage == 1:
                        nrmf.p2b()
                    elif stage == 2:
                        nrmf.p2c()
                    else:
                        nrmf.out_war = None
                        fo = nrmf.p2d([xs[:, kc * TT:(kc + 1) * TT] for kc in range(KC)],
                                      [ngs_sb[:, 2 * L * KC + kc:2 * L * KC + kc + 1] for kc in range(KC)], None)
                        st["xw"][s] = DMA("pool", yT[:, :, t * TT:(t + 1) * TT], xs.rearrange("p (k t) -> p k t", k=KC),
                                          deps=[fo], dsem="xw%d" % s)
                        st["x_rd"][s] = fo
                        final_deps.append(st["xw"][s])
                st["fin"] = fin

        nrmf = nrm
        m2_loads(0)
        m2_outproj(0)
        m2_norm2(0)
        for t in range(nt2):
            ho = None
            for fc in range(FC):
                if st["fin"] is not None and fc in (1, 3, 4, 5, 10):
                    st["fin"]({1: -1, 3: 0, 4: 1, 5: 2, 10: 3}[fc])
                    if fc == 10:
                        st["fin"] = None
                if fc == 8 and l == 0:
                    issue_casts(1, 4 if t < nt2 - 1 else 99)
                if fc == 11 and t + 1 < nt2:
                    m2_loads(t + 1)
                ws = st["wgc"] % NWG
                st["wgc"] += 1
                wl = DMA("sp", wgs[ws], wgu_bf[l, fc], deps=[cast_done[l], wgs_rd[ws]], dsem="wg%d" % ws)
                pb = (fc % 2) * 2
                wv = wgs[ws].rearrange("p (g k n) -> p g k n", g=2, k=KC)
                lastmm = None
                for gu in range(2):
                    for kc in range(KC):
                        lastmm = I("pe", "matmul", ps[pb + gu], wv[:, gu, kc, :], hv[:, kc, :],
                                   start=(kc == 0), stop=(kc == KC - 1),
                                   deps=[st["hready"][t], wl, bank_rd[pb + gu]] if kc == 0 else [])
                wgs_rd[ws] = lastmm
                sgi = fc % 2
                so = I("act", "activation", out=sg[sgi], in_=ps[pb], func=AF.Silu, deps=[lastmm, sg_rd[sgi]])
                ho = I("dve", "tensor_tensor", out=hid[:, fc * TT:(fc + 1) * TT], in0=sg[sgi], in1=ps[pb + 1],
                       op=ALU.mult, deps=[so, st["hid_rd"]])
                sg_rd[sgi] = ho
                bank_rd[pb] = ho
                bank_rd[pb + 1] = ho
            st["h_rd"] = lastmm
            st["ho"] = ho
            if t + 1 < nt2:
                m2_outproj(t + 1)
            m2_down(t, range(0, 4))
            if t + 1 < nt2:
                m2_norm2(t + 1)
            m2_down(t, range(4, 8))
            m2_finish(t)
        if st["fin"] is not None:
            for stage in range(-1, 4):
                st["fin"](stage)
            st["fin"] = None
        P.barrier()
        A.release(m2)
        if DEBUG and l == 0:
            final_deps.append(DMA("sp", dbg["x1T"][:, :, :], x1T[:, :, :], dsem="dbg"))
            P.barrier()
        if stop == "m2" and l == 0:
            if DEBUG:
                P.barrier()
                final_deps.append(DMA("sp", dbg["mixT"][:, :, :], mixT[:, :, :], dsem="dbg2"))
            P.emit(nc, final_deps)
            stack.close()
            return nc

    P.emit(nc, final_deps)
    stack.close()
    return nc


def _perm_in():
    o_q_b = 256
    o_k_b = o_q_b + 384
    o_v_b = o_k_b + 384
    o_q_c = o_v_b + 384
    o_k_c = o_q_c + 384
    o_v_c = o_k_c + 128
    cols = list(range(0, 256))
    cols += list(range(o_q_b, o_q_b + 384))
    cols += list(range(o_k_b, o_k_b + 384))
    cols += list(range(o_v_b, o_v_b + 384))
    for c in range(3):
        cols += list(range(o_q_c + 64 * c, o_q_c + 64 * c + 64))
        cols += list(range(o_q_c + 64 * (c + 3), o_q_c + 64 * (c + 3) + 64))
    cols += list(range(o_k_c, o_k_c + 128))
    cols += list(range(o_v_c, o_v_c + 128))
    return np.array(cols)


def _perm_mix():
    rows = list(range(0, 640))
    for c in range(3):
        rows += list(range(640 + 64 * c, 640 + 64 * c + 64))
        rows += list(range(640 + 64 * (c + 3), 640 + 64 * (c + 3) + 64))
    return np.array(rows)


def _const_tables():
    i = np.arange(1, 13, dtype=np.float64)
    slopes = np.exp2(-8.0 * i / 12.0)
    s_win = slopes[:6]
    s_dil = slopes[6:]
    p = np.arange(128)[:, None]
    biasC = np.zeros((128, 6, 384), np.float32)
    j = np.arange(384)[None, :]
    dist = np.abs(j - 128 - p)
    for c in range(3):
        for hh in range(2):
            head = c + 3 * hh
            biasC[:, 2 * c + hh, :] = np.where(dist <= 128, -s_win[head] * dist, NEGM)
    biasB = np.zeros((128, 6, 256), np.float32)
    j = np.arange(256)[None, :]
    dist = np.abs(j - 64 - p)
    for g in range(3):
        for hh in range(2):
            biasB[:, 2 * g + hh, :] = np.where(dist <= 64, -s_dil[2 * g + hh] * DIL[g] * dist, NEGM)
    pconst = np.zeros((128, 18), np.float32)
    rr = ((1, 2), (4, 8))
    for pc in range(2):
        for half in range(2):
            r = rr[pc][half]
            rows = slice(64 * half, 64 * half + 64)
            pconst[rows, pc] = 1.0 / (2 * r + 1)
            tt = np.arange(8)
            pconst[rows, 2 + pc * 8:2 + pc * 8 + 8] = 1.0 / (np.minimum(tt, r) + r + 1)

    def wtab(b):
        import ml_dtypes
        w = np.where(b <= NEGM / 2, 0.0, np.exp(b.astype(np.float64)))
        return w.astype(ml_dtypes.bfloat16).astype(np.float32)

    return wtab(biasC).reshape(128, -1), wtab(biasB).reshape(128, -1), pconst


def _prep(inputs):
    x = np.asarray(inputs["x"], np.float32)
    c = np.asarray(inputs["c"], np.float32)
    g = lambda k: np.asarray(inputs[k], np.float32)
    w_ada, b_ada, w_in, w_pool = g("w_ada"), g("b_ada"), g("w_in"), g("w_pool")
    pool_scale, sink_logit, w_out = g("pool_scale"), g("sink_logit"), g("w_out")
    w_gate, w_up, w_down = g("w_gate"), g("w_up"), g("w_down")
    n1, n2, fgv = g("norm1_g"), g("norm2_g"), g("final_g")
    pin, pmix = _perm_in(), _perm_mix()
    sh = {}
    sh["w_adaT"] = np.ascontiguousarray(
        np.stack([w_ada[l].T.reshape(48, 128, D).transpose(1, 0, 2) for l in range(L)]))
    sh["b_adaT"] = np.ascontiguousarray(
        np.concatenate([b_ada[l].reshape(48, 128).T for l in range(L)], axis=1))
    vec = lambda v: v.reshape(KC, 128).T
    sh["ngs"] = np.ascontiguousarray(np.concatenate(
        [vec(n1[l]) for l in range(L)] + [vec(n2[l]) for l in range(L)] + [vec(fgv)], axis=1))
    sh["w_in"] = np.ascontiguousarray(np.stack(
        [w_in[l][:, pin].reshape(KC, 128, 2048).transpose(1, 0, 2).reshape(128, KC * 2048) for l in range(L)]))
    sh["w_out"] = np.ascontiguousarray(np.stack(
        [w_out[l][pmix, :].reshape(KC, 128, D).transpose(1, 0, 2).reshape(128, KC * D) for l in range(L)]))
    wgu = np.empty((L, FC, 128, 2, KC, 128), np.float32)
    for l in range(L):
        wgu[l, :, :, 0] = w_gate[l].reshape(KC, 128, FC, 128).transpose(2, 1, 0, 3)
        wgu[l, :, :, 1] = w_up[l].reshape(KC, 128, FC, 128).transpose(2, 1, 0, 3)
    sh["wgu"] = wgu.reshape(L, FC, 128, 2048)
    sh["wd"] = np.ascontiguousarray(np.stack(
        [w_down[l].reshape(FC, 128, KC, 128).transpose(2, 1, 0, 3).reshape(KC, 128, DFF) for l in range(L)]))
    wp = np.zeros((128, L, 2, 128), np.float32)
    for l in range(L):
        for pc in range(2):
            wp[0:64, l, pc, 0:64] = w_pool[l, 2 * pc]
            wp[64:128, l, pc, 64:128] = w_pool[l, 2 * pc + 1]
    sh["wpool"] = wp.reshape(128, -1)
    sh["pscale"] = np.ascontiguousarray(np.concatenate([pool_scale[l].reshape(2, 128).T for l in range(L)], axis=1))
    sk = np.zeros((128, L, 3), np.float32)
    for l in range(L):
        for cc in range(3):
            sk[0:64, l, cc] = sink_logit[l, cc]
            sk[64:128, l, cc] = sink_logit[l, cc + 3]
    sh["sink"] = sk.reshape(128, -1)
    bC, bB, pc_ = _const_tables()
    sh["biasC"], sh["biasB"], sh["pconst"] = bC, bB, pc_
    sh["ident"] = np.eye(128, dtype=np.float32)
    in_maps = []
    for i in range(8):
        b, half = i // 2, i % 2
        idx = np.arange(NLOC) if half == 0 else (8191 - np.arange(NLOC))
        xl = x[b][idx]
        m = dict(sh)
        m["xT"] = np.ascontiguousarray(xl.reshape(NLOC, KC, 128).transpose(2, 1, 0))
        m["c_rep"] = np.ascontiguousarray(np.broadcast_to(c[b], (128, D)))
        in_maps.append(m)
    return in_maps


_NC_CACHE = {}


def kernel(**inputs):
    in_maps = _prep(inputs)
    if "nc" not in _NC_CACHE:
        _NC_CACHE["nc"] = build_program()
    nc = _NC_CACHE["nc"]
    res = run_bass_kernel_spmd(nc, in_maps, core_ids=list(range(8)))
    out = np.empty((4, 8192, D), np.float32)
    for i in range(8):
        b, half = i // 2, i % 2
        yT = np.asarray(res.results[i]["yT"])
        y = yT.transpose(2, 1, 0).reshape(NOUT, D)
        if half == 0:
            out[b, 0:NOUT] = y
        else:
            out[b, 8191 - np.arange(NOUT)] = y
    if DEBUG:
        kernel.debug = res.results
    return out
```

```python
import numpy as np
import concourse.bass as bass
import concourse.mybir as mybir
from concourse.bass_utils import run_bass_kernel_spmd

F32 = mybir.dt.float32
BF16 = mybir.dt.bfloat16
ALU = mybir.AluOpType
AF = mybir.ActivationFunctionType
AX = mybir.AxisListType

D = 1024
KC = 8
NLOC = 6144
L = 2
NKV = (6144, 5120)
NQ = (5120, 4096)
NOUT = 4096
TT = 512
DFF = 2816
FC = 22
EPS = 1e-6
NEGM = -30000.0
DEBUG = False

CH_POOL = (0, 1)
CH_QB = (2, 3, 4)
CH_KB = (5, 6, 7)
CH_VB = (8, 9, 10)
CH_QC = (11, 12, 13)
CH_KC = 14
CH_VC = 15
Q_CHUNKS = set(CH_QB) | set(CH_QC)
DIL = (1, 4, 16)
SK = 4


class Op:
    __slots__ = ("eng", "fn", "deps", "sig", "cnt", "dsem", "dval")


class Prog:
    ENGS = ("pe", "act", "dve", "pool", "sp")

    def __init__(self):
        self.ops = {e: [] for e in self.ENGS}
        self.dtot = {}
        self.last_dma = {}
        self.pending = {e: [] for e in self.ENGS}

    def add(self, eng, meth, args, kw, deps=(), dsem=None):
        op = Op()
        op.eng = eng
        op.fn = (meth, args, kw)
        op.sig = False
        op.cnt = 0
        op.dsem = dsem
        op.dval = 0
        dl = []

        def flat(x):
            if x is None:
                return
            if isinstance(x, (list, tuple)):
                for y in x:
                    flat(y)
            else:
                dl.append(x)

        flat(deps)
        if self.pending[eng]:
            dl.extend(self.pending[eng])
            self.pending[eng] = []
        op.deps = dl
        for d in dl:
            if d.dsem is None and d.eng != eng:
                d.sig = True
        if dsem is not None:
            self.dtot[dsem] = self.dtot.get(dsem, 0) + 16
            op.dval = self.dtot[dsem]
            self.last_dma[dsem] = op
        self.ops[eng].append(op)
        return op

    def barrier(self):
        deps = []
        for e in self.ENGS:
            for op in reversed(self.ops[e]):
                if op.dsem is None:
                    deps.append(op)
                    break
        deps.extend(self.last_dma.values())
        for e in self.ENGS:
            self.pending[e] = list(self.pending[e]) + deps

    def emit(self, nc, final_deps):
        for e in self.ENGS:
            c = 0
            for op in self.ops[e]:
                if op.dsem is None and op.sig:
                    c += 1
                    op.cnt = c
        import contextlib
        with contextlib.ExitStack() as st:
            esem = {e: st.enter_context(nc.semaphore("s_" + e)) for e in self.ENGS}
            dsem = {k: st.enter_context(nc.semaphore("d_" + k)) for k in self.dtot}
            block = st.enter_context(nc.Block())

            def run(engobj, name):
                waited = {}

                def wait_for(d):
                    if d.dsem is not None:
                        key, sem, val = "d_" + d.dsem, dsem[d.dsem], d.dval
                    elif d.eng == name:
                        return
                    else:
                        key, sem, val = "e_" + d.eng, esem[d.eng], d.cnt
                        assert val > 0
                    if waited.get(key, 0) < val:
                        engobj.wait_ge(sem, val)
                        waited[key] = val

                for op in self.ops[name]:
                    for d in op.deps:
                        wait_for(d)
                    ins = getattr(engobj, op.fn[0])(*op.fn[1], **op.fn[2])
                    if op.dsem is not None:
                        ins.then_inc(dsem[op.dsem], 16)
                    elif op.sig:
                        ins.then_inc(esem[name], 1)
                if name == "sp":
                    for d in final_deps:
                        wait_for(d)

            @block.tensor
            def _(e):
                run(e, "pe")

            @block.scalar
            def _(e):
                run(e, "act")

            @block.vector
            def _(e):
                run(e, "dve")

            @block.gpsimd
            def _(e):
                run(e, "pool")

            @block.sync
            def _(e):
                run(e, "sp")


class Arena:
    def __init__(self, ap, nwords):
        self.ap = ap
        self.n = nwords
        self.top = 0

    def alloc(self, nelem, dt=F32):
        words = nelem if dt == F32 else (nelem + 1) // 2
        a = self.top
        self.top += words
        assert self.top <= self.n, ("SBUF arena overflow", self.top, self.n)
        v = self.ap[:, a:a + words]
        return v if dt == F32 else v.bitcast(BF16)

    def mark(self):
        return self.top

    def release(self, m):
        self.top = m


def build_program(stop=None, DEBUG=DEBUG):
    nc = bass.Bass("TRN2", target_bir_lowering=False)

    def din(name, shape, dt=F32):
        return nc.dram_tensor(name, shape, dt, kind="ExternalInput").ap()

    xT = din("xT", [128, KC, NLOC])
    c_rep = din("c_rep", [128, D])
    w_adaT = din("w_adaT", [L, 128, 48, D])
    b_adaT = din("b_adaT", [128, L * 48])
    ngs = din("ngs", [128, (2 * L + 1) * KC])
    w_in = din("w_in", [L, 128, KC * 2048])
    w_out = din("w_out", [L, 128, KC * D])
    wgu = din("wgu", [L, FC, 128, 2048])
    wd = din("wd", [L, KC, 128, DFF])
    wpool = din("wpool", [128, L * 2 * 128])
    pscale = din("pscale", [128, L * 2])
    sink = din("sink", [128, L * 3])
    biasC = din("biasC", [128, 6 * 384])
    biasB = din("biasB", [128, 6 * 256])
    pconst = din("pconst", [128, 2 + 16])
    ident = din("ident", [128, 128])
    yT = nc.dram_tensor("yT", [128, KC, NOUT], F32, kind="ExternalOutput").ap()
    zT = nc.dram_tensor("zT", [16, 128, NLOC], BF16).ap()
    mixT = nc.dram_tensor("mixT", [8, 128, NQ[0]], BF16).ap()
    x1T = nc.dram_tensor("x1T", [128, KC, NQ[0]], F32).ap()
    wgu_bf = nc.dram_tensor("wgu_bf", [L, FC, 128, 2048], BF16).ap()
    wd_bf = nc.dram_tensor("wd_bf", [L, KC, 128, DFF], BF16).ap()
    dbg = {}
    if DEBUG:
        dbg["zT"] = nc.dram_tensor("dbg_zT", [16, 128, NLOC], BF16, kind="ExternalOutput").ap()
        dbg["mixT"] = nc.dram_tensor("dbg_mixT", [8, 128, NQ[0]], BF16, kind="ExternalOutput").ap()
        dbg["x1T"] = nc.dram_tensor("dbg_x1T", [128, KC, NQ[0]], F32, kind="ExternalOutput").ap()
        dbg["mod"] = nc.dram_tensor("dbg_mod", [128, L * 48], F32, kind="ExternalOutput").ap()

    import contextlib
    stack = contextlib.ExitStack()
    NW = 51200
    arena_t = stack.enter_context(nc.sbuf_tensor("arena", [128, NW], F32))
    A = Arena(arena_t[:, :], NW)
    ps = [stack.enter_context(nc.psum_tensor("ps%d" % i, [128, 512], F32)) for i in range(8)]
    ps = [p[:, :] for p in ps]
    P = Prog()
    final_deps = []

    def I(eng, meth, *args, deps=(), dsem=None, **kw):
        return P.add(eng, meth, args, kw, deps, dsem)

    def DMA(q, out, in_, deps=(), dsem=None):
        return P.add(q, "dma_start", (), dict(out=out, in_=in_), deps, dsem)

    identb = A.alloc(128, BF16)
    ones_f = A.alloc(128)
    ones_b = A.alloc(128, BF16)
    biasC_sb = A.alloc(6 * 384, BF16)
    biasB_sb = A.alloc(6 * 256, BF16)
    pconst_sb = A.alloc(18)
    ngs_sb = A.alloc((2 * L + 1) * KC)
    pscale_sb = A.alloc(L * 2)
    esink = A.alloc(L * 3)
    mod = A.alloc(L * 48)
    coef = A.alloc(L * 2 * KC)
    ident_f = A.alloc(128)
    wpool_sb = A.alloc(L * 2 * 128, BF16)
    vaug = A.alloc(48 * 256, BF16)
    vaug_v = vaug.rearrange("p (t c) -> p t c", c=256)

    def modv(l, m, kc):
        i = l * 48 + m * 8 + kc
        return mod[:, i:i + 1]

    def coefv(l, which, kc):
        i = (l * 2 + which) * KC + kc
        return coef[:, i:i + 1]

    const_ld = None
    for (dst, src) in ((ident_f, ident[:, :]), (pconst_sb, pconst[:, :]), (ngs_sb, ngs[:, :]), (pscale_sb, pscale[:, :]),
                       (esink, sink[:, :])):
        const_ld = DMA("sp", dst, src, dsem="const")
    wpool_ld = DMA("pool", wpool_sb, wpool[:, :], dsem="wpool")
    cast_done = [None, None]
    import os
    cast_list = {l: [(wgu_bf[l, fc], wgu[l, fc]) for fc in range(FC)] + [(wd_bf[l, dc], wd[l, dc]) for dc in range(KC)]
                 for l in range(L)}

    def issue_casts(lc, n):
        for _ in range(n):
            if cast_list[lc]:
                o, i_ = cast_list[lc].pop(0)
                cast_done[lc] = DMA("pool", o, i_, dsem="cast%d" % lc)

    bada_sb = A.alloc(L * 48)
    m0 = A.mark()
    scb = A.alloc(D)
    junk = A.alloc(D)
    wa = [A.alloc(8 * D) for _ in range(2)]
    bC_f = A.alloc(6 * 384)
    bB_f = A.alloc(6 * 256)
    DMA("sp", bC_f, biasC[:, :], dsem="bld")
    bld = DMA("sp", bB_f, biasB[:, :], dsem="bld")
    I("dve", "tensor_copy", out=biasC_sb, in_=bC_f, deps=[bld])
    wtab_ready = I("dve", "tensor_copy", out=biasB_sb, in_=bB_f)
    DMA("sp", scb, c_rep[:, :], dsem="cld")
    bada_ld = DMA("sp", bada_sb, b_adaT[:, :], dsem="cld")
    silu_op = I("act", "activation", out=scb, in_=scb, func=AF.Silu, deps=[bada_ld])
    I("dve", "tensor_copy", out=identb, in_=ident_f, deps=[const_ld])
    I("dve", "memset", ones_f, 1.0)
    I("dve", "memset", ones_b, 1.0)
    I("dve", "memset", vaug, 1.0)
    I("dve", "memset", mod, 0.0)
    I("act", "activation", out=esink, in_=esink, func=AF.Exp, deps=[const_ld])
    ada = dict(rd=[None, None], cnt=0, last=None)

    def adaln_block(l, j0, nj, wab, junkb):
        s = ada["cnt"] % 2
        ada["cnt"] += 1
        ld = DMA("sp" if l == 0 else "pool", wab[s].rearrange("p (j k) -> p j k", j=nj), w_adaT[l, :, j0:j0 + nj, :],
                 deps=[ada["rd"][s]], dsem="wa%d" % s)
        for jj in range(nj):
            i = l * 48 + j0 + jj
            ada["last"] = I("dve", "scalar_tensor_tensor", out=junkb, in0=wab[s][:, jj * D:(jj + 1) * D], scalar=1.0,
                            in1=scb, op0=ALU.mult, op1=ALU.mult, accum_out=mod[:, i:i + 1], deps=[ld, silu_op])
        ada["rd"][s] = ada["last"]

    def adaln_finish(l):
        modadd = I("pool", "tensor_tensor", out=mod[:, l * 48:(l + 1) * 48], in0=mod[:, l * 48:(l + 1) * 48],
                   in1=bada_sb[:, l * 48:(l + 1) * 48], op=ALU.add, deps=[bada_ld, ada["last"]])
        lastc = None
        for which in range(2):
            b0 = l * 48 + (1 + 3 * which) * 8
            sc = mod[:, b0:b0 + 8]
            g0 = (which * L + l) * KC
            c0 = (l * 2 + which) * KC
            lastc = I("dve", "scalar_tensor_tensor", out=coef[:, c0:c0 + KC], in0=sc, scalar=1.0, in1=ngs_sb[:, g0:g0 + KC],
                      op0=ALU.add, op1=ALU.mult, deps=[const_ld, modadd])
        return [modadd, lastc]

    for jb in range(6):
        adaln_block(0, jb * 8, 8, wa, junk)
    adaln_finish(0)
    ada["rd"] = [None, None]
    ada["cnt"] = 0
    if DEBUG:
        final_deps.append(DMA("sp", dbg["mod"][:, :], mod, deps=[P.ops["dve"][-1]], dsem="dbg"))
    P.barrier()
    A.release(m0)
    if stop == "prologue":
        P.emit(nc, final_deps)
        stack.close()
        return nc

    class Norm:
        def __init__(self, sqb, tnv, rstd, psb):
            self.sqb, self.tn, self.rstd, self.psb = sqb, tnv, rstd, psb
            self.war = {}

        def part1(self, xv, deps):
            self.xv = xv
            self.sq = I("act", "activation", out=self.sqb, in_=xv, func=AF.Square, deps=[deps, self.war.get("sqb")])

        def part2(self, outs, scales, biases):
            self.p2a()
            self.p2b()
            self.p2c()
            return self.p2d(outs, scales, biases)

        def p2a(self):
            mm = None
            for kc in range(KC):
                mm = I("pe", "matmul", self.psb, ones_b, self.sqb[:, kc * TT:(kc + 1) * TT], start=(kc == 0),
                       stop=(kc == KC - 1), deps=[self.sq, self.war.get("psb")] if kc == 0 else [])
            self.war["sqb"] = mm
            self.mm = mm

        def p2b(self):
            ln = I("act", "activation", out=self.rstd, in_=self.psb, func=AF.Ln, scale=1.0 / D, bias=EPS,
                   deps=[self.mm, self.war.get("rstd")])
            self.war["psb"] = ln
            self.ex = I("act", "activation", out=self.rstd, in_=self.rstd, func=AF.Exp, scale=-0.5)

        def p2c(self):
            mul = None
            for kc in range(KC):
                mul = I("pool", "tensor_tensor", out=self.tn[:, kc * TT:(kc + 1) * TT], in0=self.xv[:, kc * TT:(kc + 1) * TT],
                        in1=self.rstd, op=ALU.mult, deps=[self.ex, self.war.get("tn")] if kc == 0 else [])
            self.war["rstd"] = mul
            self.x_done = mul
            self.mul = mul

        def p2d(self, outs, scales, biases):
            io = None
            for kc in range(KC):
                kw = dict(scale=scales[kc])
                if biases is not None:
                    kw["bias"] = biases[kc]
                io = I("act", "activation", out=outs[kc], in_=self.tn[:, kc * TT:(kc + 1) * TT], func=AF.Identity,
                       deps=[self.mul, self.out_war] if kc == 0 else [], **kw)
            self.war["tn"] = io
            return io

    for l in range(L):
        nkv, nq = NKV[l], NQ[l]
        xsrc = xT if l == 0 else x1T
        m1 = A.mark()
        if l == 0:
            A.alloc(D)
            junk1 = A.alloc(D)
            wa1 = [A.alloc(2 * D) for _ in range(2)]
        w_in_sb = A.alloc(KC * 2048, BF16)
        w_in_v = w_in_sb.rearrange("p (k n) -> p k n", k=KC)
        xt = [A.alloc(KC * TT) for _ in range(2)]
        sqb = A.alloc(KC * TT, BF16)
        tn = A.alloc(KC * TT)
        rstd = A.alloc(TT)
        hs = [A.alloc(KC * TT, BF16) for _ in range(2)]
        zst = [A.alloc(16 * TT, BF16) for _ in range(2)]
        hvs = [h_.rearrange("p (k t) -> p k t", k=KC) for h_ in hs]
        win_ld = None
        for q in range(4):
            win_ld = DMA("pool", w_in_sb[:, q * 4096:(q + 1) * 4096], w_in[l, :, q * 4096:(q + 1) * 4096], dsem="win")
        nt = nkv // TT
        nrm = Norm(sqb, tn, rstd, ps[4])
        x_rd = [None, None]
        zw = [None, None]
        h_rds = [None, None]
        bank_rd = [None] * 8
        xlds = {}
        hready = {}

        def m1_load(t):
            s = t % 2
            xlds[t] = DMA("sp", xt[s].rearrange("p (k t) -> p k t", k=KC), xsrc[:, :, t * TT:(t + 1) * TT],
                          deps=[x_rd[s]], dsem="x%d" % s)

        def m1_norm2(t):
            s = t % 2
            nrm.out_war = h_rds[s]
            hready[t] = nrm.part2([hvs[s][:, kc, :] for kc in range(KC)], [coefv(l, 0, kc) for kc in range(KC)],
                                  [modv(l, 0, kc) for kc in range(KC)])
            x_rd[s] = nrm.x_done

        m1_load(0)
        nrm.part1(xt[0], xlds[0])
        m1_norm2(0)
        for t in range(nt):
            s = t % 2
            hv = hvs[s]
            if l == 0:
                issue_casts(0, 1)
                for q in range(2):
                    if ada["cnt"] < 24:
                        adaln_block(1, ada["cnt"] * 2, 2, wa1, junk1)
            if t + 1 < nt:
                m1_load(t + 1)
                nrm.part1(xt[(t + 1) % 2], xlds[t + 1])
            full = (t * TT) < nq + TT
            chunks = list(range(16)) if full else (list(CH_KB) + list(CH_VB) + [CH_KC, CH_VC])
            evs = []
            lastmm = None
            for ci, ch in enumerate(chunks):
                if ci == len(chunks) // 2 and t + 1 < nt:
                    m1_norm2(t + 1)
                bk = ci % 4
                for kc in range(KC):
                    lastmm = I("pe", "matmul", ps[bk], w_in_v[:, kc, ch * 128:(ch + 1) * 128], hv[:, kc, :],
                               start=(kc == 0), stop=(kc == KC - 1),
                               deps=[hready[t], win_ld, bank_rd[bk]] if kc == 0 else [])
                dst = zst[s][:, ch * TT:(ch + 1) * TT]
                scl = 0.125 if ch in Q_CHUNKS else 1.0
                ev = I("dve", "tensor_scalar", out=dst, in0=ps[bk], scalar1=scl, scalar2=None, op0=ALU.mult,
                       deps=[lastmm, zw[s]])
                bank_rd[bk] = ev
                evs.append(ev)
            h_rds[s] = lastmm
            if full:
                zw[s] = DMA("sp", zT[:, :, t * TT:(t + 1) * TT].rearrange("c p n -> p c n"),
                            zst[s].rearrange("p (c n) -> p c n", c=16), deps=evs[-1:], dsem="zw%d" % s)
            else:
                for (c0, c1) in ((5, 11), (14, 16)):
                    zw[s] = DMA("sp", zT[c0:c1, :, t * TT:(t + 1) * TT].rearrange("c p n -> p c n"),
                                zst[s][:, c0 * TT:c1 * TT].rearrange("p (c n) -> p c n", c=c1 - c0),
                                deps=evs[-1:], dsem="zw%d" % s)
        if l == 0:
            assert ada["cnt"] == 24
            adaln_finish(1)
        P.barrier()
        A.release(m1)
        if DEBUG and l == 0:
            final_deps.append(DMA("sp", dbg["zT"][:, :, :], zT[:, :, :], dsem="dbg"))
            P.barrier()
        if stop == "m1" and l == 0:
            if DEBUG:
                P.barrier()
                final_deps.append(DMA("sp", dbg["mixT"][:, :, :], mixT[:, :, :], dsem="dbg2"))
            P.emit(nc, final_deps)
            stack.close()
            return nc
        ma = A.mark()
        W = nq + 16
        H = 8 + (int(0.6 * nq) // 8) * 8
        nA = H + 16
        LB = H - 16
        nB = W - LB
        zb = A.alloc(W, BF16)
        bufA = [A.alloc(nA) for _ in range(4)]
        bufB = [A.alloc(nB) for _ in range(4)]
        pls = [A.alloc(nq, BF16) for _ in range(2)]
        msts = [A.alloc(nq, BF16) for _ in range(2)]
        tmp8s = [A.alloc(8) for _ in range(2)]
        lo, hi = slice(0, 64), slice(64, 128)
        zpad = I("dve", "memset", zb[:, 0:8], 0.0)
        if l == 0:
            issue_casts(0, 99)

        def pool_chain(eng, zsrc, bufs, n, pc, deps):
            zf, T0, T1, T2 = bufs

            def add(o, x, y, d=()):
                return I(eng, "tensor_tensor", out=o, in0=x, in1=y, op=ALU.add, deps=d)
            first = I(eng, "tensor_copy", out=zf, in_=zsrc, deps=deps)
            add(T0[:, 0:n - 1], zf[:, 0:n - 1], zf[:, 1:n])
            add(T1[:, 0:n - 3], T0[:, 0:n - 3], T0[:, 2:n - 1])
            if pc == 0:
                add(T2[lo, 8:n - 8], T0[lo, 7:n - 9], zf[lo, 9:n - 7])
                last = add(T2[hi, 8:n - 8], T1[hi, 6:n - 10], zf[hi, 10:n - 6])
            else:
                add(T0[:, 0:n - 7], T1[:, 0:n - 7], T1[:, 4:n - 3])
                add(T1[hi, 0:n - 15], T0[hi, 0:n - 15], T0[hi, 8:n - 7])
                add(T2[lo, 8:n - 8], T0[lo, 4:n - 12], zf[lo, 12:n - 4])
                last = add(T2[hi, 8:n - 8], T1[hi, 0:n - 16], zf[hi, 16:n])
            return first, last

        prev = dict(cpA=None, cpB=None, sttB=None, edge=None)
        for pc in range(2):
            pl, mst, tmp8 = pls[pc], msts[pc], tmp8s[pc]
            ld = DMA("sp", zb[:, 8:W], zT[pc, :, 0:nq + 8], deps=[prev["cpA"], prev["cpB"]], dsem="attld")
            cpB, lastB = pool_chain("pool", zb[:, LB:W], bufB, nB, pc, [ld, zpad, prev["sttB"]])
            cpA, lastA = pool_chain("dve", zb[:, 0:nA], bufA, nA, pc, [ld, zpad, prev["edge"]])
            zfA, _, _, T2A = bufA
            zfB, _, _, T2B = bufB
            edge = I("pool", "tensor_tensor", out=tmp8, in0=T2A[:, 8:16], in1=pconst_sb[:, 2 + pc * 8:2 + pc * 8 + 8],
                     op=ALU.mult, deps=[const_ld, lastA])
            inv = pconst_sb[:, pc:pc + 1]
            I("dve", "scalar_tensor_tensor", out=pl[:, 0:H - 8], in0=T2A[:, 8:H], scalar=inv, in1=zfA[:, 8:H],
              op0=ALU.mult, op1=ALU.subtract, deps=[const_ld])
            sttB = I("dve", "scalar_tensor_tensor", out=pl[:, H - 8:nq], in0=T2B[:, H - LB:nq + 8 - LB], scalar=inv,
                     in1=zfB[:, H - LB:nq + 8 - LB], op0=ALU.mult, op1=ALU.subtract, deps=[lastB])
            plast = I("dve", "tensor_tensor", out=pl[:, 0:8], in0=tmp8, in1=zfA[:, 8:16], op=ALU.subtract, deps=[edge])
            prev = dict(cpA=cpA, cpB=cpB, sttB=sttB, edge=edge)
            evl = None
            for tq in range(nq // TT):
                bk = tq % 2
                mm = I("pe", "matmul", ps[bk], wpool_sb[:, (l * 2 + pc) * 128:(l * 2 + pc + 1) * 128],
                       pl[:, tq * TT:(tq + 1) * TT], start=True, stop=True, deps=[plast, wpool_ld, bank_rd[bk]])
                evl = I("act", "activation", out=mst[:, tq * TT:(tq + 1) * TT], in_=ps[bk], func=AF.Identity,
                        scale=pscale_sb[:, l * 2 + pc:l * 2 + pc + 1], deps=[mm])
                bank_rd[bk] = evl
            DMA("sp", mixT[pc, :, 0:nq], mst, deps=[evl], dsem="mixw")
        P.barrier()
        A.release(ma)
        if stop == "pool" and l == 0:
            if DEBUG:
                P.barrier()
                final_deps.append(DMA("sp", dbg["mixT"][:, :, :], mixT[:, :, :], dsem="dbg2"))
            P.emit(nc, final_deps)
            stack.close()
            return nc
        def attn_group(QT, KT, VT, bias_of, out_fn, Lq, Lk, d, R, E0b, Eb, build_v, pe_deps):
            nm = -(-Lk // 128)
            off = (128 - R) % 128
            ns = -(-(Lq + off) // 128)
            QTv = QT.rearrange("p (j s) -> p j s", s=d)
            KTv = KT.rearrange("p (j s) -> p j s", s=d)
            VTv = VT.rearrange("p (j s) -> p j s", s=d)
            vops = []
            if build_v:
                psts = [ps[k].bitcast(BF16).rearrange("p (t c) -> p t c", c=128) for k in range(2)]
                tiles = [(r, m) for r in range(d) for m in range(nm)]
                grp_rd = [None, None]
                for gi in range(0, len(tiles), 4):
                    grp = tiles[gi:gi + 4]
                    hb = (gi // 4) % 2
                    pst = psts[hb]
                    tl = None
                    for j, (r, m) in enumerate(grp):
                        ks = min(128, Lk - 128 * m)
                        tl = I("pe", "transpose", pst[0:ks, j, :], VTv[:, 128 * m:128 * m + ks, r], identb,
                               deps=[grp_rd[hb], pe_deps] if j == 0 else [])
                    n = len(grp)
                    if hb == 0:
                        o0 = I("act", "activation", out=vaug_v[:, gi:gi + n, 0:64], in_=pst[:, 0:n, 0:64],
                               func=AF.Copy, deps=[tl])
                        o1 = I("act", "activation", out=vaug_v[:, gi:gi + n, 192:256], in_=pst[:, 0:n, 64:128],
                               func=AF.Copy, deps=[tl])
                    else:
                        o0 = I("dve", "tensor_copy", out=vaug_v[:, gi:gi + n, 0:64], in_=pst[:, 0:n, 0:64], deps=[tl])
                        o1 = I("dve", "tensor_copy", out=vaug_v[:, gi:gi + n, 192:256], in_=pst[:, 0:n, 64:128],
                               deps=[tl])
                    grp_rd[hb] = [o0, o1]
                    vops = [v for v in vops if v.eng != o1.eng] + [o1]
            steps = []
            for r in range(d):
                for m in range(nm):
                    k0 = 128 * m
                    ks = min(128, Lk - k0)
                    qlo = max(0, k0 - R)
                    qhi = min(Lq, k0 + 128 + R)
                    if ks <= 0 or qhi <= qlo:
                        continue
                    pieces = []
                    for s_ in range((qlo + off) // 128, (qhi - 1 + off) // 128 + 1):
                        a = max(qlo, 128 * s_ - off)
                        b = min(qhi, 128 * s_ - off + 128)
                        pieces.append((s_, a, b))
                    steps.append((r, m, k0, ks, qlo, qhi, pieces))
            ncontrib = {}
            for (r, m, k0, ks, qlo, qhi, pieces) in steps:
                for (s_, a, b) in pieces:
                    ncontrib[(r, s_)] = ncontrib.get((r, s_), 0) + 1
            seen = {}
            total_sb = d * ns
            gen_started = [[-1, -1], [-1, -1]]
            cur_evacs = [[[], []], [[], []]]
            gen_done = {}
            Lops, Eops, Xops = {}, {}, {}
            pv_last = {}
            nsteps = len(steps)
            for i in range(nsteps + SK):
                if i < nsteps:
                    (r, m, k0, ks, qlo, qhi, pieces) = steps[i]
                    sl = i % 2
                    sl3 = i % (SK + 1)
                    nqs = qhi - qlo
                    bo = qlo - (k0 - R)
                    for hh in range(2):
                        rows = slice(64 * hh, 64 * hh + 64)
                        psS = ps[hh * 2 + sl]
                        bt = bias_of(hh)
                        eb = Eb[hh][sl3]
                        e0 = E0b[hh][sl3]
                        so = I("pe", "matmul", psS[0:ks, 0:nqs], KTv[rows, k0:k0 + ks, r], QTv[rows, qlo:qhi, r],
                               start=True, stop=True,
                               deps=[Xops.get((i - 2, hh)), (vops + [pe_deps]) if i < 2 else None])
                        xo = I("act", "activation", out=e0[0:ks, 0:nqs], in_=psS[0:ks, 0:nqs], func=AF.Exp,
                               deps=[so, Eops.get((i - SK - 1, hh))])
                        Xops[(i, hh)] = xo
                        eo = I("dve" if hh == 0 else "pool", "tensor_tensor", out=eb[0:ks, 0:nqs], in0=e0[0:ks, 0:nqs],
                               in1=bt[0:ks, bo:bo + nqs], op=ALU.mult, deps=[xo, pv_last.get((i - SK - 1, hh)), wtab_ready])
                        lo_ = eo
                        Lops[(i, hh)] = lo_
                        Eops[(i, hh)] = eo
                if i >= SK:
                    j = i - SK
                    (r, m, k0, ks, qlo, qhi, pieces) = steps[j]
                    ti = r * nm + m
                    for hh in range(2):
                        eb = Eb[hh][j % (SK + 1)]
                        pvl = None
                        groups = []
                        for (s_, a, b) in pieces:
                            sbi = r * ns + s_
                            G = sbi // 4
                            col = (sbi % 4) * 128 + (a - (128 * s_ - off))
                            if groups and groups[-1]["G"] == G:
                                groups[-1]["pcs"].append((s_, a, b, col))
                            else:
                                groups.append(dict(G=G, pcs=[(s_, a, b, col)]))
                        for grp in groups:
                            G = grp["G"]
                            bank = G % 2
                            acc = ps[4 + hh * 2 + bank]
                            a0 = grp["pcs"][0][1]
                            b1 = grp["pcs"][-1][2]
                            col0 = grp["pcs"][0][3]
                            assert grp["pcs"][-1][3] + (b1 - grp["pcs"][-1][1]) - col0 == b1 - a0
                            first = gen_started[hh][bank] != G
                            deps = [Eops[(j, hh)]]
                            if first:
                                gen_started[hh][bank] = G
                                deps.append(cur_evacs[hh][bank])
                                cur_evacs[hh][bank] = []
                            pvl = I("pe", "matmul", acc[:, col0:col0 + (b1 - a0)], vaug_v[0:ks, ti, hh * 128:(hh + 1) * 128],
                                    eb[0:ks, a0 - qlo:b1 - qlo], start=first, stop=False, skip_group_check=True, deps=deps)
                            for (s_, a, b, col) in grp["pcs"]:
                                seen[(hh, r, s_)] = seen.get((hh, r, s_), 0) + 1
                                if seen[(hh, r, s_)] == ncontrib[(r, s_)]:
                                    gd = gen_done.setdefault((hh, G), [])
                                    gd.append((r, s_, a, b, col))
                                    if len(gd) == min(4, total_sb - 4 * G):
                                        evs_ = out_fn(hh, G, gd, acc, pvl)
                                        for h2 in range(2):
                                            cur_evacs[h2][bank].extend(evs_)
                        pv_last[(j, hh)] = pvl

        mb = A.mark()
        QT = A.alloc(NLOC, BF16)
        KT = A.alloc(NLOC, BF16)
        VT = A.alloc(NLOC, BF16)
        Ynum = [A.alloc(nq) for _ in range(3)]
        Dsum = A.alloc(nq)
        lgb = [[A.alloc(384, BF16) for _ in range(SK + 1)] for _ in range(2)]
        Eb = [[A.alloc(384, BF16) for _ in range(SK + 1)] for _ in range(2)]
        for g in range(3):
            d = DIL[g]
            DMA("sp", QT[:, 0:nq], zT[CH_QB[g], :, 0:nq], dsem="attld")
            DMA("sp", KT[:, 0:nkv], zT[CH_KB[g], :, 0:nkv], dsem="attld")
            ldl = DMA("sp", VT[:, 0:nkv], zT[CH_VB[g], :, 0:nkv], dsem="attld")
            Yv = Ynum[g].rearrange("p (j s) -> p j s", s=d)
            Dv = Dsum.rearrange("p (j s) -> p j s", s=d)

            def out_B(hh, G, slots, acc, pvl, g=g, Yv=Yv, Dv=Dv):
                nrows = slice(0, 64) if hh == 0 else slice(64, 128)
                drows = slice(64, 128) if hh == 0 else slice(0, 64)
                ops = []
                for (r, s_, a, b, col) in slots:
                    n = b - a
                    ops.append(I("act", "activation", out=Yv[nrows, a:b, r], in_=acc[nrows, col:col + n], func=AF.Copy,
                                 deps=[pvl]))
                    if g == 0:
                        ops.append(I("dve", "tensor_copy", out=Dv[nrows, a:b, r], in_=acc[drows, col:col + n], deps=[pvl]))
                    else:
                        ops.append(I("dve", "tensor_tensor", out=Dv[nrows, a:b, r], in0=Dv[nrows, a:b, r],
                                     in1=acc[drows, col:col + n], op=ALU.add, deps=[pvl]))
                return ops

            attn_group(QT[:, 0:nq], KT[:, 0:nkv], VT[:, 0:nkv],
                       (lambda hh, g=g: biasB_sb[:, (2 * g + hh) * 256:(2 * g + hh + 1) * 256]),
                       out_B, nq // d, nkv // d, d, 64, lgb, Eb, True, ldl)
            P.barrier()
        I("act", "activation", out=Dsum, in_=Dsum, func=AF.Ln)
        rcp = I("act", "activation", out=Dsum, in_=Dsum, func=AF.Exp, scale=-1.0)
        mstb = [KT[:, 0:nq], VT[:, 0:nq], QT[:, 0:nq]]
        for g in range(3):
            eng = "dve" if g != 1 else "pool"
            mo = I(eng, "tensor_tensor", out=mstb[g], in0=Ynum[g], in1=Dsum, op=ALU.mult, deps=[rcp])
            DMA("sp", mixT[2 + g, :, 0:nq], mstb[g], deps=[mo], dsem="mixw")
        P.barrier()
        A.release(mb)

        if stop == "attb" and l == 0:
            if DEBUG:
                P.barrier()
                final_deps.append(DMA("sp", dbg["mixT"][:, :, :], mixT[:, :, :], dsem="dbg2"))
            P.emit(nc, final_deps)
            stack.close()
            return nc
        mc = A.mark()
        w_out_sb = A.alloc(KC * D, BF16)
        wout_ld = None
        for q in range(2):
            wout_ld = DMA("pool", w_out_sb[:, q * 4096:(q + 1) * 4096], w_out[l, :, q * 4096:(q + 1) * 4096], dsem="wout")
        nkc = nq + 128
        KT = A.alloc(nkc, BF16)
        VT = A.alloc(nkc, BF16)
        QTc = [A.alloc(nq, BF16) for _ in range(2)]
        mstc = [A.alloc(nq, BF16) for _ in range(2)]
        Dt = [A.alloc(512) for _ in range(2)]
        lgb = [[A.alloc(384, BF16) for _ in range(SK + 1)] for _ in range(2)]
        Eb = [[A.alloc(384, BF16) for _ in range(SK + 1)] for _ in range(2)]
        DMA("sp", KT, zT[CH_KC, :, 0:nkc], dsem="attld")
        ldk = DMA("sp", VT, zT[CH_VC, :, 0:nkc], dsem="attld")
        dctr = [0]
        for c in range(3):
            qs = c % 2
            ldq = DMA("sp", QTc[qs], zT[CH_QC[c], :, 0:nq], dsem="attq%d" % qs)
            state = {}

            def out_C(hh, G, slots, acc, pvl, c=c, qs=qs, state=state):
                state[(G, hh)] = (acc, pvl)
                if (G, 0) not in state or (G, 1) not in state:
                    return []
                acc0, pv0 = state[(G, 0)]
                acc1, pv1 = state[(G, 1)]
                Dd = Dt[dctr[0] % 2]
                dctr[0] += 1
                q0 = G * 512
                ops = []
                ops.append(I("dve", "tensor_copy", out=Dd[0:64, :], in_=acc0[64:128, :], deps=[pv0, pv1]))
                ops.append(I("dve", "tensor_copy", out=Dd[64:128, :], in_=acc1[0:64, :]))
                cp2 = ops[-1]
                ops.append(I("act", "activation", out=Dd, in_=Dd, func=AF.Ln, bias=esink[:, l * 3 + c:l * 3 + c + 1],
                             deps=[cp2]))
                rcp_ = I("act", "activation", out=Dd, in_=Dd, func=AF.Exp, scale=-1.0)
                ops.append(rcp_)
                ops.append(I("dve", "tensor_tensor", out=mstc[qs][0:64, q0:q0 + 512], in0=acc0[0:64, :],
                             in1=Dd[0:64, :], op=ALU.mult, deps=[rcp_]))
                ops.append(I("dve", "tensor_tensor", out=mstc[qs][64:128, q0:q0 + 512], in0=acc1[64:128, :],
                             in1=Dd[64:128, :], op=ALU.mult))
                state["last"] = ops[-1]
                return ops

            attn_group(QTc[qs], KT, VT,
                       (lambda hh, c=c: biasC_sb[:, (2 * c + hh) * 384:(2 * c + hh + 1) * 384]),
                       out_C, nq, nkc, 1, 128, lgb, Eb, c == 0, [ldq, ldk])
            DMA("sp", mixT[5 + c, :, 0:nq], mstc[qs], deps=[state["last"]], dsem="mixc%d" % qs)
            P.barrier()
        A.release(mc)
        if DEBUG and l == 0:
            final_deps.append(DMA("sp", dbg["mixT"][:, :, :], mixT[:, :, :], dsem="dbg"))
            P.barrier()

        if stop == "attc" and l == 0:
            if DEBUG:
                P.barrier()
                final_deps.append(DMA("sp", dbg["mixT"][:, :, :], mixT[:, :, :], dsem="dbg2"))
            P.emit(nc, final_deps)
            stack.close()
            return nc
        m2 = A.mark()
        w_out_sb = A.alloc(KC * D, BF16)
        w_out_v = w_out_sb.rearrange("p (k n) -> p k n", k=KC)
        xt = [A.alloc(KC * TT) for _ in range(2)]
        mx = [A.alloc(KC * TT, BF16) for _ in range(2)]
        sqb = A.alloc(KC * TT, BF16)
        tn = A.alloc(KC * TT)
        rstd = A.alloc(TT)
        h = A.alloc(KC * TT, BF16)
        hid = A.alloc(FC * TT, BF16)
        sg = [A.alloc(TT) for _ in range(2)]
        NWG, NWD = 5, 3
        wgs = [A.alloc(2048, BF16) for _ in range(NWG)]
        wds = [A.alloc(DFF, BF16) for _ in range(NWD)]
        hv = h.rearrange("p (k t) -> p k t", k=KC)
        nt2 = nq // TT
        nrm = Norm(sqb, tn, rstd, ps[6])
        st = dict(x_rd=[None, None], mx_rd=[None, None], h_rd=None, hid_rd=None, xw=[None, None],
                  wgc=0, wdc=0, lastres={}, ldx={}, ldm={}, hready={}, fin=None)
        wgs_rd = [None] * NWG
        wds_rd = [None] * NWD
        sg_rd = [None, None]
        bank_rd = [None] * 8

        def m2_loads(t):
            s = t % 2
            st["ldx"][t] = DMA("pool", xt[s].rearrange("p (k t) -> p k t", k=KC), xsrc[:, :, t * TT:(t + 1) * TT],
                               deps=[st["x_rd"][s], st["xw"][s]], dsem="x%d" % s)
            st["ldm"][t] = DMA("pool", mx[s].rearrange("p (k t) -> p k t", k=KC),
                               mixT[:, :, t * TT:(t + 1) * TT].rearrange("c p n -> p c n"),
                               deps=[st["mx_rd"][s]], dsem="mx%d" % s)

        def m2_outproj(t):
            s = t % 2
            xs = xt[s]
            mxv = mx[s].rearrange("p (k t) -> p k t", k=KC)
            res = None
            lastmm = None
            for dc in range(KC):
                bk = 4 + dc % 2
                for kc in range(KC):
                    lastmm = I("pe", "matmul", ps[bk], w_out_v[:, kc, dc * 128:(dc + 1) * 128], mxv[:, kc, :],
                               start=(kc == 0), stop=(kc == KC - 1),
                               deps=[st["ldm"][t], wout_ld, bank_rd[bk]] if kc == 0 else [])
                res = I("dve", "scalar_tensor_tensor", out=xs[:, dc * TT:(dc + 1) * TT], in0=ps[bk], scalar=modv(l, 2, dc),
                        in1=xs[:, dc * TT:(dc + 1) * TT], op0=ALU.mult, op1=ALU.add, deps=[lastmm, st["ldx"][t]])
                bank_rd[bk] = res
            st["mx_rd"][s] = lastmm
            nrm.part1(xs, res)

        def m2_norm2(t):
            nrm.out_war = st["h_rd"]
            st["hready"][t] = nrm.part2([hv[:, kc, :] for kc in range(KC)], [coefv(l, 1, kc) for kc in range(KC)],
                                        [modv(l, 3, kc) for kc in range(KC)])

        def m2_down(t, dcs):
            s = t % 2
            xs = xt[s]
            res = None
            lastmm = None
            for dc in dcs:
                ws = st["wdc"] % NWD
                st["wdc"] += 1
                wl = DMA("sp", wds[ws], wd_bf[l, dc], deps=[cast_done[l], wds_rd[ws]], dsem="wd%d" % ws)
                bk = 4 + dc % 2
                wv = wds[ws].rearrange("p (f n) -> p f n", f=FC)
                for fc in range(FC):
                    lastmm = I("pe", "matmul", ps[bk], wv[:, fc, :], hid[:, fc * TT:(fc + 1) * TT],
                               start=(fc == 0), stop=(fc == FC - 1),
                               deps=[st["ho"], wl, bank_rd[bk]] if fc == 0 else [])
                wds_rd[ws] = lastmm
                res = I("dve", "scalar_tensor_tensor", out=xs[:, dc * TT:(dc + 1) * TT], in0=ps[bk], scalar=modv(l, 5, dc),
                        in1=xs[:, dc * TT:(dc + 1) * TT], op0=ALU.mult, op1=ALU.add, deps=[lastmm])
                bank_rd[bk] = res
            st["hid_rd"] = lastmm
            st["lastres"][t] = res

        def m2_finish(t):
            s = t % 2
            xs = xt[s]
            res = st["lastres"][t]
            if l == 0:
                def fin0(stage, t=t, s=s, xs=xs, res=res):
                    if stage == 3:
                        st["xw"][s] = DMA("pool", x1T[:, :, t * TT:(t + 1) * TT], xs.rearrange("p (k t) -> p k t", k=KC),
                                          deps=[res], dsem="xw%d" % s)
                        st["x_rd"][s] = res
                st["fin"] = fin0
            else:
                def fin(stage, t=t, s=s, xs=xs, res=res):
                    if stage == -1:
                        nrmf.part1(xs, res)
                    elif stage == 0:
                        nrmf.p2a()
                    elif stage == 1:
                        nrmf.p2b()
                    elif stage == 2:
                        nrmf.p2c()
                    else:
                        nrmf.out_war = None
                        fo = nrmf.p2d([xs[:, kc * TT:(kc + 1) * TT] for kc in range(KC)],
                                      [ngs_sb[:, 2 * L * KC + kc:2 * L * KC + kc + 1] for kc in range(KC)], None)
                        st["xw"][s] = DMA("pool", yT[:, :, t * TT:(t + 1) * TT], xs.rearrange("p (k t) -> p k t", k=KC),
                                          deps=[fo], dsem="xw%d" % s)
                        st["x_rd"][s] = fo
                        final_deps.append(st["xw"][s])
                st["fin"] = fin

        nrmf = nrm
        m2_loads(0)
        m2_outproj(0)
        m2_norm2(0)
        for t in range(nt2):
            ho = None
            for fc in range(FC):
                if st["fin"] is not None and fc in (1, 3, 4, 5, 10):
                    st["fin"]({1: -1, 3: 0, 4: 1, 5: 2, 10: 3}[fc])
                    if fc == 10:
                        st["fin"] = None
                if fc == 8 and l == 0:
                    issue_casts(1, 4 if t < nt2 - 1 else 99)
                if fc == 11 and t + 1 < nt2:
                    m2_loads(t + 1)
                ws = st["wgc"] % NWG
                st["wgc"] += 1
                wl = DMA("sp", wgs[ws], wgu_bf[l, fc], deps=[cast_done[l], wgs_rd[ws]], dsem="wg%d" % ws)
                pb = (fc % 2) * 2
                wv = wgs[ws].rearrange("p (g k n) -> p g k n", g=2, k=KC)
                lastmm = None
                for gu in range(2):
                    for kc in range(KC):
                        lastmm = I("pe", "matmul", ps[pb + gu], wv[:, gu, kc, :], hv[:, kc, :],
                                   start=(kc == 0), stop=(kc == KC - 1),
                                   deps=[st["hready"][t], wl, bank_rd[pb + gu]] if kc == 0 else [])
                wgs_rd[ws] = lastmm
                sgi = fc % 2
                so = I("act", "activation", out=sg[sgi], in_=ps[pb], func=AF.Silu, deps=[lastmm, sg_rd[sgi]])
                ho = I("dve", "tensor_tensor", out=hid[:, fc * TT:(fc + 1) * TT], in0=sg[sgi], in1=ps[pb + 1],
                       op=ALU.mult, deps=[so, st["hid_rd"]])
                sg_rd[sgi] = ho
                bank_rd[pb] = ho
                bank_rd[pb + 1] = ho
            st["h_rd"] = lastmm
            st["ho"] = ho
            if t + 1 < nt2:
                m2_outproj(t + 1)
            m2_down(t, range(0, 4))
            if t + 1 < nt2:
                m2_norm2(t + 1)
            m2_down(t, range(4, 8))
            m2_finish(t)
        if st["fin"] is not None:
            for stage in range(-1, 4):
                st["fin"](stage)
            st["fin"] = None
        P.barrier()
        A.release(m2)
        if DEBUG and l == 0:
            final_deps.append(DMA("sp", dbg["x1T"][:, :, :], x1T[:, :, :], dsem="dbg"))
            P.barrier()
        if stop == "m2" and l == 0:
            if DEBUG:
                P.barrier()
                final_deps.append(DMA("sp", dbg["mixT"][:, :, :], mixT[:, :, :], dsem="dbg2"))
            P.emit(nc, final_deps)
            stack.close()
            return nc

    P.emit(nc, final_deps)
    stack.close()
    return nc


def _perm_in():
    o_q_b = 256
    o_k_b = o_q_b + 384
    o_v_b = o_k_b + 384
    o_q_c = o_v_b + 384
    o_k_c = o_q_c + 384
    o_v_c = o_k_c + 128
    cols = list(range(0, 256))
    cols += list(range(o_q_b, o_q_b + 384))
    cols += list(range(o_k_b, o_k_b + 384))
    cols += list(range(o_v_b, o_v_b + 384))
    for c in range(3):
        cols += list(range(o_q_c + 64 * c, o_q_c + 64 * c + 64))
        cols += list(range(o_q_c + 64 * (c + 3), o_q_c + 64 * (c + 3) + 64))
    cols += list(range(o_k_c, o_k_c + 128))
    cols += list(range(o_v_c, o_v_c + 128))
    return np.array(cols)


def _perm_mix():
    rows = list(range(0, 640))
    for c in range(3):
        rows += list(range(640 + 64 * c, 640 + 64 * c + 64))
        rows += list(range(640 + 64 * (c + 3), 640 + 64 * (c + 3) + 64))
    return np.array(rows)


def _const_tables():
    i = np.arange(1, 13, dtype=np.float64)
    slopes = np.exp2(-8.0 * i / 12.0)
    s_win = slopes[:6]
    s_dil = slopes[6:]
    p = np.arange(128)[:, None]
    biasC = np.zeros((128, 6, 384), np.float32)
    j = np.arange(384)[None, :]
    dist = np.abs(j - 128 - p)
    for c in range(3):
        for hh in range(2):
            head = c + 3 * hh
            biasC[:, 2 * c + hh, :] = np.where(dist <= 128, -s_win[head] * dist, NEGM)
    biasB = np.zeros((128, 6, 256), np.float32)
    j = np.arange(256)[None, :]
    dist = np.abs(j - 64 - p)
    for g in range(3):
        for hh in range(2):
            biasB[:, 2 * g + hh, :] = np.where(dist <= 64, -s_dil[2 * g + hh] * DIL[g] * dist, NEGM)
    pconst = np.zeros((128, 18), np.float32)
    rr = ((1, 2), (4, 8))
    for pc in range(2):
        for half in range(2):
            r = rr[pc][half]
            rows = slice(64 * half, 64 * half + 64)
            pconst[rows, pc] = 1.0 / (2 * r + 1)
            tt = np.arange(8)
            pconst[rows, 2 + pc * 8:2 + pc * 8 + 8] = 1.0 / (np.minimum(tt, r) + r + 1)

    def wtab(b):
        import ml_dtypes
        w = np.where(b <= NEGM / 2, 0.0, np.exp(b.astype(np.float64)))
        return w.astype(ml_dtypes.bfloat16).astype(np.float32)

    return wtab(biasC).reshape(128, -1), wtab(biasB).reshape(128, -1), pconst


def _prep(inputs):
    x = np.asarray(inputs["x"], np.float32)
    c = np.asarray(inputs["c"], np.float32)
    g = lambda k: np.asarray(inputs[k], np.float32)
    w_ada, b_ada, w_in, w_pool = g("w_ada"), g("b_ada"), g("w_in"), g("w_pool")
    pool_scale, sink_logit, w_out = g("pool_scale"), g("sink_logit"), g("w_out")
    w_gate, w_up, w_down = g("w_gate"), g("w_up"), g("w_down")
    n1, n2, fgv = g("norm1_g"), g("norm2_g"), g("final_g")
    pin, pmix = _perm_in(), _perm_mix()
    sh = {}
    sh["w_adaT"] = np.ascontiguousarray(
        np.stack([w_ada[l].T.reshape(48, 128, D).transpose(1, 0, 2) for l in range(L)]))
    sh["b_adaT"] = np.ascontiguousarray(
        np.concatenate([b_ada[l].reshape(48, 128).T for l in range(L)], axis=1))
    vec = lambda v: v.reshape(KC, 128).T
    sh["ngs"] = np.ascontiguousarray(np.concatenate(
        [vec(n1[l]) for l in range(L)] + [vec(n2[l]) for l in range(L)] + [vec(fgv)], axis=1))
    sh["w_in"] = np.ascontiguousarray(np.stack(
        [w_in[l][:, pin].reshape(KC, 128, 2048).transpose(1, 0, 2).reshape(128, KC * 2048) for l in range(L)]))
    sh["w_out"] = np.ascontiguousarray(np.stack(
        [w_out[l][pmix, :].reshape(KC, 128, D).transpose(1, 0, 2).reshape(128, KC * D) for l in range(L)]))
    wgu = np.empty((L, FC, 128, 2, KC, 128), np.float32)
    for l in range(L):
        wgu[l, :, :, 0] = w_gate[l].reshape(KC, 128, FC, 128).transpose(2, 1, 0, 3)
        wgu[l, :, :, 1] = w_up[l].reshape(KC, 128, FC, 128).transpose(2, 1, 0, 3)
    sh["wgu"] = wgu.reshape(L, FC, 128, 2048)
    sh["wd"] = np.ascontiguousarray(np.stack(
        [w_down[l].reshape(FC, 128, KC, 128).transpose(2, 1, 0, 3).reshape(KC, 128, DFF) for l in range(L)]))
    wp = np.zeros((128, L, 2, 128), np.float32)
    for l in range(L):
        for pc in range(2):
            wp[0:64, l, pc, 0:64] = w_pool[l, 2 * pc]
            wp[64:128, l, pc, 64:128] = w_pool[l, 2 * pc + 1]
    sh["wpool"] = wp.reshape(128, -1)
    sh["pscale"] = np.ascontiguousarray(np.concatenate([pool_scale[l].reshape(2, 128).T for l in range(L)], axis=1))
    sk = np.zeros((128, L, 3), np.float32)
    for l in range(L):
        for cc in range(3):
            sk[0:64, l, cc] = sink_logit[l, cc]
            sk[64:128, l, cc] = sink_logit[l, cc + 3]
    sh["sink"] = sk.reshape(128, -1)
    bC, bB, pc_ = _const_tables()
    sh["biasC"], sh["biasB"], sh["pconst"] = bC, bB, pc_
    sh["ident"] = np.eye(128, dtype=np.float32)
    in_maps = []
    for i in range(8):
        b, half = i // 2, i % 2
        idx = np.arange(NLOC) if half == 0 else (8191 - np.arange(NLOC))
        xl = x[b][idx]
        m = dict(sh)
        m["xT"] = np.ascontiguousarray(xl.reshape(NLOC, KC, 128).transpose(2, 1, 0))
        m["c_rep"] = np.ascontiguousarray(np.broadcast_to(c[b], (128, D)))
        in_maps.append(m)
    return in_maps


_NC_CACHE = {}


def kernel(**inputs):
    in_maps = _prep(inputs)
    if "nc" not in _NC_CACHE:
        _NC_CACHE["nc"] = build_program()
    nc = _NC_CACHE["nc"]
    res = run_bass_kernel_spmd(nc, in_maps, core_ids=list(range(8)))
    out = np.empty((4, 8192, D), np.float32)
    for i in range(8):
        b, half = i // 2, i % 2
        yT = np.asarray(res.results[i]["yT"])
        y = yT.transpose(2, 1, 0).reshape(NOUT, D)
        if half == 0:
            out[b, 0:NOUT] = y
        else:
            out[b, 8191 - np.arange(NOUT)] = y
    if DEBUG:
        kernel.debug = res.results
    return out
```

```python
import numpy as np
import concourse.bass as bass
import concourse.mybir as mybir
from concourse.bass_utils import run_bass_kernel_spmd

F32 = mybir.dt.float32
BF16 = mybir.dt.bfloat16
ALU = mybir.AluOpType
AF = mybir.ActivationFunctionType
AX = mybir.AxisListType

D = 1024
KC = 8
NLOC = 6144
L = 2
NKV = (6144, 5120)
NQ = (5120, 4096)
NOUT = 4096
TT = 512
DFF = 2816
FC = 22
EPS = 1e-6
NEGM = -30000.0
DEBUG = False

CH_POOL = (0, 1)
CH_QB = (2, 3, 4)
CH_KB = (5, 6, 7)
CH_VB = (8, 9, 10)
CH_QC = (11, 12, 13)
CH_KC = 14
CH_VC = 15
Q_CHUNKS = set(CH_QB) | set(CH_QC)
DIL = (1, 4, 16)
SK = 4


class Op:
    __slots__ = ("eng", "fn", "deps", "sig", "cnt", "dsem", "dval")


class Prog:
    ENGS = ("pe", "act", "dve", "pool", "sp")

    def __init__(self):
        self.ops = {e: [] for e in self.ENGS}
        self.dtot = {}
        self.last_dma = {}
        self.pending = {e: [] for e in self.ENGS}

    def add(self, eng, meth, args, kw, deps=(), dsem=None):
        op = Op()
        op.eng = eng
        op.fn = (meth, args, kw)
        op.sig = False
        op.cnt = 0
        op.dsem = dsem
        op.dval = 0
        dl = []

        def flat(x):
            if x is None:
                return
            if isinstance(x, (list, tuple)):
                for y in x:
                    flat(y)
            else:
                dl.append(x)

        flat(deps)
        if self.pending[eng]:
            dl.extend(self.pending[eng])
            self.pending[eng] = []
        op.deps = dl
        for d in dl:
            if d.dsem is None and d.eng != eng:
                d.sig = True
        if dsem is not None:
            self.dtot[dsem] = self.dtot.get(dsem, 0) + 16
            op.dval = self.dtot[dsem]
            self.last_dma[dsem] = op
        self.ops[eng].append(op)
        return op

    def barrier(self):
        deps = []
        for e in self.ENGS:
            for op in reversed(self.ops[e]):
                if op.dsem is None:
                    deps.append(op)
                    break
        deps.extend(self.last_dma.values())
        for e in self.ENGS:
            self.pending[e] = list(self.pending[e]) + deps

    def emit(self, nc, final_deps):
        for e in self.ENGS:
            c = 0
            for op in self.ops[e]:
                if op.dsem is None and op.sig:
                    c += 1
                    op.cnt = c
        import contextlib
        with contextlib.ExitStack() as st:
            esem = {e: st.enter_context(nc.semaphore("s_" + e)) for e in self.ENGS}
            dsem = {k: st.enter_context(nc.semaphore("d_" + k)) for k in self.dtot}
            block = st.enter_context(nc.Block())

            def run(engobj, name):
                waited = {}

                def wait_for(d):
                    if d.dsem is not None:
                        key, sem, val = "d_" + d.dsem, dsem[d.dsem], d.dval
                    elif d.eng == name:
                        return
                    else:
                        key, sem, val = "e_" + d.eng, esem[d.eng], d.cnt
                        assert val > 0
                    if waited.get(key, 0) < val:
                        engobj.wait_ge(sem, val)
                        waited[key] = val

                for op in self.ops[name]:
                    for d in op.deps:
                        wait_for(d)
                    ins = getattr(engobj, op.fn[0])(*op.fn[1], **op.fn[2])
                    if op.dsem is not None:
                        ins.then_inc(dsem[op.dsem], 16)
                    elif op.sig:
                        ins.then_inc(esem[name], 1)
                if name == "sp":
                    for d in final_deps:
                        wait_for(d)

            @block.tensor
            def _(e):
                run(e, "pe")

            @block.scalar
            def _(e):
                run(e, "act")

            @block.vector
            def _(e):
                run(e, "dve")

            @block.gpsimd
            def _(e):
                run(e, "pool")

            @block.sync
            def _(e):
                run(e, "sp")


class Arena:
    def __init__(self, ap, nwords):
        self.ap = ap
        self.n = nwords
        self.top = 0

    def alloc(self, nelem, dt=F32):
        words = nelem if dt == F32 else (nelem + 1) // 2
        a = self.top
        self.top += words
        assert self.top <= self.n, ("SBUF arena overflow", self.top, self.n)
        v = self.ap[:, a:a + words]
        return v if dt == F32 else v.bitcast(BF16)

    def mark(self):
        return self.top

    def release(self, m):
        self.top = m


def build_program(stop=None, DEBUG=DEBUG):
    nc = bass.Bass("TRN2", target_bir_lowering=False)

    def din(name, shape, dt=F32):
        return nc.dram_tensor(name, shape, dt, kind="ExternalInput").ap()

    xT = din("xT", [128, KC, NLOC])
    c_rep = din("c_rep", [128, D])
    w_adaT = din("w_adaT", [L, 128, 48, D])
    b_adaT = din("b_adaT", [128, L * 48])
    ngs = din("ngs", [128, (2 * L + 1) * KC])
    w_in = din("w_in", [L, 128, KC * 2048])
    w_out = din("w_out", [L, 128, KC * D])
    wgu = din("wgu", [L, FC, 128, 2048])
    wd = din("wd", [L, KC, 128, DFF])
    wpool = din("wpool", [128, L * 2 * 128])
    pscale = din("pscale", [128, L * 2])
    sink = din("sink", [128, L * 3])
    biasC = din("biasC", [128, 6 * 384])
    biasB = din("biasB", [128, 6 * 256])
    pconst = din("pconst", [128, 2 + 16])
    ident = din("ident", [128, 128])
    yT = nc.dram_tensor("yT", [128, KC, NOUT], F32, kind="ExternalOutput").ap()
    zT = nc.dram_tensor("zT", [16, 128, NLOC], BF16).ap()
    mixT = nc.dram_tensor("mixT", [8, 128, NQ[0]], BF16).ap()
    x1T = nc.dram_tensor("x1T", [128, KC, NQ[0]], F32).ap()
    wgu_bf = nc.dram_tensor("wgu_bf", [L, FC, 128, 2048], BF16).ap()
    wd_bf = nc.dram_tensor("wd_bf", [L, KC, 128, DFF], BF16).ap()
    dbg = {}
    if DEBUG:
        dbg["zT"] = nc.dram_tensor("dbg_zT", [16, 128, NLOC], BF16, kind="ExternalOutput").ap()
        dbg["mixT"] = nc.dram_tensor("dbg_mixT", [8, 128, NQ[0]], BF16, kind="ExternalOutput").ap()
        dbg["x1T"] = nc.dram_tensor("dbg_x1T", [128, KC, NQ[0]], F32, kind="ExternalOutput").ap()
        dbg["mod"] = nc.dram_tensor("dbg_mod", [128, L * 48], F32, kind="ExternalOutput").ap()

    import contextlib
    stack = contextlib.ExitStack()
    NW = 51200
    arena_t = stack.enter_context(nc.sbuf_tensor("arena", [128, NW], F32))
    A = Arena(arena_t[:, :], NW)
    ps = [stack.enter_context(nc.psum_tensor("ps%d" % i, [128, 512], F32)) for i in range(8)]
    ps = [p[:, :] for p in ps]
    P = Prog()
    final_deps = []

    def I(eng, meth, *args, deps=(), dsem=None, **kw):
        return P.add(eng, meth, args, kw, deps, dsem)

    def DMA(q, out, in_, deps=(), dsem=None):
        return P.add(q, "dma_start", (), dict(out=out, in_=in_), deps, dsem)

    identb = A.alloc(128, BF16)
    ones_f = A.alloc(128)
    ones_b = A.alloc(128, BF16)
    biasC_sb = A.alloc(6 * 384, BF16)
    biasB_sb = A.alloc(6 * 256, BF16)
    pconst_sb = A.alloc(18)
    ngs_sb = A.alloc((2 * L + 1) * KC)
    pscale_sb = A.alloc(L * 2)
    esink = A.alloc(L * 3)
    mod = A.alloc(L * 48)
    coef = A.alloc(L * 2 * KC)
    ident_f = A.alloc(128)
    wpool_sb = A.alloc(L * 2 * 128, BF16)
    vaug = A.alloc(48 * 256, BF16)
    vaug_v = vaug.rearrange("p (t c) -> p t c", c=256)

    def modv(l, m, kc):
        i = l * 48 + m * 8 + kc
        return mod[:, i:i + 1]

    def coefv(l, which, kc):
        i = (l * 2 + which) * KC + kc
        return coef[:, i:i + 1]

    const_ld = None
    for (dst, src) in ((ident_f, ident[:, :]), (pconst_sb, pconst[:, :]), (ngs_sb, ngs[:, :]), (pscale_sb, pscale[:, :]),
                       (esink, sink[:, :])):
        const_ld = DMA("sp", dst, src, dsem="const")
    wpool_ld = DMA("pool", wpool_sb, wpool[:, :], dsem="wpool")
    cast_done = [None, None]
    import os
    cast_list = {l: [(wgu_bf[l, fc], wgu[l, fc]) for fc in range(FC)] + [(wd_bf[l, dc], wd[l, dc]) for dc in range(KC)]
                 for l in range(L)}

    def issue_casts(lc, n):
        for _ in range(n):
            if cast_list[lc]:
                o, i_ = cast_list[lc].pop(0)
                cast_done[lc] = DMA("pool", o, i_, dsem="cast%d" % lc)

    bada_sb = A.alloc(L * 48)
    m0 = A.mark()
    scb = A.alloc(D)
    junk = A.alloc(D)
    wa = [A.alloc(8 * D) for _ in range(2)]
    bC_f = A.alloc(6 * 384)
    bB_f = A.alloc(6 * 256)
    DMA("sp", bC_f, biasC[:, :], dsem="bld")
    bld = DMA("sp", bB_f, biasB[:, :], dsem="bld")
    I("dve", "tensor_copy", out=biasC_sb, in_=bC_f, deps=[bld])
    wtab_ready = I("dve", "tensor_copy", out=biasB_sb, in_=bB_f)
    DMA("sp", scb, c_rep[:, :], dsem="cld")
    bada_ld = DMA("sp", bada_sb, b_adaT[:, :], dsem="cld")
    silu_op = I("act", "activation", out=scb, in_=scb, func=AF.Silu, deps=[bada_ld])
    I("dve", "tensor_copy", out=identb, in_=ident_f, deps=[const_ld])
    I("dve", "memset", ones_f, 1.0)
    I("dve", "memset", ones_b, 1.0)
    I("dve", "memset", vaug, 1.0)
    I("dve", "memset", mod, 0.0)
    I("act", "activation", out=esink, in_=esink, func=AF.Exp, deps=[const_ld])
    ada = dict(rd=[None, None], cnt=0, last=None)

    def adaln_block(l, j0, nj, wab, junkb):
        s = ada["cnt"] % 2
        ada["cnt"] += 1
        ld = DMA("sp" if l == 0 else "pool", wab[s].rearrange("p (j k) -> p j k", j=nj), w_adaT[l, :, j0:j0 + nj, :],
                 deps=[ada["rd"][s]], dsem="wa%d" % s)
        for jj in range(nj):
            i = l * 48 + j0 + jj
            ada["last"] = I("dve", "scalar_tensor_tensor", out=junkb, in0=wab[s][:, jj * D:(jj + 1) * D], scalar=1.0,
                            in1=scb, op0=ALU.mult, op1=ALU.mult, accum_out=mod[:, i:i + 1], deps=[ld, silu_op])
        ada["rd"][s] = ada["last"]

    def adaln_finish(l):
        modadd = I("pool", "tensor_tensor", out=mod[:, l * 48:(l + 1) * 48], in0=mod[:, l * 48:(l + 1) * 48],
                   in1=bada_sb[:, l * 48:(l + 1) * 48], op=ALU.add, deps=[bada_ld, ada["last"]])
        lastc = None
        for which in range(2):
            b0 = l * 48 + (1 + 3 * which) * 8
            sc = mod[:, b0:b0 + 8]
            g0 = (which * L + l) * KC
            c0 = (l * 2 + which) * KC
            lastc = I("dve", "scalar_tensor_tensor", out=coef[:, c0:c0 + KC], in0=sc, scalar=1.0, in1=ngs_sb[:, g0:g0 + KC],
                      op0=ALU.add, op1=ALU.mult, deps=[const_ld, modadd])
        return [modadd, lastc]

    for jb in range(6):
        adaln_block(0, jb * 8, 8, wa, junk)
    adaln_finish(0)
    ada["rd"] = [None, None]
    ada["cnt"] = 0
    if DEBUG:
        final_deps.append(DMA("sp", dbg["mod"][:, :], mod, deps=[P.ops["dve"][-1]], dsem="dbg"))
    P.barrier()
    A.release(m0)
    if stop == "prologue":
        P.emit(nc, final_deps)
        stack.close()
        return nc

    class Norm:
        def __init__(self, sqb, tnv, rstd, psb):
            self.sqb, self.tn, self.rstd, self.psb = sqb, tnv, rstd, psb
            self.war = {}

        def part1(self, xv, deps):
            self.xv = xv
            self.sq = I("act", "activation", out=self.sqb, in_=xv, func=AF.Square, deps=[deps, self.war.get("sqb")])

        def part2(self, outs, scales, biases):
            self.p2a()
            self.p2b()
            self.p2c()
            return self.p2d(outs, scales, biases)

        def p2a(self):
            mm = None
            for kc in range(KC):
                mm = I("pe", "matmul", self.psb, ones_b, self.sqb[:, kc * TT:(kc + 1) * TT], start=(kc == 0),
                       stop=(kc == KC - 1), deps=[self.sq, self.war.get("psb")] if kc == 0 else [])
            self.war["sqb"] = mm
            self.mm = mm

        def p2b(self):
            ln = I("act", "activation", out=self.rstd, in_=self.psb, func=AF.Ln, scale=1.0 / D, bias=EPS,
                   deps=[self.mm, self.war.get("rstd")])
            self.war["psb"] = ln
            self.ex = I("act", "activation", out=self.rstd, in_=self.rstd, func=AF.Exp, scale=-0.5)

        def p2c(self):
            mul = None
            for kc in range(KC):
                mul = I("pool", "tensor_tensor", out=self.tn[:, kc * TT:(kc + 1) * TT], in0=self.xv[:, kc * TT:(kc + 1) * TT],
                        in1=self.rstd, op=ALU.mult, deps=[self.ex, self.war.get("tn")] if kc == 0 else [])
            self.war["rstd"] = mul
            self.x_done = mul
            self.mul = mul

        def p2d(self, outs, scales, biases):
            io = None
            for kc in range(KC):
                kw = dict(scale=scales[kc])
                if biases is not None:
                    kw["bias"] = biases[kc]
                io = I("act", "activation", out=outs[kc], in_=self.tn[:, kc * TT:(kc + 1) * TT], func=AF.Identity,
                       deps=[self.mul, self.out_war] if kc == 0 else [], **kw)
            self.war["tn"] = io
            return io

    for l in range(L):
        nkv, nq = NKV[l], NQ[l]
        xsrc = xT if l == 0 else x1T
        m1 = A.mark()
        if l == 0:
            A.alloc(D)
            junk1 = A.alloc(D)
            wa1 = [A.alloc(2 * D) for _ in range(2)]
        w_in_sb = A.alloc(KC * 2048, BF16)
        w_in_v = w_in_sb.rearrange("p (k n) -> p k n", k=KC)
        xt = [A.alloc(KC * TT) for _ in range(2)]
        sqb = A.alloc(KC * TT, BF16)
        tn = A.alloc(KC * TT)
        rstd = A.alloc(TT)
        hs = [A.alloc(KC * TT, BF16) for _ in range(2)]
        zst = [A.alloc(16 * TT, BF16) for _ in range(2)]
        hvs = [h_.rearrange("p (k t) -> p k t", k=KC) for h_ in hs]
        win_ld = None
        for q in range(4):
            win_ld = DMA("pool", w_in_sb[:, q * 4096:(q + 1) * 4096], w_in[l, :, q * 4096:(q + 1) * 4096], dsem="win")
        nt = nkv // TT
        nrm = Norm(sqb, tn, rstd, ps[4])
        x_rd = [None, None]
        zw = [None, None]
        h_rds = [None, None]
        bank_rd = [None] * 8
        xlds = {}
        hready = {}

        def m1_load(t):
            s = t % 2
            xlds[t] = DMA("sp", xt[s].rearrange("p (k t) -> p k t", k=KC), xsrc[:, :, t * TT:(t + 1) * TT],
                          deps=[x_rd[s]], dsem="x%d" % s)

        def m1_norm2(t):
            s = t % 2
            nrm.out_war = h_rds[s]
            hready[t] = nrm.part2([hvs[s][:, kc, :] for kc in range(KC)], [coefv(l, 0, kc) for kc in range(KC)],
                                  [modv(l, 0, kc) for kc in range(KC)])
            x_rd[s] = nrm.x_done

        m1_load(0)
        nrm.part1(xt[0], xlds[0])
        m1_norm2(0)
        for t in range(nt):
            s = t % 2
            hv = hvs[s]
            if l == 0:
                issue_casts(0, 1)
                for q in range(2):
                    if ada["cnt"] < 24:
                        adaln_block(1, ada["cnt"] * 2, 2, wa1, junk1)
            if t + 1 < nt:
                m1_load(t + 1)
                nrm.part1(xt[(t + 1) % 2], xlds[t + 1])
            full = (t * TT) < nq + TT
            chunks = list(range(16)) if full else (list(CH_KB) + list(CH_VB) + [CH_KC, CH_VC])
            evs = []
            lastmm = None
            for ci, ch in enumerate(chunks):
                if ci == len(chunks) // 2 and t + 1 < nt:
                    m1_norm2(t + 1)
                bk = ci % 4
                for kc in range(KC):
                    lastmm = I("pe", "matmul", ps[bk], w_in_v[:, kc, ch * 128:(ch + 1) * 128], hv[:, kc, :],
                               start=(kc == 0), stop=(kc == KC - 1),
                               deps=[hready[t], win_ld, bank_rd[bk]] if kc == 0 else [])
                dst = zst[s][:, ch * TT:(ch + 1) * TT]
                scl = 0.125 if ch in Q_CHUNKS else 1.0
                ev = I("dve", "tensor_scalar", out=dst, in0=ps[bk], scalar1=scl, scalar2=None, op0=ALU.mult,
                       deps=[lastmm, zw[s]])
                bank_rd[bk] = ev
                evs.append(ev)
            h_rds[s] = lastmm
            if full:
                zw[s] = DMA("sp", zT[:, :, t * TT:(t + 1) * TT].rearrange("c p n -> p c n"),
                            zst[s].rearrange("p (c n) -> p c n", c=16), deps=evs[-1:], dsem="zw%d" % s)
            else:
                for (c0, c1) in ((5, 11), (14, 16)):
                    zw[s] = DMA("sp", zT[c0:c1, :, t * TT:(t + 1) * TT].rearrange("c p n -> p c n"),
                                zst[s][:, c0 * TT:c1 * TT].rearrange("p (c n) -> p c n", c=c1 - c0),
                                deps=evs[-1:], dsem="zw%d" % s)
        if l == 0:
            assert ada["cnt"] == 24
            adaln_finish(1)
        P.barrier()
        A.release(m1)
        if DEBUG and l == 0:
            final_deps.append(DMA("sp", dbg["zT"][:, :, :], zT[:, :, :], dsem="dbg"))
            P.barrier()
        if stop == "m1" and l == 0:
            if DEBUG:
                P.barrier()
                final_deps.append(DMA("sp", dbg["mixT"][:, :, :], mixT[:, :, :], dsem="dbg2"))
            P.emit(nc, final_deps)
            stack.close()
            return nc
        ma = A.mark()
        W = nq + 16
        H = 8 + (int(0.6 * nq) // 8) * 8
        nA = H + 16
        LB = H - 16
        nB = W - LB
        zb = A.alloc(W, BF16)
        bufA = [A.alloc(nA) for _ in range(4)]
        bufB = [A.alloc(nB) for _ in range(4)]
        pls = [A.alloc(nq, BF16) for _ in range(2)]
        msts = [A.alloc(nq, BF16) for _ in range(2)]
        tmp8s = [A.alloc(8) for _ in range(2)]
        lo, hi = slice(0, 64), slice(64, 128)
        zpad = I("dve", "memset", zb[:, 0:8], 0.0)
        if l == 0:
            issue_casts(0, 99)

        def pool_chain(eng, zsrc, bufs, n, pc, deps):
            zf, T0, T1, T2 = bufs

            def add(o, x, y, d=()):
                return I(eng, "tensor_tensor", out=o, in0=x, in1=y, op=ALU.add, deps=d)
            first = I(eng, "tensor_copy", out=zf, in_=zsrc, deps=deps)
            add(T0[:, 0:n - 1], zf[:, 0:n - 1], zf[:, 1:n])
            add(T1[:, 0:n - 3], T0[:, 0:n - 3], T0[:, 2:n - 1])
            if pc == 0:
                add(T2[lo, 8:n - 8], T0[lo, 7:n - 9], zf[lo, 9:n - 7])
                last = add(T2[hi, 8:n - 8], T1[hi, 6:n - 10], zf[hi, 10:n - 6])
            else:
                add(T0[:, 0:n - 7], T1[:, 0:n - 7], T1[:, 4:n - 3])
                add(T1[hi, 0:n - 15], T0[hi, 0:n - 15], T0[hi, 8:n - 7])
                add(T2[lo, 8:n - 8], T0[lo, 4:n - 12], zf[lo, 12:n - 4])
                last = add(T2[hi, 8:n - 8], T1[hi, 0:n - 16], zf[hi, 16:n])
            return first, last

        prev = dict(cpA=None, cpB=None, sttB=None, edge=None)
        for pc in range(2):
            pl, mst, tmp8 = pls[pc], msts[pc], tmp8s[pc]
            ld = DMA("sp", zb[:, 8:W], zT[pc, :, 0:nq + 8], deps=[prev["cpA"], prev["cpB"]], dsem="attld")
            cpB, lastB = pool_chain("pool", zb[:, LB:W], bufB, nB, pc, [ld, zpad, prev["sttB"]])
            cpA, lastA = pool_chain("dve", zb[:, 0:nA], bufA, nA, pc, [ld, zpad, prev["edge"]])
            zfA, _, _, T2A = bufA
            zfB, _, _, T2B = bufB
            edge = I("pool", "tensor_tensor", out=tmp8, in0=T2A[:, 8:16], in1=pconst_sb[:, 2 + pc * 8:2 + pc * 8 + 8],
                     op=ALU.mult, deps=[const_ld, lastA])
            inv = pconst_sb[:, pc:pc + 1]
            I("dve", "scalar_tensor_tensor", out=pl[:, 0:H - 8], in0=T2A[:, 8:H], scalar=inv, in1=zfA[:, 8:H],
              op0=ALU.mult, op1=ALU.subtract, deps=[const_ld])
            sttB = I("dve", "scalar_tensor_tensor", out=pl[:, H - 8:nq], in0=T2B[:, H - LB:nq + 8 - LB], scalar=inv,
                     in1=zfB[:, H - LB:nq + 8 - LB], op0=ALU.mult, op1=ALU.subtract, deps=[lastB])
            plast = I("dve", "tensor_tensor", out=pl[:, 0:8], in0=tmp8, in1=zfA[:, 8:16], op=ALU.subtract, deps=[edge])
            prev = dict(cpA=cpA, cpB=cpB, sttB=sttB, edge=edge)
            evl = None
            for tq in range(nq // TT):
                bk = tq % 2
                mm = I("pe", "matmul", ps[bk], wpool_sb[:, (l * 2 + pc) * 128:(l * 2 + pc + 1) * 128],
                       pl[:, tq * TT:(tq + 1) * TT], start=True, stop=True, deps=[plast, wpool_ld, bank_rd[bk]])
                evl = I("act", "activation", out=mst[:, tq * TT:(tq + 1) * TT], in_=ps[bk], func=AF.Identity,
                        scale=pscale_sb[:, l * 2 + pc:l * 2 + pc + 1], deps=[mm])
                bank_rd[bk] = evl
            DMA("sp", mixT[pc, :, 0:nq], mst, deps=[evl], dsem="mixw")
        P.barrier()
        A.release(ma)
        if stop == "pool" and l == 0:
            if DEBUG:
                P.barrier()
                final_deps.append(DMA("sp", dbg["mixT"][:, :, :], mixT[:, :, :], dsem="dbg2"))
            P.emit(nc, final_deps)
            stack.close()
            return nc
        def attn_group(QT, KT, VT, bias_of, out_fn, Lq, Lk, d, R, E0b, Eb, build_v, pe_deps):
            nm = -(-Lk // 128)
            off = (128 - R) % 128
            ns = -(-(Lq + off) // 128)
            QTv = QT.rearrange("p (j s) -> p j s", s=d)
            KTv = KT.rearrange("p (j s) -> p j s", s=d)
            VTv = VT.rearrange("p (j s) -> p j s", s=d)
            vops = []
            if build_v:
                psts = [ps[k].bitcast(BF16).rearrange("p (t c) -> p t c", c=128) for k in range(2)]
                tiles = [(r, m) for r in range(d) for m in range(nm)]
                grp_rd = [None, None]
                for gi in range(0, len(tiles), 4):
                    grp = tiles[gi:gi + 4]
                    hb = (gi // 4) % 2
                    pst = psts[hb]
                    tl = None
                    for j, (r, m) in enumerate(grp):
                        ks = min(128, Lk - 128 * m)
                        tl = I("pe", "transpose", pst[0:ks, j, :], VTv[:, 128 * m:128 * m + ks, r], identb,
                               deps=[grp_rd[hb], pe_deps] if j == 0 else [])
                    n = len(grp)
                    if hb == 0:
                        o0 = I("act", "activation", out=vaug_v[:, gi:gi + n, 0:64], in_=pst[:, 0:n, 0:64],
                               func=AF.Copy, deps=[tl])
                        o1 = I("act", "activation", out=vaug_v[:, gi:gi + n, 192:256], in_=pst[:, 0:n, 64:128],
                               func=AF.Copy, deps=[tl])
                    else:
                        o0 = I("dve", "tensor_copy", out=vaug_v[:, gi:gi + n, 0:64], in_=pst[:, 0:n, 0:64], deps=[tl])
                        o1 = I("dve", "tensor_copy", out=vaug_v[:, gi:gi + n, 192:256], in_=pst[:, 0:n, 64:128],
                               deps=[tl])
                    grp_rd[hb] = [o0, o1]
                    vops = [v for v in vops if v.eng != o1.eng] + [o1]
            steps = []
            for r in range(d):
                for m in range(nm):
                    k0 = 128 * m
                    ks = min(128, Lk - k0)
                    qlo = max(0, k0 - R)
                    qhi = min(Lq, k0 + 128 + R)
                    if ks <= 0 or qhi <= qlo:
                        continue
                    pieces = []
                    for s_ in range((qlo + off) // 128, (qhi - 1 + off) // 128 + 1):
                        a = max(qlo, 128 * s_ - off)
                        b = min(qhi, 128 * s_ - off + 128)
                        pieces.append((s_, a, b))
                    steps.append((r, m, k0, ks, qlo, qhi, pieces))
            ncontrib = {}
            for (r, m, k0, ks, qlo, qhi, pieces) in steps:
                for (s_, a, b) in pieces:
                    ncontrib[(r, s_)] = ncontrib.get((r, s_), 0) + 1
            seen = {}
            total_sb = d * ns
            gen_started = [[-1, -1], [-1, -1]]
            cur_evacs = [[[], []], [[], []]]
            gen_done = {}
            Lops, Eops, Xops = {}, {}, {}
            pv_last = {}
            nsteps = len(steps)
            for i in range(nsteps + SK):
                if i < nsteps:
                    (r, m, k0, ks, qlo, qhi, pieces) = steps[i]
                    sl = i % 2
                    sl3 = i % (SK + 1)
                    nqs = qhi - qlo
                    bo = qlo - (k0 - R)
                    for hh in range(2):
                        rows = slice(64 * hh, 64 * hh + 64)
                        psS = ps[hh * 2 + sl]
                        bt = bias_of(hh)
                        eb = Eb[hh][sl3]
                        e0 = E0b[hh][sl3]
                        so = I("pe", "matmul", psS[0:ks, 0:nqs], KTv[rows, k0:k0 + ks, r], QTv[rows, qlo:qhi, r],
                               start=True, stop=True,
                               deps=[Xops.get((i - 2, hh)), (vops + [pe_deps]) if i < 2 else None])
                        xo = I("act", "activation", out=e0[0:ks, 0:nqs], in_=psS[0:ks, 0:nqs], func=AF.Exp,
                               deps=[so, Eops.get((i - SK - 1, hh))])
                        Xops[(i, hh)] = xo
                        eo = I("dve", "tensor_tensor", out=eb[0:ks, 0:nqs], in0=e0[0:ks, 0:nqs],
                               in1=bt[0:ks, bo:bo + nqs], op=ALU.mult, deps=[xo, pv_last.get((i - SK - 1, hh)), wtab_ready])
                        lo_ = eo
                        Lops[(i, hh)] = lo_
                        Eops[(i, hh)] = eo
                if i >= SK:
                    j = i - SK
                    (r, m, k0, ks, qlo, qhi, pieces) = steps[j]
                    ti = r * nm + m
                    for hh in range(2):
                        eb = Eb[hh][j % (SK + 1)]
                        pvl = None
                        groups = []
                        for (s_, a, b) in pieces:
                            sbi = r * ns + s_
                            G = sbi // 4
                            col = (sbi % 4) * 128 + (a - (128 * s_ - off))
                            if groups and groups[-1]["G"] == G:
                                groups[-1]["pcs"].append((s_, a, b, col))
                            else:
                                groups.append(dict(G=G, pcs=[(s_, a, b, col)]))
                        for grp in groups:
                            G = grp["G"]
                            bank = G % 2
                            acc = ps[4 + hh * 2 + bank]
                            a0 = grp["pcs"][0][1]
                            b1 = grp["pcs"][-1][2]
                            col0 = grp["pcs"][0][3]
                            assert grp["pcs"][-1][3] + (b1 - grp["pcs"][-1][1]) - col0 == b1 - a0
                            first = gen_started[hh][bank] != G
                            deps = [Eops[(j, hh)]]
                            if first:
                                gen_started[hh][bank] = G
                                deps.append(cur_evacs[hh][bank])
                                cur_evacs[hh][bank] = []
                            pvl = I("pe", "matmul", acc[:, col0:col0 + (b1 - a0)], vaug_v[0:ks, ti, hh * 128:(hh + 1) * 128],
                                    eb[0:ks, a0 - qlo:b1 - qlo], start=first, stop=False, skip_group_check=True, deps=deps)
                            for (s_, a, b, col) in grp["pcs"]:
                                seen[(hh, r, s_)] = seen.get((hh, r, s_), 0) + 1
                                if seen[(hh, r, s_)] == ncontrib[(r, s_)]:
                                    gd = gen_done.setdefault((hh, G), [])
                                    gd.append((r, s_, a, b, col))
                                    if len(gd) == min(4, total_sb - 4 * G):
                                        evs_ = out_fn(hh, G, gd, acc, pvl)
                                        for h2 in range(2):
                                            cur_evacs[h2][bank].extend(evs_)
                        pv_last[(j, hh)] = pvl

        mb = A.mark()
        QT = A.alloc(NLOC, BF16)
        KT = A.alloc(NLOC, BF16)
        VT = A.alloc(NLOC, BF16)
        Ynum = [A.alloc(nq) for _ in range(3)]
        Dsum = A.alloc(nq)
        lgb = [[A.alloc(384, BF16) for _ in range(SK + 1)] for _ in range(2)]
        Eb = [[A.alloc(384, BF16) for _ in range(SK + 1)] for _ in range(2)]
        for g in range(3):
            d = DIL[g]
            DMA("sp", QT[:, 0:nq], zT[CH_QB[g], :, 0:nq], dsem="attld")
            DMA("sp", KT[:, 0:nkv], zT[CH_KB[g], :, 0:nkv], dsem="attld")
            ldl = DMA("sp", VT[:, 0:nkv], zT[CH_VB[g], :, 0:nkv], dsem="attld")
            Yv = Ynum[g].rearrange("p (j s) -> p j s", s=d)
            Dv = Dsum.rearrange("p (j s) -> p j s", s=d)

            def out_B(hh, G, slots, acc, pvl, g=g, Yv=Yv, Dv=Dv):
                nrows = slice(0, 64) if hh == 0 else slice(64, 128)
                drows = slice(64, 128) if hh == 0 else slice(0, 64)
                ops = []
                for (r, s_, a, b, col) in slots:
                    n = b - a
                    ops.append(I("act", "activation", out=Yv[nrows, a:b, r], in_=acc[nrows, col:col + n], func=AF.Copy,
                                 deps=[pvl]))
                    if g == 0:
                        ops.append(I("dve", "tensor_copy", out=Dv[nrows, a:b, r], in_=acc[drows, col:col + n], deps=[pvl]))
                    else:
                        ops.append(I("dve", "tensor_tensor", out=Dv[nrows, a:b, r], in0=Dv[nrows, a:b, r],
                                     in1=acc[drows, col:col + n], op=ALU.add, deps=[pvl]))
                return ops

            attn_group(QT[:, 0:nq], KT[:, 0:nkv], VT[:, 0:nkv],
                       (lambda hh, g=g: biasB_sb[:, (2 * g + hh) * 256:(2 * g + hh + 1) * 256]),
                       out_B, nq // d, nkv // d, d, 64, lgb, Eb, True, ldl)
            P.barrier()
        I("act", "activation", out=Dsum, in_=Dsum, func=AF.Ln)
        rcp = I("act", "activation", out=Dsum, in_=Dsum, func=AF.Exp, scale=-1.0)
        mstb = [KT[:, 0:nq], VT[:, 0:nq], QT[:, 0:nq]]
        for g in range(3):
            eng = "dve" if g != 1 else "pool"
            mo = I(eng, "tensor_tensor", out=mstb[g], in0=Ynum[g], in1=Dsum, op=ALU.mult, deps=[rcp])
            DMA("sp", mixT[2 + g, :, 0:nq], mstb[g], deps=[mo], dsem="mixw")
        P.barrier()
        A.release(mb)

        if stop == "attb" and l == 0:
            if DEBUG:
                P.barrier()
                final_deps.append(DMA("sp", dbg["mixT"][:, :, :], mixT[:, :, :], dsem="dbg2"))
            P.emit(nc, final_deps)
            stack.close()
            return nc
        mc = A.mark()
        w_out_sb = A.alloc(KC * D, BF16)
        wout_ld = None
        for q in range(2):
            wout_ld = DMA("pool", w_out_sb[:, q * 4096:(q + 1) * 4096], w_out[l, :, q * 4096:(q + 1) * 4096], dsem="wout")
        nkc = nq + 128
        KT = A.alloc(nkc, BF16)
        VT = A.alloc(nkc, BF16)
        QTc = [A.alloc(nq, BF16) for _ in range(2)]
        mstc = [A.alloc(nq, BF16) for _ in range(2)]
        Dt = [A.alloc(512) for _ in range(2)]
        lgb = [[A.alloc(384, BF16) for _ in range(SK + 1)] for _ in range(2)]
        Eb = [[A.alloc(384, BF16) for _ in range(SK + 1)] for _ in range(2)]
        DMA("sp", KT, zT[CH_KC, :, 0:nkc], dsem="attld")
        ldk = DMA("sp", VT, zT[CH_VC, :, 0:nkc], dsem="attld")
        dctr = [0]
        for c in range(3):
            qs = c % 2
            ldq = DMA("sp", QTc[qs], zT[CH_QC[c], :, 0:nq], dsem="attq%d" % qs)
            state = {}

            def out_C(hh, G, slots, acc, pvl, c=c, qs=qs, state=state):
                state[(G, hh)] = (acc, pvl)
                if (G, 0) not in state or (G, 1) not in state:
                    return []
                acc0, pv0 = state[(G, 0)]
                acc1, pv1 = state[(G, 1)]
                Dd = Dt[dctr[0] % 2]
                dctr[0] += 1
                q0 = G * 512
                ops = []
                ops.append(I("dve", "tensor_copy", out=Dd[0:64, :], in_=acc0[64:128, :], deps=[pv0, pv1]))
                ops.append(I("dve", "tensor_copy", out=Dd[64:128, :], in_=acc1[0:64, :]))
                cp2 = ops[-1]
                ops.append(I("act", "activation", out=Dd, in_=Dd, func=AF.Ln, bias=esink[:, l * 3 + c:l * 3 + c + 1],
                             deps=[cp2]))
                rcp_ = I("act", "activation", out=Dd, in_=Dd, func=AF.Exp, scale=-1.0)
                ops.append(rcp_)
                ops.append(I("dve", "tensor_tensor", out=mstc[qs][0:64, q0:q0 + 512], in0=acc0[0:64, :],
                             in1=Dd[0:64, :], op=ALU.mult, deps=[rcp_]))
                ops.append(I("dve", "tensor_tensor", out=mstc[qs][64:128, q0:q0 + 512], in0=acc1[64:128, :],
                             in1=Dd[64:128, :], op=ALU.mult))
                state["last"] = ops[-1]
                return ops

            attn_group(QTc[qs], KT, VT,
                       (lambda hh, c=c: biasC_sb[:, (2 * c + hh) * 384:(2 * c + hh + 1) * 384]),
                       out_C, nq, nkc, 1, 128, lgb, Eb, c == 0, [ldq, ldk])
            DMA("sp", mixT[5 + c, :, 0:nq], mstc[qs], deps=[state["last"]], dsem="mixc%d" % qs)
            P.barrier()
        A.release(mc)
        if DEBUG and l == 0:
            final_deps.append(DMA("sp", dbg["mixT"][:, :, :], mixT[:, :, :], dsem="dbg"))
            P.barrier()

        if stop == "attc" and l == 0:
            if DEBUG:
                P.barrier()
                final_deps.append(DMA("sp", dbg["mixT"][:, :, :], mixT[:, :, :], dsem="dbg2"))
            P.emit(nc, final_deps)
            stack.close()
            return nc
        m2 = A.mark()
        w_out_sb = A.alloc(KC * D, BF16)
        w_out_v = w_out_sb.rearrange("p (k n) -> p k n", k=KC)
        xt = [A.alloc(KC * TT) for _ in range(2)]
        mx = [A.alloc(KC * TT, BF16) for _ in range(2)]
        sqb = A.alloc(KC * TT, BF16)
        tn = A.alloc(KC * TT)
        rstd = A.alloc(TT)
        h = A.alloc(KC * TT, BF16)
        hid = A.alloc(FC * TT, BF16)
        sg = [A.alloc(TT) for _ in range(2)]
        NWG, NWD = 5, 3
        wgs = [A.alloc(2048, BF16) for _ in range(NWG)]
        wds = [A.alloc(DFF, BF16) for _ in range(NWD)]
        hv = h.rearrange("p (k t) -> p k t", k=KC)
        nt2 = nq // TT
        nrm = Norm(sqb, tn, rstd, ps[6])
        st = dict(x_rd=[None, None], mx_rd=[None, None], h_rd=None, hid_rd=None, xw=[None, None],
                  wgc=0, wdc=0, lastres={}, ldx={}, ldm={}, hready={}, fin=None)
        wgs_rd = [None] * NWG
        wds_rd = [None] * NWD
        sg_rd = [None, None]
        bank_rd = [None] * 8

        def m2_loads(t):
            s = t % 2
            st["ldx"][t] = DMA("pool", xt[s].rearrange("p (k t) -> p k t", k=KC), xsrc[:, :, t * TT:(t + 1) * TT],
                               deps=[st["x_rd"][s], st["xw"][s]], dsem="x%d" % s)
            st["ldm"][t] = DMA("pool", mx[s].rearrange("p (k t) -> p k t", k=KC),
                               mixT[:, :, t * TT:(t + 1) * TT].rearrange("c p n -> p c n"),
                               deps=[st["mx_rd"][s]], dsem="mx%d" % s)

        def m2_outproj(t):
            s = t % 2
            xs = xt[s]
            mxv = mx[s].rearrange("p (k t) -> p k t", k=KC)
            res = None
            lastmm = None
            for dc in range(KC):
                bk = 4 + dc % 2
                for kc in range(KC):
                    lastmm = I("pe", "matmul", ps[bk], w_out_v[:, kc, dc * 128:(dc + 1) * 128], mxv[:, kc, :],
                               start=(kc == 0), stop=(kc == KC - 1),
                               deps=[st["ldm"][t], wout_ld, bank_rd[bk]] if kc == 0 else [])
                res = I("dve", "scalar_tensor_tensor", out=xs[:, dc * TT:(dc + 1) * TT], in0=ps[bk], scalar=modv(l, 2, dc),
                        in1=xs[:, dc * TT:(dc + 1) * TT], op0=ALU.mult, op1=ALU.add, deps=[lastmm, st["ldx"][t]])
                bank_rd[bk] = res
            st["mx_rd"][s] = lastmm
            nrm.part1(xs, res)

        def m2_norm2(t):
            nrm.out_war = st["h_rd"]
            st["hready"][t] = nrm.part2([hv[:, kc, :] for kc in range(KC)], [coefv(l, 1, kc) for kc in range(KC)],
                                        [modv(l, 3, kc) for kc in range(KC)])

        def m2_down(t, dcs):
            s = t % 2
            xs = xt[s]
            res = None
            lastmm = None
            for dc in dcs:
                ws = st["wdc"] % NWD
                st["wdc"] += 1
                wl = DMA("sp", wds[ws], wd_bf[l, dc], deps=[cast_done[l], wds_rd[ws]], dsem="wd%d" % ws)
                bk = 4 + dc % 2
                wv = wds[ws].rearrange("p (f n) -> p f n", f=FC)
                for fc in range(FC):
                    lastmm = I("pe", "matmul", ps[bk], wv[:, fc, :], hid[:, fc * TT:(fc + 1) * TT],
                               start=(fc == 0), stop=(fc == FC - 1),
                               deps=[st["ho"], wl, bank_rd[bk]] if fc == 0 else [])
                wds_rd[ws] = lastmm
                res = I("dve", "scalar_tensor_tensor", out=xs[:, dc * TT:(dc + 1) * TT], in0=ps[bk], scalar=modv(l, 5, dc),
                        in1=xs[:, dc * TT:(dc + 1) * TT], op0=ALU.mult, op1=ALU.add, deps=[lastmm])
                bank_rd[bk] = res
            st["hid_rd"] = lastmm
            st["lastres"][t] = res

        def m2_finish(t):
            s = t % 2
            xs = xt[s]
            res = st["lastres"][t]
            if l == 0:
                def fin0(stage, t=t, s=s, xs=xs, res=res):
                    if stage == 3:
                        st["xw"][s] = DMA("pool", x1T[:, :, t * TT:(t + 1) * TT], xs.rearrange("p (k t) -> p k t", k=KC),
                                          deps=[res], dsem="xw%d" % s)
                        st["x_rd"][s] = res
                st["fin"] = fin0
            else:
                def fin(stage, t=t, s=s, xs=xs, res=res):
                    if stage == -1:
                        nrmf.part1(xs, res)
                    elif stage == 0:
                        nrmf.p2a()
                    elif stage == 1:
                        nrmf.p2b()
                    elif stage == 2:
                        nrmf.p2c()
                    else:
                        nrmf.out_war = None
                        fo = nrmf.p2d([xs[:, kc * TT:(kc + 1) * TT] for kc in range(KC)],
                                      [ngs_sb[:, 2 * L * KC + kc:2 * L * KC + kc + 1] for kc in range(KC)], None)
                        st["xw"][s] = DMA("pool", yT[:, :, t * TT:(t + 1) * TT], xs.rearrange("p (k t) -> p k t", k=KC),
                                          deps=[fo], dsem="xw%d" % s)
                        st["x_rd"][s] = fo
                        final_deps.append(st["xw"][s])
                st["fin"] = fin

        nrmf = nrm
        m2_loads(0)
        m2_outproj(0)
        m2_norm2(0)
        for t in range(nt2):
            ho = None
            for fc in range(FC):
                if st["fin"] is not None and fc in (1, 3, 4, 5, 10):
                    st["fin"]({1: -1, 3: 0, 4: 1, 5: 2, 10: 3}[fc])
                    if fc == 10:
                        st["fin"] = None
                if fc == 8 and l == 0:
                    issue_casts(1, 4 if t < nt2 - 1 else 99)
                if fc == 11 and t + 1 < nt2:
                    m2_loads(t + 1)
                ws = st["wgc"] % NWG
                st["wgc"] += 1
                wl = DMA("sp", wgs[ws], wgu_bf[l, fc], deps=[cast_done[l], wgs_rd[ws]], dsem="wg%d" % ws)
                pb = (fc % 2) * 2
                wv = wgs[ws].rearrange("p (g k n) -> p g k n", g=2, k=KC)
                lastmm = None
                for gu in range(2):
                    for kc in range(KC):
                        lastmm = I("pe", "matmul", ps[pb + gu], wv[:, gu, kc, :], hv[:, kc, :],
                                   start=(kc == 0), stop=(kc == KC - 1),
                                   deps=[st["hready"][t], wl, bank_rd[pb + gu]] if kc == 0 else [])
                wgs_rd[ws] = lastmm
                sgi = fc % 2
                so = I("act", "activation", out=sg[sgi], in_=ps[pb], func=AF.Silu, deps=[lastmm, sg_rd[sgi]])
                ho = I("dve", "tensor_tensor", out=hid[:, fc * TT:(fc + 1) * TT], in0=sg[sgi], in1=ps[pb + 1],
                       op=ALU.mult, deps=[so, st["hid_rd"]])
                sg_rd[sgi] = ho
                bank_rd[pb] = ho
                bank_rd[pb + 1] = ho
            st["h_rd"] = lastmm
            st["ho"] = ho
            if t + 1 < nt2:
                m2_outproj(t + 1)
            m2_down(t, range(0, 4))
            if t + 1 < nt2:
                m2_norm2(t + 1)
            m2_down(t, range(4, 8))
            m2_finish(t)
        if st["fin"] is not None:
            for stage in range(-1, 4):
                st["fin"](stage)
            st["fin"] = None
        P.barrier()
        A.release(m2)
        if DEBUG and l == 0:
            final_deps.append(DMA("sp", dbg["x1T"][:, :, :], x1T[:, :, :], dsem="dbg"))
            P.barrier()
        if stop == "m2" and l == 0:
            if DEBUG:
                P.barrier()
                final_deps.append(DMA("sp", dbg["mixT"][:, :, :], mixT[:, :, :], dsem="dbg2"))
            P.emit(nc, final_deps)
            stack.close()
            return nc

    P.emit(nc, final_deps)
    stack.close()
    return nc


def _perm_in():
    o_q_b = 256
    o_k_b = o_q_b + 384
    o_v_b = o_k_b + 384
    o_q_c = o_v_b + 384
    o_k_c = o_q_c + 384
    o_v_c = o_k_c + 128
    cols = list(range(0, 256))
    cols += list(range(o_q_b, o_q_b + 384))
    cols += list(range(o_k_b, o_k_b + 384))
    cols += list(range(o_v_b, o_v_b + 384))
    for c in range(3):
        cols += list(range(o_q_c + 64 * c, o_q_c + 64 * c + 64))
        cols += list(range(o_q_c + 64 * (c + 3), o_q_c + 64 * (c + 3) + 64))
    cols += list(range(o_k_c, o_k_c + 128))
    cols += list(range(o_v_c, o_v_c + 128))
    return np.array(cols)


def _perm_mix():
    rows = list(range(0, 640))
    for c in range(3):
        rows += list(range(640 + 64 * c, 640 + 64 * c + 64))
        rows += list(range(640 + 64 * (c + 3), 640 + 64 * (c + 3) + 64))
    return np.array(rows)


def _const_tables():
    i = np.arange(1, 13, dtype=np.float64)
    slopes = np.exp2(-8.0 * i / 12.0)
    s_win = slopes[:6]
    s_dil = slopes[6:]
    p = np.arange(128)[:, None]
    biasC = np.zeros((128, 6, 384), np.float32)
    j = np.arange(384)[None, :]
    dist = np.abs(j - 128 - p)
    for c in range(3):
        for hh in range(2):
            head = c + 3 * hh
            biasC[:, 2 * c + hh, :] = np.where(dist <= 128, -s_win[head] * dist, NEGM)
    biasB = np.zeros((128, 6, 256), np.float32)
    j = np.arange(256)[None, :]
    dist = np.abs(j - 64 - p)
    for g in range(3):
        for hh in range(2):
            biasB[:, 2 * g + hh, :] = np.where(dist <= 64, -s_dil[2 * g + hh] * DIL[g] * dist, NEGM)
    pconst = np.zeros((128, 18), np.float32)
    rr = ((1, 2), (4, 8))
    for pc in range(2):
        for half in range(2):
            r = rr[pc][half]
            rows = slice(64 * half, 64 * half + 64)
            pconst[rows, pc] = 1.0 / (2 * r + 1)
            tt = np.arange(8)
            pconst[rows, 2 + pc * 8:2 + pc * 8 + 8] = 1.0 / (np.minimum(tt, r) + r + 1)

    def wtab(b):
        import ml_dtypes
        w = np.where(b <= NEGM / 2, 0.0, np.exp(b.astype(np.float64)))
        return w.astype(ml_dtypes.bfloat16).astype(np.float32)

    return wtab(biasC).reshape(128, -1), wtab(biasB).reshape(128, -1), pconst


def _prep(inputs):
    x = np.asarray(inputs["x"], np.float32)
    c = np.asarray(inputs["c"], np.float32)
    g = lambda k: np.asarray(inputs[k], np.float32)
    w_ada, b_ada, w_in, w_pool = g("w_ada"), g("b_ada"), g("w_in"), g("w_pool")
    pool_scale, sink_logit, w_out = g("pool_scale"), g("sink_logit"), g("w_out")
    w_gate, w_up, w_down = g("w_gate"), g("w_up"), g("w_down")
    n1, n2, fgv = g("norm1_g"), g("norm2_g"), g("final_g")
    pin, pmix = _perm_in(), _perm_mix()
    sh = {}
    sh["w_adaT"] = np.ascontiguousarray(
        np.stack([w_ada[l].T.reshape(48, 128, D).transpose(1, 0, 2) for l in range(L)]))
    sh["b_adaT"] = np.ascontiguousarray(
        np.concatenate([b_ada[l].reshape(48, 128).T for l in range(L)], axis=1))
    vec = lambda v: v.reshape(KC, 128).T
    sh["ngs"] = np.ascontiguousarray(np.concatenate(
        [vec(n1[l]) for l in range(L)] + [vec(n2[l]) for l in range(L)] + [vec(fgv)], axis=1))
    sh["w_in"] = np.ascontiguousarray(np.stack(
        [w_in[l][:, pin].reshape(KC, 128, 2048).transpose(1, 0, 2).reshape(128, KC * 2048) for l in range(L)]))
    sh["w_out"] = np.ascontiguousarray(np.stack(
        [w_out[l][pmix, :].reshape(KC, 128, D).transpose(1, 0, 2).reshape(128, KC * D) for l in range(L)]))
    wgu = np.empty((L, FC, 128, 2, KC, 128), np.float32)
    for l in range(L):
        wgu[l, :, :, 0] = w_gate[l].reshape(KC, 128, FC, 128).transpose(2, 1, 0, 3)
        wgu[l, :, :, 1] = w_up[l].reshape(KC, 128, FC, 128).transpose(2, 1, 0, 3)
    sh["wgu"] = wgu.reshape(L, FC, 128, 2048)
    sh["wd"] = np.ascontiguousarray(np.stack(
        [w_down[l].reshape(FC, 128, KC, 128).transpose(2, 1, 0, 3).reshape(KC, 128, DFF) for l in range(L)]))
    wp = np.zeros((128, L, 2, 128), np.float32)
    for l in range(L):
        for pc in range(2):
            wp[0:64, l, pc, 0:64] = w_pool[l, 2 * pc]
            wp[64:128, l, pc, 64:128] = w_pool[l, 2 * pc + 1]
    sh["wpool"] = wp.reshape(128, -1)
    sh["pscale"] = np.ascontiguousarray(np.concatenate([pool_scale[l].reshape(2, 128).T for l in range(L)], axis=1))
    sk = np.zeros((128, L, 3), np.float32)
    for l in range(L):
        for cc in range(3):
            sk[0:64, l, cc] = sink_logit[l, cc]
            sk[64:128, l, cc] = sink_logit[l, cc + 3]
    sh["sink"] = sk.reshape(128, -1)
    bC, bB, pc_ = _const_tables()
    sh["biasC"], sh["biasB"], sh["pconst"] = bC, bB, pc_
    sh["ident"] = np.eye(128, dtype=np.float32)
    in_maps = []
    for i in range(8):
        b, half = i // 2, i % 2
        idx = np.arange(NLOC) if half == 0 else (8191 - np.arange(NLOC))
        xl = x[b][idx]
        m = dict(sh)
        m["xT"] = np.ascontiguousarray(xl.reshape(NLOC, KC, 128).transpose(2, 1, 0))
        m["c_rep"] = np.ascontiguousarray(np.broadcast_to(c[b], (128, D)))
        in_maps.append(m)
    return in_maps


_NC_CACHE = {}


def kernel(**inputs):
    in_maps = _prep(inputs)
    if "nc" not in _NC_CACHE:
        _NC_CACHE["nc"] = build_program()
    nc = _NC_CACHE["nc"]
    res = run_bass_kernel_spmd(nc, in_maps, core_ids=list(range(8)))
    out = np.empty((4, 8192, D), np.float32)
    for i in range(8):
        b, half = i // 2, i % 2
        yT = np.asarray(res.results[i]["yT"])
        y = yT.transpose(2, 1, 0).reshape(NOUT, D)
        if half == 0:
            out[b, 0:NOUT] = y
        else:
            out[b, 8191 - np.arange(NOUT)] = y
    if DEBUG:
        kernel.debug = res.results
    return out
```

```python
import numpy as np
import concourse.bass as bass
import concourse.mybir as mybir
from concourse.bass_utils import run_bass_kernel_spmd

F32 = mybir.dt.float32
BF16 = mybir.dt.bfloat16
ALU = mybir.AluOpType
AF = mybir.ActivationFunctionType
AX = mybir.AxisListType

D = 1024
KC = 8
NLOC = 6144
L = 2
NKV = (6144, 5120)
NQ = (5120, 4096)
NOUT = 4096
TT = 512
DFF = 2816
FC = 22
EPS = 1e-6
NEGM = -30000.0
DEBUG = False

CH_POOL = (0, 1)
CH_QB = (2, 3, 4)
CH_KB = (5, 6, 7)
CH_VB = (8, 9, 10)
CH_QC = (11, 12, 13)
CH_KC = 14
CH_VC = 15
Q_CHUNKS = set(CH_QB) | set(CH_QC)
DIL = (1, 4, 16)
SK = 4


class Op:
    __slots__ = ("eng", "fn", "deps", "sig", "cnt", "dsem", "dval")


class Prog:
    ENGS = ("pe", "act", "dve", "pool", "sp")

    def __init__(self):
        self.ops = {e: [] for e in self.ENGS}
        self.dtot = {}
        self.last_dma = {}
        self.pending = {e: [] for e in self.ENGS}

    def add(self, eng, meth, args, kw, deps=(), dsem=None):
        op = Op()
        op.eng = eng
        op.fn = (meth, args, kw)
        op.sig = False
        op.cnt = 0
        op.dsem = dsem
        op.dval = 0
        dl = []

        def flat(x):
            if x is None:
                return
            if isinstance(x, (list, tuple)):
                for y in x:
                    flat(y)
            else:
                dl.append(x)

        flat(deps)
        if self.pending[eng]:
            dl.extend(self.pending[eng])
            self.pending[eng] = []
        op.deps = dl
        for d in dl:
            if d.dsem is None and d.eng != eng:
                d.sig = True
        if dsem is not None:
            self.dtot[dsem] = self.dtot.get(dsem, 0) + 16
            op.dval = self.dtot[dsem]
            self.last_dma[dsem] = op
        self.ops[eng].append(op)
        return op

    def barrier(self):
        deps = []
        for e in self.ENGS:
            for op in reversed(self.ops[e]):
                if op.dsem is None:
                    deps.append(op)
                    break
        deps.extend(self.last_dma.values())
        for e in self.ENGS:
            self.pending[e] = list(self.pending[e]) + deps

    def emit(self, nc, final_deps):
        for e in self.ENGS:
            c = 0
            for op in self.ops[e]:
                if op.dsem is None and op.sig:
                    c += 1
                    op.cnt = c
        import contextlib
        with contextlib.ExitStack() as st:
            esem = {e: st.enter_context(nc.semaphore("s_" + e)) for e in self.ENGS}
            dsem = {k: st.enter_context(nc.semaphore("d_" + k)) for k in self.dtot}
            block = st.enter_context(nc.Block())

            def run(engobj, name):
                waited = {}

                def wait_for(d):
                    if d.dsem is not None:
                        key, sem, val = "d_" + d.dsem, dsem[d.dsem], d.dval
                    elif d.eng == name:
                        return
                    else:
                        key, sem, val = "e_" + d.eng, esem[d.eng], d.cnt
                        assert val > 0
                    if waited.get(key, 0) < val:
                        engobj.wait_ge(sem, val)
                        waited[key] = val

                for op in self.ops[name]:
                    for d in op.deps:
                        wait_for(d)
                    ins = getattr(engobj, op.fn[0])(*op.fn[1], **op.fn[2])
                    if op.dsem is not None:
                        ins.then_inc(dsem[op.dsem], 16)
                    elif op.sig:
                        ins.then_inc(esem[name], 1)
                if name == "sp":
                    for d in final_deps:
                        wait_for(d)

            @block.tensor
            def _(e):
                run(e, "pe")

            @block.scalar
            def _(e):
                run(e, "act")

            @block.vector
            def _(e):
                run(e, "dve")

            @block.gpsimd
            def _(e):
                run(e, "pool")

            @block.sync
            def _(e):
                run(e, "sp")


class Arena:
    def __init__(self, ap, nwords):
        self.ap = ap
        self.n = nwords
        self.top = 0

    def alloc(self, nelem, dt=F32):
        words = nelem if dt == F32 else (nelem + 1) // 2
        a = self.top
        self.top += words
        assert self.top <= self.n, ("SBUF arena overflow", self.top, self.n)
        v = self.ap[:, a:a + words]
        return v if dt == F32 else v.bitcast(BF16)

    def mark(self):
        return self.top

    def release(self, m):
        self.top = m


def build_program(stop=None, DEBUG=DEBUG):
    nc = bass.Bass("TRN2", target_bir_lowering=False)

    def din(name, shape, dt=F32):
        return nc.dram_tensor(name, shape, dt, kind="ExternalInput").ap()

    xT = din("xT", [128, KC, NLOC])
    c_rep = din("c_rep", [128, D])
    w_adaT = din("w_adaT", [L, 128, 48, D])
    b_adaT = din("b_adaT", [128, L * 48])
    ngs = din("ngs", [128, (2 * L + 1) * KC])
    w_in = din("w_in", [L, 128, KC * 2048])
    w_out = din("w_out", [L, 128, KC * D])
    wgu = din("wgu", [L, FC, 128, 2048])
    wd = din("wd", [L, KC, 128, DFF])
    wpool = din("wpool", [128, L * 2 * 128])
    pscale = din("pscale", [128, L * 2])
    sink = din("sink", [128, L * 3])
    biasC = din("biasC", [128, 6 * 384])
    biasB = din("biasB", [128, 6 * 256])
    pconst = din("pconst", [128, 2 + 16])
    ident = din("ident", [128, 128])
    yT = nc.dram_tensor("yT", [128, KC, NOUT], F32, kind="ExternalOutput").ap()
    zT = nc.dram_tensor("zT", [16, 128, NLOC], BF16).ap()
    mixT = nc.dram_tensor("mixT", [8, 128, NQ[0]], BF16).ap()
    x1T = nc.dram_tensor("x1T", [128, KC, NQ[0]], F32).ap()
    wgu_bf = nc.dram_tensor("wgu_bf", [L, FC, 128, 2048], BF16).ap()
    wd_bf = nc.dram_tensor("wd_bf", [L, KC, 128, DFF], BF16).ap()
    dbg = {}
    if DEBUG:
        dbg["zT"] = nc.dram_tensor("dbg_zT", [16, 128, NLOC], BF16, kind="ExternalOutput").ap()
        dbg["mixT"] = nc.dram_tensor("dbg_mixT", [8, 128, NQ[0]], BF16, kind="ExternalOutput").ap()
        dbg["x1T"] = nc.dram_tensor("dbg_x1T", [128, KC, NQ[0]], F32, kind="ExternalOutput").ap()
        dbg["mod"] = nc.dram_tensor("dbg_mod", [128, L * 48], F32, kind="ExternalOutput").ap()

    import contextlib
    stack = contextlib.ExitStack()
    NW = 51200
    arena_t = stack.enter_context(nc.sbuf_tensor("arena", [128, NW], F32))
    A = Arena(arena_t[:, :], NW)
    ps = [stack.enter_context(nc.psum_tensor("ps%d" % i, [128, 512], F32)) for i in range(8)]
    ps = [p[:, :] for p in ps]
    P = Prog()
    final_deps = []

    def I(eng, meth, *args, deps=(), dsem=None, **kw):
        return P.add(eng, meth, args, kw, deps, dsem)

    def DMA(q, out, in_, deps=(), dsem=None):
        return P.add(q, "dma_start", (), dict(out=out, in_=in_), deps, dsem)

    identb = A.alloc(128, BF16)
    ones_f = A.alloc(128)
    ones_b = A.alloc(128, BF16)
    biasC_sb = A.alloc(6 * 384, BF16)
    biasB_sb = A.alloc(6 * 256, BF16)
    pconst_sb = A.alloc(18)
    ngs_sb = A.alloc((2 * L + 1) * KC)
    pscale_sb = A.alloc(L * 2)
    esink = A.alloc(L * 3)
    mod = A.alloc(L * 48)
    coef = A.alloc(L * 2 * KC)
    ident_f = A.alloc(128)
    wpool_sb = A.alloc(L * 2 * 128, BF16)
    vaug = A.alloc(48 * 256, BF16)
    vaug_v = vaug.rearrange("p (t c) -> p t c", c=256)

    def modv(l, m, kc):
        i = l * 48 + m * 8 + kc
        return mod[:, i:i + 1]

    def coefv(l, which, kc):
        i = (l * 2 + which) * KC + kc
        return coef[:, i:i + 1]

    const_ld = None
    for (dst, src) in ((ident_f, ident[:, :]), (pconst_sb, pconst[:, :]), (ngs_sb, ngs[:, :]), (pscale_sb, pscale[:, :]),
                       (esink, sink[:, :])):
        const_ld = DMA("sp", dst, src, dsem="const")
    wpool_ld = DMA("pool", wpool_sb, wpool[:, :], dsem="wpool")
    cast_done = [None, None]
    import os
    cast_list = {l: [(wgu_bf[l, fc], wgu[l, fc]) for fc in range(FC)] + [(wd_bf[l, dc], wd[l, dc]) for dc in range(KC)]
                 for l in range(L)}

    def issue_casts(lc, n):
        for _ in range(n):
            if cast_list[lc]:
                o, i_ = cast_list[lc].pop(0)
                cast_done[lc] = DMA("pool", o, i_, dsem="cast%d" % lc)

    bada_sb = A.alloc(L * 48)
    m0 = A.mark()
    scb = A.alloc(D)
    junk = A.alloc(D)
    wa = [A.alloc(8 * D) for _ in range(2)]
    bC_f = A.alloc(6 * 384)
    bB_f = A.alloc(6 * 256)
    DMA("sp", bC_f, biasC[:, :], dsem="bld")
    bld = DMA("sp", bB_f, biasB[:, :], dsem="bld")
    I("dve", "tensor_copy", out=biasC_sb, in_=bC_f, deps=[bld])
    wtab_ready = I("dve", "tensor_copy", out=biasB_sb, in_=bB_f)
    DMA("sp", scb, c_rep[:, :], dsem="cld")
    bada_ld = DMA("sp", bada_sb, b_adaT[:, :], dsem="cld")
    silu_op = I("act", "activation", out=scb, in_=scb, func=AF.Silu, deps=[bada_ld])
    I("dve", "tensor_copy", out=identb, in_=ident_f, deps=[const_ld])
    I("dve", "memset", ones_f, 1.0)
    I("dve", "memset", ones_b, 1.0)
    I("dve", "memset", vaug, 1.0)
    I("dve", "memset", mod, 0.0)
    I("act", "activation", out=esink, in_=esink, func=AF.Exp, deps=[const_ld])
    ada = dict(rd=[None, None], cnt=0, last=None)

    def adaln_block(l, j0, nj, wab, junkb):
        s = ada["cnt"] % 2
        ada["cnt"] += 1
        ld = DMA("sp" if l == 0 else "pool", wab[s].rearrange("p (j k) -> p j k", j=nj), w_adaT[l, :, j0:j0 + nj, :],
                 deps=[ada["rd"][s]], dsem="wa%d" % s)
        for jj in range(nj):
            i = l * 48 + j0 + jj
            ada["last"] = I("dve", "scalar_tensor_tensor", out=junkb, in0=wab[s][:, jj * D:(jj + 1) * D], scalar=1.0,
                            in1=scb, op0=ALU.mult, op1=ALU.mult, accum_out=mod[:, i:i + 1], deps=[ld, silu_op])
        ada["rd"][s] = ada["last"]

    def adaln_finish(l):
        modadd = I("pool", "tensor_tensor", out=mod[:, l * 48:(l + 1) * 48], in0=mod[:, l * 48:(l + 1) * 48],
                   in1=bada_sb[:, l * 48:(l + 1) * 48], op=ALU.add, deps=[bada_ld, ada["last"]])
        lastc = None
        for which in range(2):
            b0 = l * 48 + (1 + 3 * which) * 8
            sc = mod[:, b0:b0 + 8]
            g0 = (which * L + l) * KC
            c0 = (l * 2 + which) * KC
            lastc = I("dve", "scalar_tensor_tensor", out=coef[:, c0:c0 + KC], in0=sc, scalar=1.0, in1=ngs_sb[:, g0:g0 + KC],
                      op0=ALU.add, op1=ALU.mult, deps=[const_ld, modadd])
        return [modadd, lastc]

    for jb in range(6):
        adaln_block(0, jb * 8, 8, wa, junk)
    adaln_finish(0)
    ada["rd"] = [None, None]
    ada["cnt"] = 0
    if DEBUG:
        final_deps.append(DMA("sp", dbg["mod"][:, :], mod, deps=[P.ops["dve"][-1]], dsem="dbg"))
    P.barrier()
    A.release(m0)
    if stop == "prologue":
        P.emit(nc, final_deps)
        stack.close()
        return nc

    class Norm:
        def __init__(self, sqb, tnv, rstd, psb):
            self.sqb, self.tn, self.rstd, self.psb = sqb, tnv, rstd, psb
            self.war = {}

        def part1(self, xv, deps):
            self.xv = xv
            self.sq = I("act", "activation", out=self.sqb, in_=xv, func=AF.Square, deps=[deps, self.war.get("sqb")])

        def part2(self, outs, scales, biases):
            self.p2a()
            self.p2b()
            self.p2c()
            return self.p2d(outs, scales, biases)

        def p2a(self):
            mm = None
            for kc in range(KC):
                mm = I("pe", "matmul", self.psb, ones_b, self.sqb[:, kc * TT:(kc + 1) * TT], start=(kc == 0),
                       stop=(kc == KC - 1), deps=[self.sq, self.war.get("psb")] if kc == 0 else [])
            self.war["sqb"] = mm
            self.mm = mm

        def p2b(self):
            ln = I("act", "activation", out=self.rstd, in_=self.psb, func=AF.Ln, scale=1.0 / D, bias=EPS,
                   deps=[self.mm, self.war.get("rstd")])
            self.war["psb"] = ln
            self.ex = I("act", "activation", out=self.rstd, in_=self.rstd, func=AF.Exp, scale=-0.5)

        def p2c(self):
            mul = None
            for kc in range(KC):
                mul = I("pool", "tensor_tensor", out=self.tn[:, kc * TT:(kc + 1) * TT], in0=self.xv[:, kc * TT:(kc + 1) * TT],
                        in1=self.rstd, op=ALU.mult, deps=[self.ex, self.war.get("tn")] if kc == 0 else [])
            self.war["rstd"] = mul
            self.x_done = mul
            self.mul = mul

        def p2d(self, outs, scales, biases):
            io = None
            for kc in range(KC):
                kw = dict(scale=scales[kc])
                if biases is not None:
                    kw["bias"] = biases[kc]
                io = I("act", "activation", out=outs[kc], in_=self.tn[:, kc * TT:(kc + 1) * TT], func=AF.Identity,
                       deps=[self.mul, self.out_war] if kc == 0 else [], **kw)
            self.war["tn"] = io
            return io

    for l in range(L):
        nkv, nq = NKV[l], NQ[l]
        xsrc = xT if l == 0 else x1T
        m1 = A.mark()
        if l == 0:
            A.alloc(D)
            junk1 = A.alloc(D)
            wa1 = [A.alloc(2 * D) for _ in range(2)]
        w_in_sb = A.alloc(KC * 2048, BF16)
        w_in_v = w_in_sb.rearrange("p (k n) -> p k n", k=KC)
        xt = [A.alloc(KC * TT) for _ in range(2)]
        sqb = A.alloc(KC * TT, BF16)
        tn = A.alloc(KC * TT)
        rstd = A.alloc(TT)
        hs = [A.alloc(KC * TT, BF16) for _ in range(2)]
        zst = [A.alloc(16 * TT, BF16) for _ in range(2)]
        hvs = [h_.rearrange("p (k t) -> p k t", k=KC) for h_ in hs]
        win_ld = None
        for q in range(4):
            win_ld = DMA("pool", w_in_sb[:, q * 4096:(q + 1) * 4096], w_in[l, :, q * 4096:(q + 1) * 4096], dsem="win")
        nt = nkv // TT
        nrm = Norm(sqb, tn, rstd, ps[4])
        x_rd = [None, None]
        zw = [None, None]
        h_rds = [None, None]
        bank_rd = [None] * 8
        xlds = {}
        hready = {}

        def m1_load(t):
            s = t % 2
            xlds[t] = DMA("sp", xt[s].rearrange("p (k t) -> p k t", k=KC), xsrc[:, :, t * TT:(t + 1) * TT],
                          deps=[x_rd[s]], dsem="x%d" % s)

        def m1_norm2(t):
            s = t % 2
            nrm.out_war = h_rds[s]
            hready[t] = nrm.part2([hvs[s][:, kc, :] for kc in range(KC)], [coefv(l, 0, kc) for kc in range(KC)],
                                  [modv(l, 0, kc) for kc in range(KC)])
            x_rd[s] = nrm.x_done

        m1_load(0)
        nrm.part1(xt[0], xlds[0])
        m1_norm2(0)
        for t in range(nt):
            s = t % 2
            hv = hvs[s]
            if l == 0:
                for q in range(2):
                    if ada["cnt"] < 24:
                        adaln_block(1, ada["cnt"] * 2, 2, wa1, junk1)
            if t + 1 < nt:
                m1_load(t + 1)
                nrm.part1(xt[(t + 1) % 2], xlds[t + 1])
            full = (t * TT) < nq + TT
            chunks = list(range(16)) if full else (list(CH_KB) + list(CH_VB) + [CH_KC, CH_VC])
            evs = []
            lastmm = None
            for ci, ch in enumerate(chunks):
                if ci == len(chunks) // 2 and t + 1 < nt:
                    m1_norm2(t + 1)
                bk = ci % 4
                for kc in range(KC):
                    lastmm = I("pe", "matmul", ps[bk], w_in_v[:, kc, ch * 128:(ch + 1) * 128], hv[:, kc, :],
                               start=(kc == 0), stop=(kc == KC - 1),
                               deps=[hready[t], win_ld, bank_rd[bk]] if kc == 0 else [])
                dst = zst[s][:, ch * TT:(ch + 1) * TT]
                scl = 0.125 if ch in Q_CHUNKS else 1.0
                ev = I("dve", "tensor_scalar", out=dst, in0=ps[bk], scalar1=scl, scalar2=None, op0=ALU.mult,
                       deps=[lastmm, zw[s]])
                bank_rd[bk] = ev
                evs.append(ev)
            h_rds[s] = lastmm
            if full:
                zw[s] = DMA("sp", zT[:, :, t * TT:(t + 1) * TT].rearrange("c p n -> p c n"),
                            zst[s].rearrange("p (c n) -> p c n", c=16), deps=evs[-1:], dsem="zw%d" % s)
            else:
                for (c0, c1) in ((5, 11), (14, 16)):
                    zw[s] = DMA("sp", zT[c0:c1, :, t * TT:(t + 1) * TT].rearrange("c p n -> p c n"),
                                zst[s][:, c0 * TT:c1 * TT].rearrange("p (c n) -> p c n", c=c1 - c0),
                                deps=evs[-1:], dsem="zw%d" % s)
        if l == 0:
            assert ada["cnt"] == 24
            adaln_finish(1)
        P.barrier()
        A.release(m1)
        if DEBUG and l == 0:
            final_deps.append(DMA("sp", dbg["zT"][:, :, :], zT[:, :, :], dsem="dbg"))
            P.barrier()
        if stop == "m1" and l == 0:
            if DEBUG:
                P.barrier()
                final_deps.append(DMA("sp", dbg["mixT"][:, :, :], mixT[:, :, :], dsem="dbg2"))
            P.emit(nc, final_deps)
            stack.close()
            return nc
        ma = A.mark()
        W = nq + 16
        H = 8 + (int(0.6 * nq) // 8) * 8
        nA = H + 16
        LB = H - 16
        nB = W - LB
        zb = A.alloc(W, BF16)
        bufA = [A.alloc(nA) for _ in range(4)]
        bufB = [A.alloc(nB) for _ in range(4)]
        pls = [A.alloc(nq, BF16) for _ in range(2)]
        msts = [A.alloc(nq, BF16) for _ in range(2)]
        tmp8s = [A.alloc(8) for _ in range(2)]
        lo, hi = slice(0, 64), slice(64, 128)
        zpad = I("dve", "memset", zb[:, 0:8], 0.0)

        def pool_chain(eng, zsrc, bufs, n, pc, deps):
            zf, T0, T1, T2 = bufs

            def add(o, x, y, d=()):
                return I(eng, "tensor_tensor", out=o, in0=x, in1=y, op=ALU.add, deps=d)
            first = I(eng, "tensor_copy", out=zf, in_=zsrc, deps=deps)
            add(T0[:, 0:n - 1], zf[:, 0:n - 1], zf[:, 1:n])
            add(T1[:, 0:n - 3], T0[:, 0:n - 3], T0[:, 2:n - 1])
            if pc == 0:
                add(T2[lo, 8:n - 8], T0[lo, 7:n - 9], zf[lo, 9:n - 7])
                last = add(T2[hi, 8:n - 8], T1[hi, 6:n - 10], zf[hi, 10:n - 6])
            else:
                add(T0[:, 0:n - 7], T1[:, 0:n - 7], T1[:, 4:n - 3])
                add(T1[hi, 0:n - 15], T0[hi, 0:n - 15], T0[hi, 8:n - 7])
                add(T2[lo, 8:n - 8], T0[lo, 4:n - 12], zf[lo, 12:n - 4])
                last = add(T2[hi, 8:n - 8], T1[hi, 0:n - 16], zf[hi, 16:n])
            return first, last

        prev = dict(cpA=None, cpB=None, sttB=None, edge=None)
        for pc in range(2):
            pl, mst, tmp8 = pls[pc], msts[pc], tmp8s[pc]
            ld = DMA("sp", zb[:, 8:W], zT[pc, :, 0:nq + 8], deps=[prev["cpA"], prev["cpB"]], dsem="attld")
            cpB, lastB = pool_chain("pool", zb[:, LB:W], bufB, nB, pc, [ld, zpad, prev["sttB"]])
            cpA, lastA = pool_chain("dve", zb[:, 0:nA], bufA, nA, pc, [ld, zpad, prev["edge"]])
            zfA, _, _, T2A = bufA
            zfB, _, _, T2B = bufB
            edge = I("pool", "tensor_tensor", out=tmp8, in0=T2A[:, 8:16], in1=pconst_sb[:, 2 + pc * 8:2 + pc * 8 + 8],
                     op=ALU.mult, deps=[const_ld, lastA])
            inv = pconst_sb[:, pc:pc + 1]
            I("dve", "scalar_tensor_tensor", out=pl[:, 0:H - 8], in0=T2A[:, 8:H], scalar=inv, in1=zfA[:, 8:H],
              op0=ALU.mult, op1=ALU.subtract, deps=[const_ld])
            sttB = I("dve", "scalar_tensor_tensor", out=pl[:, H - 8:nq], in0=T2B[:, H - LB:nq + 8 - LB], scalar=inv,
                     in1=zfB[:, H - LB:nq + 8 - LB], op0=ALU.mult, op1=ALU.subtract, deps=[lastB])
            plast = I("dve", "tensor_tensor", out=pl[:, 0:8], in0=tmp8, in1=zfA[:, 8:16], op=ALU.subtract, deps=[edge])
            prev = dict(cpA=cpA, cpB=cpB, sttB=sttB, edge=edge)
            evl = None
            for tq in range(nq // TT):
                bk = tq % 2
                mm = I("pe", "matmul", ps[bk], wpool_sb[:, (l * 2 + pc) * 128:(l * 2 + pc + 1) * 128],
                       pl[:, tq * TT:(tq + 1) * TT], start=True, stop=True, deps=[plast, wpool_ld, bank_rd[bk]])
                evl = I("act", "activation", out=mst[:, tq * TT:(tq + 1) * TT], in_=ps[bk], func=AF.Identity,
                        scale=pscale_sb[:, l * 2 + pc:l * 2 + pc + 1], deps=[mm])
                bank_rd[bk] = evl
            DMA("sp", mixT[pc, :, 0:nq], mst, deps=[evl], dsem="mixw")
        P.barrier()
        A.release(ma)
        if stop == "pool" and l == 0:
            if DEBUG:
                P.barrier()
                final_deps.append(DMA("sp", dbg["mixT"][:, :, :], mixT[:, :, :], dsem="dbg2"))
            P.emit(nc, final_deps)
            stack.close()
            return nc
        def attn_group(QT, KT, VT, bias_of, out_fn, Lq, Lk, d, R, E0b, Eb, build_v, pe_deps):
            nm = -(-Lk // 128)
            off = (128 - R) % 128
            ns = -(-(Lq + off) // 128)
            QTv = QT.rearrange("p (j s) -> p j s", s=d)
            KTv = KT.rearrange("p (j s) -> p j s", s=d)
            VTv = VT.rearrange("p (j s) -> p j s", s=d)
            vops = []
            if build_v:
                psts = [ps[k].bitcast(BF16).rearrange("p (t c) -> p t c", c=128) for k in range(2)]
                tiles = [(r, m) for r in range(d) for m in range(nm)]
                grp_rd = [None, None]
                for gi in range(0, len(tiles), 4):
                    grp = tiles[gi:gi + 4]
                    hb = (gi // 4) % 2
                    pst = psts[hb]
                    tl = None
                    for j, (r, m) in enumerate(grp):
                        ks = min(128, Lk - 128 * m)
                        tl = I("pe", "transpose", pst[0:ks, j, :], VTv[:, 128 * m:128 * m + ks, r], identb,
                               deps=[grp_rd[hb], pe_deps] if j == 0 else [])
                    n = len(grp)
                    if hb == 0:
                        o0 = I("act", "activation", out=vaug_v[:, gi:gi + n, 0:64], in_=pst[:, 0:n, 0:64],
                               func=AF.Copy, deps=[tl])
                        o1 = I("act", "activation", out=vaug_v[:, gi:gi + n, 192:256], in_=pst[:, 0:n, 64:128],
                               func=AF.Copy, deps=[tl])
                    else:
                        o0 = I("dve", "tensor_copy", out=vaug_v[:, gi:gi + n, 0:64], in_=pst[:, 0:n, 0:64], deps=[tl])
                        o1 = I("dve", "tensor_copy", out=vaug_v[:, gi:gi + n, 192:256], in_=pst[:, 0:n, 64:128],
                               deps=[tl])
                    grp_rd[hb] = [o0, o1]
                    vops = [v for v in vops if v.eng != o1.eng] + [o1]
            steps = []
            for r in range(d):
                for m in range(nm):
                    k0 = 128 * m
                    ks = min(128, Lk - k0)
                    qlo = max(0, k0 - R)
                    qhi = min(Lq, k0 + 128 + R)
                    if ks <= 0 or qhi <= qlo:
                        continue
                    pieces = []
                    for s_ in range((qlo + off) // 128, (qhi - 1 + off) // 128 + 1):
                        a = max(qlo, 128 * s_ - off)
                        b = min(qhi, 128 * s_ - off + 128)
                        pieces.append((s_, a, b))
                    steps.append((r, m, k0, ks, qlo, qhi, pieces))
            ncontrib = {}
            for (r, m, k0, ks, qlo, qhi, pieces) in steps:
                for (s_, a, b) in pieces:
                    ncontrib[(r, s_)] = ncontrib.get((r, s_), 0) + 1
            seen = {}
            total_sb = d * ns
            gen_started = [[-1, -1], [-1, -1]]
            cur_evacs = [[[], []], [[], []]]
            gen_done = {}
            Lops, Eops, Xops = {}, {}, {}
            pv_last = {}
            nsteps = len(steps)
            for i in range(nsteps + SK):
                if i < nsteps:
                    (r, m, k0, ks, qlo, qhi, pieces) = steps[i]
                    sl = i % 2
                    sl3 = i % (SK + 1)
                    nqs = qhi - qlo
                    bo = qlo - (k0 - R)
                    for hh in range(2):
                        rows = slice(64 * hh, 64 * hh + 64)
                        psS = ps[hh * 2 + sl]
                        bt = bias_of(hh)
                        eb = Eb[hh][sl3]
                        e0 = E0b[hh][sl3]
                        so = I("pe", "matmul", psS[0:ks, 0:nqs], KTv[rows, k0:k0 + ks, r], QTv[rows, qlo:qhi, r],
                               start=True, stop=True,
                               deps=[Xops.get((i - 2, hh)), (vops + [pe_deps]) if i < 2 else None])
                        xo = I("act", "activation", out=e0[0:ks, 0:nqs], in_=psS[0:ks, 0:nqs], func=AF.Exp,
                               deps=[so, Eops.get((i - SK - 1, hh))])
                        Xops[(i, hh)] = xo
                        eo = I("dve", "tensor_tensor", out=eb[0:ks, 0:nqs], in0=e0[0:ks, 0:nqs],
                               in1=bt[0:ks, bo:bo + nqs], op=ALU.mult, deps=[xo, pv_last.get((i - SK - 1, hh)), wtab_ready])
                        lo_ = eo
                        Lops[(i, hh)] = lo_
                        Eops[(i, hh)] = eo
                if i >= SK:
                    j = i - SK
                    (r, m, k0, ks, qlo, qhi, pieces) = steps[j]
                    ti = r * nm + m
                    for hh in range(2):
                        eb = Eb[hh][j % (SK + 1)]
                        pvl = None
                        groups = []
                        for (s_, a, b) in pieces:
                            sbi = r * ns + s_
                            G = sbi // 4
                            col = (sbi % 4) * 128 + (a - (128 * s_ - off))
                            if groups and groups[-1]["G"] == G:
                                groups[-1]["pcs"].append((s_, a, b, col))
                            else:
                                groups.append(dict(G=G, pcs=[(s_, a, b, col)]))
                        for grp in groups:
                            G = grp["G"]
                            bank = G % 2
                            acc = ps[4 + hh * 2 + bank]
                            a0 = grp["pcs"][0][1]
                            b1 = grp["pcs"][-1][2]
                            col0 = grp["pcs"][0][3]
                            assert grp["pcs"][-1][3] + (b1 - grp["pcs"][-1][1]) - col0 == b1 - a0
                            first = gen_started[hh][bank] != G
                            deps = [Eops[(j, hh)]]
                            if first:
                                gen_started[hh][bank] = G
                                deps.append(cur_evacs[hh][bank])
                                cur_evacs[hh][bank] = []
                            pvl = I("pe", "matmul", acc[:, col0:col0 + (b1 - a0)], vaug_v[0:ks, ti, hh * 128:(hh + 1) * 128],
                                    eb[0:ks, a0 - qlo:b1 - qlo], start=first, stop=False, skip_group_check=True, deps=deps)
                            for (s_, a, b, col) in grp["pcs"]:
                                seen[(hh, r, s_)] = seen.get((hh, r, s_), 0) + 1
                                if seen[(hh, r, s_)] == ncontrib[(r, s_)]:
                                    gd = gen_done.setdefault((hh, G), [])
                                    gd.append((r, s_, a, b, col))
                                    if len(gd) == min(4, total_sb - 4 * G):
                                        evs_ = out_fn(hh, G, gd, acc, pvl)
                                        for h2 in range(2):
                                            cur_evacs[h2][bank].extend(evs_)
                        pv_last[(j, hh)] = pvl

        if l == 0:
            issue_casts(0, 99)
        mb = A.mark()
        QT = A.alloc(NLOC, BF16)
        KT = A.alloc(NLOC, BF16)
        VT = A.alloc(NLOC, BF16)
        Ynum = [A.alloc(nq) for _ in range(3)]
        Dsum = A.alloc(nq)
        lgb = [[A.alloc(384, BF16) for _ in range(SK + 1)] for _ in range(2)]
        Eb = [[A.alloc(384, BF16) for _ in range(SK + 1)] for _ in range(2)]
        for g in range(3):
            d = DIL[g]
            DMA("sp", QT[:, 0:nq], zT[CH_QB[g], :, 0:nq], dsem="attld")
            DMA("sp", KT[:, 0:nkv], zT[CH_KB[g], :, 0:nkv], dsem="attld")
            ldl = DMA("sp", VT[:, 0:nkv], zT[CH_VB[g], :, 0:nkv], dsem="attld")
            Yv = Ynum[g].rearrange("p (j s) -> p j s", s=d)
            Dv = Dsum.rearrange("p (j s) -> p j s", s=d)

            def out_B(hh, G, slots, acc, pvl, g=g, Yv=Yv, Dv=Dv):
                nrows = slice(0, 64) if hh == 0 else slice(64, 128)
                drows = slice(64, 128) if hh == 0 else slice(0, 64)
                ops = []
                for (r, s_, a, b, col) in slots:
                    n = b - a
                    ops.append(I("act", "activation", out=Yv[nrows, a:b, r], in_=acc[nrows, col:col + n], func=AF.Copy,
                                 deps=[pvl]))
                    if g == 0:
                        ops.append(I("dve", "tensor_copy", out=Dv[nrows, a:b, r], in_=acc[drows, col:col + n], deps=[pvl]))
                    else:
                        ops.append(I("dve", "tensor_tensor", out=Dv[nrows, a:b, r], in0=Dv[nrows, a:b, r],
                                     in1=acc[drows, col:col + n], op=ALU.add, deps=[pvl]))
                return ops

            attn_group(QT[:, 0:nq], KT[:, 0:nkv], VT[:, 0:nkv],
                       (lambda hh, g=g: biasB_sb[:, (2 * g + hh) * 256:(2 * g + hh + 1) * 256]),
                       out_B, nq // d, nkv // d, d, 64, lgb, Eb, True, ldl)
            P.barrier()
        I("act", "activation", out=Dsum, in_=Dsum, func=AF.Ln)
        rcp = I("act", "activation", out=Dsum, in_=Dsum, func=AF.Exp, scale=-1.0)
        mstb = [KT[:, 0:nq], VT[:, 0:nq], QT[:, 0:nq]]
        for g in range(3):
            eng = "dve" if g != 1 else "pool"
            mo = I(eng, "tensor_tensor", out=mstb[g], in0=Ynum[g], in1=Dsum, op=ALU.mult, deps=[rcp])
            DMA("sp", mixT[2 + g, :, 0:nq], mstb[g], deps=[mo], dsem="mixw")
        P.barrier()
        A.release(mb)

        if stop == "attb" and l == 0:
            if DEBUG:
                P.barrier()
                final_deps.append(DMA("sp", dbg["mixT"][:, :, :], mixT[:, :, :], dsem="dbg2"))
            P.emit(nc, final_deps)
            stack.close()
            return nc
        mc = A.mark()
        w_out_sb = A.alloc(KC * D, BF16)
        wout_ld = None
        for q in range(2):
            wout_ld = DMA("pool", w_out_sb[:, q * 4096:(q + 1) * 4096], w_out[l, :, q * 4096:(q + 1) * 4096], dsem="wout")
        nkc = nq + 128
        KT = A.alloc(nkc, BF16)
        VT = A.alloc(nkc, BF16)
        QTc = [A.alloc(nq, BF16) for _ in range(2)]
        mstc = [A.alloc(nq, BF16) for _ in range(2)]
        Dt = [A.alloc(512) for _ in range(2)]
        lgb = [[A.alloc(384, BF16) for _ in range(SK + 1)] for _ in range(2)]
        Eb = [[A.alloc(384, BF16) for _ in range(SK + 1)] for _ in range(2)]
        DMA("sp", KT, zT[CH_KC, :, 0:nkc], dsem="attld")
        ldk = DMA("sp", VT, zT[CH_VC, :, 0:nkc], dsem="attld")
        dctr = [0]
        for c in range(3):
            qs = c % 2
            ldq = DMA("sp", QTc[qs], zT[CH_QC[c], :, 0:nq], dsem="attq%d" % qs)
            state = {}

            def out_C(hh, G, slots, acc, pvl, c=c, qs=qs, state=state):
                state[(G, hh)] = (acc, pvl)
                if (G, 0) not in state or (G, 1) not in state:
                    return []
                acc0, pv0 = state[(G, 0)]
                acc1, pv1 = state[(G, 1)]
                Dd = Dt[dctr[0] % 2]
                dctr[0] += 1
                q0 = G * 512
                ops = []
                ops.append(I("dve", "tensor_copy", out=Dd[0:64, :], in_=acc0[64:128, :], deps=[pv0, pv1]))
                ops.append(I("dve", "tensor_copy", out=Dd[64:128, :], in_=acc1[0:64, :]))
                cp2 = ops[-1]
                ops.append(I("act", "activation", out=Dd, in_=Dd, func=AF.Ln, bias=esink[:, l * 3 + c:l * 3 + c + 1],
                             deps=[cp2]))
                rcp_ = I("act", "activation", out=Dd, in_=Dd, func=AF.Exp, scale=-1.0)
                ops.append(rcp_)
                ops.append(I("dve", "tensor_tensor", out=mstc[qs][0:64, q0:q0 + 512], in0=acc0[0:64, :],
                             in1=Dd[0:64, :], op=ALU.mult, deps=[rcp_]))
                ops.append(I("dve", "tensor_tensor", out=mstc[qs][64:128, q0:q0 + 512], in0=acc1[64:128, :],
                             in1=Dd[64:128, :], op=ALU.mult))
                state["last"] = ops[-1]
                return ops

            attn_group(QTc[qs], KT, VT,
                       (lambda hh, c=c: biasC_sb[:, (2 * c + hh) * 384:(2 * c + hh + 1) * 384]),
                       out_C, nq, nkc, 1, 128, lgb, Eb, c == 0, [ldq, ldk])
            DMA("sp", mixT[5 + c, :, 0:nq], mstc[qs], deps=[state["last"]], dsem="mixc%d" % qs)
            P.barrier()
        A.release(mc)
        if DEBUG and l == 0:
            final_deps.append(DMA("sp", dbg["mixT"][:, :, :], mixT[:, :, :], dsem="dbg"))
            P.barrier()

        if stop == "attc" and l == 0:
            if DEBUG:
                P.barrier()
                final_deps.append(DMA("sp", dbg["mixT"][:, :, :], mixT[:, :, :], dsem="dbg2"))
            P.emit(nc, final_deps)
            stack.close()
            return nc
        m2 = A.mark()
        w_out_sb = A.alloc(KC * D, BF16)
        w_out_v = w_out_sb.rearrange("p (k n) -> p k n", k=KC)
        xt = [A.alloc(KC * TT) for _ in range(2)]
        mx = [A.alloc(KC * TT, BF16) for _ in range(2)]
        sqb = A.alloc(KC * TT, BF16)
        tn = A.alloc(KC * TT)
        rstd = A.alloc(TT)
        h = A.alloc(KC * TT, BF16)
        hid = A.alloc(FC * TT, BF16)
        sg = [A.alloc(TT) for _ in range(2)]
        NWG, NWD = 5, 3
        wgs = [A.alloc(2048, BF16) for _ in range(NWG)]
        wds = [A.alloc(DFF, BF16) for _ in range(NWD)]
        hv = h.rearrange("p (k t) -> p k t", k=KC)
        nt2 = nq // TT
        nrm = Norm(sqb, tn, rstd, ps[6])
        st = dict(x_rd=[None, None], mx_rd=[None, None], h_rd=None, hid_rd=None, xw=[None, None],
                  wgc=0, wdc=0, lastres={}, ldx={}, ldm={}, hready={}, fin=None)
        wgs_rd = [None] * NWG
        wds_rd = [None] * NWD
        sg_rd = [None, None]
        bank_rd = [None] * 8

        def m2_loads(t):
            s = t % 2
            st["ldx"][t] = DMA("pool", xt[s].rearrange("p (k t) -> p k t", k=KC), xsrc[:, :, t * TT:(t + 1) * TT],
                               deps=[st["x_rd"][s], st["xw"][s]], dsem="x%d" % s)
            st["ldm"][t] = DMA("pool", mx[s].rearrange("p (k t) -> p k t", k=KC),
                               mixT[:, :, t * TT:(t + 1) * TT].rearrange("c p n -> p c n"),
                               deps=[st["mx_rd"][s]], dsem="mx%d" % s)

        def m2_outproj(t):
            s = t % 2
            xs = xt[s]
            mxv = mx[s].rearrange("p (k t) -> p k t", k=KC)
            res = None
            lastmm = None
            for dc in range(KC):
                bk = 4 + dc % 2
                for kc in range(KC):
                    lastmm = I("pe", "matmul", ps[bk], w_out_v[:, kc, dc * 128:(dc + 1) * 128], mxv[:, kc, :],
                               start=(kc == 0), stop=(kc == KC - 1),
                               deps=[st["ldm"][t], wout_ld, bank_rd[bk]] if kc == 0 else [])
                res = I("dve", "scalar_tensor_tensor", out=xs[:, dc * TT:(dc + 1) * TT], in0=ps[bk], scalar=modv(l, 2, dc),
                        in1=xs[:, dc * TT:(dc + 1) * TT], op0=ALU.mult, op1=ALU.add, deps=[lastmm, st["ldx"][t]])
                bank_rd[bk] = res
            st["mx_rd"][s] = lastmm
            nrm.part1(xs, res)

        def m2_norm2(t):
            nrm.out_war = st["h_rd"]
            st["hready"][t] = nrm.part2([hv[:, kc, :] for kc in range(KC)], [coefv(l, 1, kc) for kc in range(KC)],
                                        [modv(l, 3, kc) for kc in range(KC)])

        def m2_down(t, dcs):
            s = t % 2
            xs = xt[s]
            res = None
            lastmm = None
            for dc in dcs:
                ws = st["wdc"] % NWD
                st["wdc"] += 1
                wl = DMA("sp", wds[ws], wd_bf[l, dc], deps=[cast_done[l], wds_rd[ws]], dsem="wd%d" % ws)
                bk = 4 + dc % 2
                wv = wds[ws].rearrange("p (f n) -> p f n", f=FC)
                for fc in range(FC):
                    lastmm = I("pe", "matmul", ps[bk], wv[:, fc, :], hid[:, fc * TT:(fc + 1) * TT],
                               start=(fc == 0), stop=(fc == FC - 1),
                               deps=[st["ho"], wl, bank_rd[bk]] if fc == 0 else [])
                wds_rd[ws] = lastmm
                res = I("dve", "scalar_tensor_tensor", out=xs[:, dc * TT:(dc + 1) * TT], in0=ps[bk], scalar=modv(l, 5, dc),
                        in1=xs[:, dc * TT:(dc + 1) * TT], op0=ALU.mult, op1=ALU.add, deps=[lastmm])
                bank_rd[bk] = res
            st["hid_rd"] = lastmm
            st["lastres"][t] = res

        def m2_finish(t):
            s = t % 2
            xs = xt[s]
            res = st["lastres"][t]
            if l == 0:
                def fin0(stage, t=t, s=s, xs=xs, res=res):
                    if stage == 3:
                        st["xw"][s] = DMA("pool", x1T[:, :, t * TT:(t + 1) * TT], xs.rearrange("p (k t) -> p k t", k=KC),
                                          deps=[res], dsem="xw%d" % s)
                        st["x_rd"][s] = res
                st["fin"] = fin0
            else:
                def fin(stage, t=t, s=s, xs=xs, res=res):
                    if stage == -1:
                        nrmf.part1(xs, res)
                    elif stage == 0:
                        nrmf.p2a()
                    elif stage == 1:
                        nrmf.p2b()
                    elif stage == 2:
                        nrmf.p2c()
                    else:
                        nrmf.out_war = None
                        fo = nrmf.p2d([xs[:, kc * TT:(kc + 1) * TT] for kc in range(KC)],
                                      [ngs_sb[:, 2 * L * KC + kc:2 * L * KC + kc + 1] for kc in range(KC)], None)
                        st["xw"][s] = DMA("pool", yT[:, :, t * TT:(t + 1) * TT], xs.rearrange("p (k t) -> p k t", k=KC),
                                          deps=[fo], dsem="xw%d" % s)
                        st["x_rd"][s] = fo
                        final_deps.append(st["xw"][s])
                st["fin"] = fin

        nrmf = nrm
        m2_loads(0)
        m2_outproj(0)
        m2_norm2(0)
        for t in range(nt2):
            ho = None
            for fc in range(FC):
                if st["fin"] is not None and fc in (1, 3, 4, 5, 10):
                    st["fin"]({1: -1, 3: 0, 4: 1, 5: 2, 10: 3}[fc])
                    if fc == 10:
                        st["fin"] = None
                if fc == 8 and l == 0:
                    issue_casts(1, 4 if t < nt2 - 1 else 99)
                if fc == 11 and t + 1 < nt2:
                    m2_loads(t + 1)
                ws = st["wgc"] % NWG
                st["wgc"] += 1
                wl = DMA("sp", wgs[ws], wgu_bf[l, fc], deps=[cast_done[l], wgs_rd[ws]], dsem="wg%d" % ws)
                pb = (fc % 2) * 2
                wv = wgs[ws].rearrange("p (g k n) -> p g k n", g=2, k=KC)
                lastmm = None
                for gu in range(2):
                    for kc in range(KC):
                        lastmm = I("pe", "matmul", ps[pb + gu], wv[:, gu, kc, :], hv[:, kc, :],
                                   start=(kc == 0), stop=(kc == KC - 1),
                                   deps=[st["hready"][t], wl, bank_rd[pb + gu]] if kc == 0 else [])
                wgs_rd[ws] = lastmm
                sgi = fc % 2
                so = I("act", "activation", out=sg[sgi], in_=ps[pb], func=AF.Silu, deps=[lastmm, sg_rd[sgi]])
                ho = I("dve", "tensor_tensor", out=hid[:, fc * TT:(fc + 1) * TT], in0=sg[sgi], in1=ps[pb + 1],
                       op=ALU.mult, deps=[so, st["hid_rd"]])
                sg_rd[sgi] = ho
                bank_rd[pb] = ho
                bank_rd[pb + 1] = ho
            st["h_rd"] = lastmm
            st["ho"] = ho
            if t + 1 < nt2:
                m2_outproj(t + 1)
            m2_down(t, range(0, 4))
            if t + 1 < nt2:
                m2_norm2(t + 1)
            m2_down(t, range(4, 8))
            m2_finish(t)
        if st["fin"] is not None:
            for stage in range(-1, 4):
                st["fin"](stage)
            st["fin"] = None
        P.barrier()
        A.release(m2)
        if DEBUG and l == 0:
            final_deps.append(DMA("sp", dbg["x1T"][:, :, :], x1T[:, :, :], dsem="dbg"))
            P.barrier()
        if stop == "m2" and l == 0:
            if DEBUG:
                P.barrier()
                final_deps.append(DMA("sp", dbg["mixT"][:, :, :], mixT[:, :, :], dsem="dbg2"))
            P.emit(nc, final_deps)
            stack.close()
            return nc

    P.emit(nc, final_deps)
    stack.close()
    return nc


def _perm_in():
    o_q_b = 256
    o_k_b = o_q_b + 384
    o_v_b = o_k_b + 384
    o_q_c = o_v_b + 384
    o_k_c = o_q_c + 384
    o_v_c = o_k_c + 128
    cols = list(range(0, 256))
    cols += list(range(o_q_b, o_q_b + 384))
    cols += list(range(o_k_b, o_k_b + 384))
    cols += list(range(o_v_b, o_v_b + 384))
    for c in range(3):
        cols += list(range(o_q_c + 64 * c, o_q_c + 64 * c + 64))
        cols += list(range(o_q_c + 64 * (c + 3), o_q_c + 64 * (c + 3) + 64))
    cols += list(range(o_k_c, o_k_c + 128))
    cols += list(range(o_v_c, o_v_c + 128))
    return np.array(cols)


def _perm_mix():
    rows = list(range(0, 640))
    for c in range(3):
        rows += list(range(640 + 64 * c, 640 + 64 * c + 64))
        rows += list(range(640 + 64 * (c + 3), 640 + 64 * (c + 3) + 64))
    return np.array(rows)


def _const_tables():
    i = np.arange(1, 13, dtype=np.float64)
    slopes = np.exp2(-8.0 * i / 12.0)
    s_win = slopes[:6]
    s_dil = slopes[6:]
    p = np.arange(128)[:, None]
    biasC = np.zeros((128, 6, 384), np.float32)
    j = np.arange(384)[None, :]
    dist = np.abs(j - 128 - p)
    for c in range(3):
        for hh in range(2):
            head = c + 3 * hh
            biasC[:, 2 * c + hh, :] = np.where(dist <= 128, -s_win[head] * dist, NEGM)
    biasB = np.zeros((128, 6, 256), np.float32)
    j = np.arange(256)[None, :]
    dist = np.abs(j - 64 - p)
    for g in range(3):
        for hh in range(2):
            biasB[:, 2 * g + hh, :] = np.where(dist <= 64, -s_dil[2 * g + hh] * DIL[g] * dist, NEGM)
    pconst = np.zeros((128, 18), np.float32)
    rr = ((1, 2), (4, 8))
    for pc in range(2):
        for half in range(2):
            r = rr[pc][half]
            rows = slice(64 * half, 64 * half + 64)
            pconst[rows, pc] = 1.0 / (2 * r + 1)
            tt = np.arange(8)
            pconst[rows, 2 + pc * 8:2 + pc * 8 + 8] = 1.0 / (np.minimum(tt, r) + r + 1)

    def wtab(b):
        import ml_dtypes
        w = np.where(b <= NEGM / 2, 0.0, np.exp(b.astype(np.float64)))
        return w.astype(ml_dtypes.bfloat16).astype(np.float32)

    return wtab(biasC).reshape(128, -1), wtab(biasB).reshape(128, -1), pconst


def _prep(inputs):
    x = np.asarray(inputs["x"], np.float32)
    c = np.asarray(inputs["c"], np.float32)
    g = lambda k: np.asarray(inputs[k], np.float32)
    w_ada, b_ada, w_in, w_pool = g("w_ada"), g("b_ada"), g("w_in"), g("w_pool")
    pool_scale, sink_logit, w_out = g("pool_scale"), g("sink_logit"), g("w_out")
    w_gate, w_up, w_down = g("w_gate"), g("w_up"), g("w_down")
    n1, n2, fgv = g("norm1_g"), g("norm2_g"), g("final_g")
    pin, pmix = _perm_in(), _perm_mix()
    sh = {}
    sh["w_adaT"] = np.ascontiguousarray(
        np.stack([w_ada[l].T.reshape(48, 128, D).transpose(1, 0, 2) for l in range(L)]))
    sh["b_adaT"] = np.ascontiguousarray(
        np.concatenate([b_ada[l].reshape(48, 128).T for l in range(L)], axis=1))
    vec = lambda v: v.reshape(KC, 128).T
    sh["ngs"] = np.ascontiguousarray(np.concatenate(
        [vec(n1[l]) for l in range(L)] + [vec(n2[l]) for l in range(L)] + [vec(fgv)], axis=1))
    sh["w_in"] = np.ascontiguousarray(np.stack(
        [w_in[l][:, pin].reshape(KC, 128, 2048).transpose(1, 0, 2).reshape(128, KC * 2048) for l in range(L)]))
    sh["w_out"] = np.ascontiguousarray(np.stack(
        [w_out[l][pmix, :].reshape(KC, 128, D).transpose(1, 0, 2).reshape(128, KC * D) for l in range(L)]))
    wgu = np.empty((L, FC, 128, 2, KC, 128), np.float32)
    for l in range(L):
        wgu[l, :, :, 0] = w_gate[l].reshape(KC, 128, FC, 128).transpose(2, 1, 0, 3)
        wgu[l, :, :, 1] = w_up[l].reshape(KC, 128, FC, 128).transpose(2, 1, 0, 3)
    sh["wgu"] = wgu.reshape(L, FC, 128, 2048)
    sh["wd"] = np.ascontiguousarray(np.stack(
        [w_down[l].reshape(FC, 128, KC, 128).transpose(2, 1, 0, 3).reshape(KC, 128, DFF) for l in range(L)]))
    wp = np.zeros((128, L, 2, 128), np.float32)
    for l in range(L):
        for pc in range(2):
            wp[0:64, l, pc, 0:64] = w_pool[l, 2 * pc]
            wp[64:128, l, pc, 64:128] = w_pool[l, 2 * pc + 1]
    sh["wpool"] = wp.reshape(128, -1)
    sh["pscale"] = np.ascontiguousarray(np.concatenate([pool_scale[l].reshape(2, 128).T for l in range(L)], axis=1))
    sk = np.zeros((128, L, 3), np.float32)
    for l in range(L):
        for cc in range(3):
            sk[0:64, l, cc] = sink_logit[l, cc]
            sk[64:128, l, cc] = sink_logit[l, cc + 3]
    sh["sink"] = sk.reshape(128, -1)
    bC, bB, pc_ = _const_tables()
    sh["biasC"], sh["biasB"], sh["pconst"] = bC, bB, pc_
    sh["ident"] = np.eye(128, dtype=np.float32)
    in_maps = []
    for i in range(8):
        b, half = i // 2, i % 2
        idx = np.arange(NLOC) if half == 0 else (8191 - np.arange(NLOC))
        xl = x[b][idx]
        m = dict(sh)
        m["xT"] = np.ascontiguousarray(xl.reshape(NLOC, KC, 128).transpose(2, 1, 0))
        m["c_rep"] = np.ascontiguousarray(np.broadcast_to(c[b], (128, D)))
        in_maps.append(m)
    return in_maps


_NC_CACHE = {}


def kernel(**inputs):
    in_maps = _prep(inputs)
    if "nc" not in _NC_CACHE:
        _NC_CACHE["nc"] = build_program()
    nc = _NC_CACHE["nc"]
    res = run_bass_kernel_spmd(nc, in_maps, core_ids=list(range(8)))
    out = np.empty((4, 8192, D), np.float32)
    for i in range(8):
        b, half = i // 2, i % 2
        yT = np.asarray(res.results[i]["yT"])
        y = yT.transpose(2, 1, 0).reshape(NOUT, D)
        if half == 0:
            out[b, 0:NOUT] = y
        else:
            out[b, 8191 - np.arange(NOUT)] = y
    if DEBUG:
        kernel.debug = res.results
    return out
```
